# Optimizing a Trainium2 kernel written in Bass

```python
import math
import jax, jax.numpy as jnp
from jax import lax
import numpy as np

D_MODEL = 1024
BATCH = 16
SEQ = 256
DEPTH = 2
DEC_BATCH = 2
DEC_SEQ = 1024
PAST_LEN = 256

GRID_W = 64
ROPE_THETA = 10000.0
Q_BLOCK = 128
EPS = 1e-6
N_EVEN = (DEPTH + 1) // 2
N_ODD = DEPTH // 2
MLA_HEADS = 8
MLA_Q_LORA = 384
MLA_KV_LORA = 256
MLA_NOPE = 64
MLA_ROPE = 32
MLA_V = 64
MLA_W = MLA_HEADS * MLA_V
GQA_HEADS = 8
GQA_KV_HEADS = 2
GQA_HD = 64
GQA_W = GQA_HEADS * GQA_HD
MLSTM_HEADS = 4
MLSTM_HD = 128
MLSTM_CHUNK = 64
MLSTM_W = MLSTM_HEADS * MLSTM_HD
F_BIAS_OFFSET = 3.0
DIFF_HEADS = 4
DIFF_QK = 64
DIFF_V = 128
DIFF_W = DIFF_HEADS * DIFF_V

MIX_W = MLA_W + GQA_W
EVEN_SPLITS = (MLA_Q_LORA, MLA_KV_LORA, MLA_ROPE, MLA_W,
               GQA_HEADS * GQA_HD, GQA_KV_HEADS * GQA_HD, GQA_KV_HEADS * GQA_HD, GQA_W)
ODD_SPLITS = (MLSTM_W, MLSTM_W, MLSTM_W, MLSTM_W, 4 * MLSTM_HEADS, MLSTM_W,
              DIFF_HEADS * 2 * DIFF_QK, DIFF_HEADS * 2 * DIFF_QK, DIFF_HEADS * DIFF_V, DIFF_W)
EVEN_IN = sum(EVEN_SPLITS)
ODD_IN = sum(ODD_SPLITS)

kernel_name = 'hybrid_mla_gqa_mlstm_diff_prefix_dit_step'


def rmsnorm(x, g):
    xf = x.astype(jnp.float32)
    y = xf * lax.rsqrt(jnp.mean(xf * xf, axis=-1, keepdims=True) + EPS)
    return (y * g.astype(jnp.float32)).astype(x.dtype)


def split_cols(u, sizes):
    return jnp.split(u, np.cumsum(sizes)[:-1].tolist(), axis=-1)


def adaln(cvec, w, b):
    m = jax.nn.silu(cvec) @ w + b
    return jnp.split(m, 3, axis=-1)


def grid_rope(n_tokens, rot_dim):
    n_rows = n_tokens // GRID_W
    rows, cols = jnp.meshgrid(jnp.arange(n_rows), jnp.arange(GRID_W), indexing='ij')
    rows = rows.reshape(-1).astype(jnp.float32)
    cols = cols.reshape(-1).astype(jnp.float32)
    half = rot_dim // 2
    inv = 1.0 / (ROPE_THETA ** (jnp.arange(0, half, 2, dtype=jnp.float32) / half))
    ang = jnp.concatenate([rows[:, None] * inv, cols[:, None] * inv], axis=-1)
    return jnp.cos(ang), jnp.sin(ang)


def apply_rope(x, cos, sin):
    xf = x.astype(jnp.float32).reshape(*x.shape[:-1], -1, 2)
    x0, x1 = xf[..., 0], xf[..., 1]
    c = cos[None, :, None, :]
    s = sin[None, :, None, :]
    out = jnp.stack([x0 * c - x1 * s, x0 * s + x1 * c], axis=-1).reshape(x.shape)
    return out.astype(x.dtype)


def attend(q, k, v):
    B, S, Hk, G, dk = q.shape
    scale = dk ** -0.5
    nb = S // Q_BLOCK
    qb = jnp.moveaxis(q.reshape(B, nb, Q_BLOCK, Hk, G, dk), 1, 0)

    def block(qi):
        s = jnp.einsum('bqhgd,bthd->bhgqt', qi, k).astype(jnp.float32) * scale
        p = jax.nn.softmax(s, axis=-1).astype(v.dtype)
        return jnp.einsum('bhgqt,bthd->bqhgd', p, v)

    o = lax.map(block, qb)
    return jnp.moveaxis(o, 0, 1).reshape(B, S, Hk, G, v.shape[-1])


def mlstm_scan(q, k, v, ig, lf, C0, n0, m0):
    B, S, H, d = q.shape
    L = MLSTM_CHUNK
    nc = S // L

    def chunks(a):
        a = a.astype(jnp.float32).reshape(B, nc, L, H, *a.shape[3:])
        return jnp.moveaxis(jnp.moveaxis(a, 1, 0), 3, 2)

    causal = jnp.tril(jnp.ones((L, L), dtype=bool))

    def step(carry, xs):
        C, n, m = carry
        qc, kc, vc, ic, fc = xs
        b = jnp.cumsum(fc, axis=-1)
        Dm = jnp.where(causal, b[..., :, None] - b[..., None, :] + ic[..., None, :], -jnp.inf)
        inter = b + m[..., None]
        mt = jnp.maximum(inter, Dm.max(axis=-1))
        w_inter = jnp.exp(inter - mt)
        W = jnp.exp(Dm - mt[..., None])
        Sm = jnp.einsum('bhtk,bhsk->bhts', qc, kc) * W
        num = w_inter[..., None] * jnp.einsum('bhvk,bhtk->bhtv', C, qc) + jnp.einsum('bhts,bhsv->bhtv', Sm, vc)
        den = w_inter * jnp.einsum('bhk,bhtk->bht', n, qc) + Sm.sum(axis=-1)
        hc = num / jnp.maximum(jnp.abs(den), jnp.exp(-mt))[..., None]
        bL = b[..., -1]
        m_new = mt[..., -1]
        decay = jnp.exp(bL + m - m_new)
        wk = jnp.exp(bL[..., None] - b + ic - m_new[..., None])
        C_new = decay[..., None, None] * C + jnp.einsum('bhs,bhsv,bhsk->bhvk', wk, vc, kc)
        n_new = decay[..., None] * n + jnp.einsum('bhs,bhsk->bhk', wk, kc)
        return (C_new, n_new, m_new), hc

    init = (C0.astype(jnp.float32), n0.astype(jnp.float32), m0.astype(jnp.float32))
    (C, n, m), hs = lax.scan(step, init, (chunks(q), chunks(k), chunks(v), chunks(ig), chunks(lf)))
    hs = jnp.swapaxes(jnp.moveaxis(hs, 0, 1), 2, 3).reshape(B, S, H, d)
    return hs.astype(q.dtype), C, n, m


def even_mix(h, w_in, qn_g, w_uq, kvn_g, w_ukv, gq_g, gk_g, w_out, ctx):
    B, S, _ = h.shape
    u = h @ w_in
    cq, ckv, kr, g_a, q_b, k_b, v_b, g_b = split_cols(u, EVEN_SPLITS)
    q = (rmsnorm(cq, qn_g) @ w_uq).reshape(B, S, MLA_HEADS, MLA_NOPE + MLA_ROPE)
    q_nope, q_rope = q[..., :MLA_NOPE], q[..., MLA_NOPE:]
    ckv = rmsnorm(ckv, kvn_g)
    kr = kr[:, :, None, :]
    q_b = rmsnorm(q_b.reshape(B, S, GQA_HEADS, GQA_HD), gq_g)
    k_b = rmsnorm(k_b.reshape(B, S, GQA_KV_HEADS, GQA_HD), gk_g)
    v_b = v_b.reshape(B, S, GQA_KV_HEADS, GQA_HD)
    if ctx is None:
        new = (ckv, kr[:, :, 0], k_b, v_b)
        ckv_all, kr_all, kb_all, vb_all = ckv, kr, k_b, v_b
    else:
        c_ckv, c_kr, c_k, c_v = ctx
        cos_a, sin_a = grid_rope(S, MLA_ROPE)
        q_rope = apply_rope(q_rope, cos_a, sin_a)
        kr = apply_rope(kr, cos_a, sin_a)
        cos_b, sin_b = grid_rope(S, GQA_HD)
        q_b = apply_rope(q_b, cos_b, sin_b)
        k_b = apply_rope(k_b, cos_b, sin_b)
        ckv_all = jnp.concatenate([c_ckv.astype(ckv.dtype), ckv], axis=1)
        kr_all = jnp.concatenate([c_kr[:, :, None, :].astype(kr.dtype), kr], axis=1)
        kb_all = jnp.concatenate([c_k.astype(k_b.dtype), k_b], axis=1)
        vb_all = jnp.concatenate([c_v.astype(v_b.dtype), v_b], axis=1)
        new = None
    T = ckv_all.shape[1]
    kv = (ckv_all @ w_ukv).reshape(B, T, MLA_HEADS, MLA_NOPE + MLA_V)
    k_a = jnp.concatenate([kv[..., :MLA_NOPE], jnp.broadcast_to(kr_all, (B, T, MLA_HEADS, MLA_ROPE))], axis=-1)
    v_a = kv[..., MLA_NOPE:]
    q_a = jnp.concatenate([q_nope, q_rope], axis=-1)[:, :, :, None, :]
    o_a = attend(q_a, k_a, v_a).reshape(B, S, MLA_W) * jax.nn.silu(g_a)
    q_bg = q_b.reshape(B, S, GQA_KV_HEADS, GQA_HEADS // GQA_KV_HEADS, GQA_HD)
    o_b = attend(q_bg, kb_all, vb_all).reshape(B, S, GQA_W) * jax.nn.silu(g_b)
    return jnp.concatenate([o_a, o_b], axis=-1) @ w_out, new


def odd_mix(h, w_in, gate_b, mnorm_g, lam, dnorm_g, w_out, layer_idx, ctx):
    B, S, _ = h.shape
    u = h @ w_in
    qm, km, vm, om, gates, zm, qd, kd, vd, gd = split_cols(u, ODD_SPLITS)
    H, d = MLSTM_HEADS, MLSTM_HD
    qm = qm.reshape(B, S, H, d)
    km = km.reshape(B, S, H, d) * (d ** -0.5)
    vm = vm.reshape(B, S, H, d)
    g = (gates.astype(jnp.float32) + gate_b.astype(jnp.float32)).reshape(B, S, 4, H)
    ig_f, lf_f = g[:, :, 0], jax.nn.log_sigmoid(g[:, :, 1])
    ig_b, lf_b = g[:, :, 2], jax.nn.log_sigmoid(g[:, :, 3])
    if ctx is None:
        C0 = jnp.zeros((B, 2, H, d, d), jnp.float32)
        n0 = jnp.zeros((B, 2, H, d), jnp.float32)
        m0 = jnp.zeros((B, 2, H), jnp.float32)
    else:
        C0, n0, m0 = ctx[0], ctx[1], ctx[2]
    rev = lambda a: a[:, ::-1]
    h_f, Cf, nf, mf = mlstm_scan(qm, km, vm, ig_f, lf_f, C0[:, 0], n0[:, 0], m0[:, 0])
    h_b, Cb, nb, mb = mlstm_scan(rev(qm), rev(km), rev(vm), rev(ig_b), rev(lf_b), C0[:, 1], n0[:, 1], m0[:, 1])
    hm = jax.nn.sigmoid(om).reshape(B, S, H, d) * (h_f + rev(h_b))
    hm = rmsnorm(hm, mnorm_g.reshape(H, d)).reshape(B, S, MLSTM_W) * jax.nn.silu(zm)
    qd = qd.reshape(B, S, DIFF_HEADS * 2, DIFF_QK)
    kd = kd.reshape(B, S, DIFF_HEADS * 2, DIFF_QK)
    vd = vd.reshape(B, S, DIFF_HEADS, DIFF_V)
    if ctx is None:
        new = (jnp.stack([Cf, Cb], axis=1).astype(h.dtype), jnp.stack([nf, nb], axis=1).astype(h.dtype),
               jnp.stack([mf, mb], axis=1).astype(h.dtype),
               kd.reshape(B, S, DIFF_HEADS, 2 * DIFF_QK), vd)
        kd_all, vd_all = kd, vd
    else:
        c_kd, c_vd = ctx[3], ctx[4]
        cos_d, sin_d = grid_rope(S, DIFF_QK)
        qd = apply_rope(qd, cos_d, sin_d)
        kd = apply_rope(kd, cos_d, sin_d)
        kd_all = jnp.concatenate([c_kd.reshape(B, -1, DIFF_HEADS * 2, DIFF_QK).astype(kd.dtype), kd], axis=1)
        vd_all = jnp.concatenate([c_vd.astype(vd.dtype), vd], axis=1)
        new = None
    T = kd_all.shape[1]
    q4 = qd.reshape(B, S, DIFF_HEADS, 2, DIFF_QK)
    k4 = kd_all.reshape(B, T, DIFF_HEADS, 2, DIFF_QK)
    a1 = attend(q4[:, :, :, 0:1], k4[:, :, :, 0], vd_all)[:, :, :, 0]
    a2 = attend(q4[:, :, :, 1:2], k4[:, :, :, 1], vd_all)[:, :, :, 0]
    lam_init = 0.8 - 0.6 * math.exp(-0.3 * layer_idx)
    lf32 = lam.astype(jnp.float32)
    lam_val = jnp.exp(jnp.sum(lf32[0] * lf32[1])) - jnp.exp(jnp.sum(lf32[2] * lf32[3])) + lam_init
    od = a1 - lam_val.astype(a1.dtype) * a2
    od = (rmsnorm(od, dnorm_g) * (1.0 - lam_init)).reshape(B, S, DIFF_W) * jax.nn.silu(gd)
    return jnp.concatenate([hm, od], axis=-1) @ w_out, new


def setup_inputs(seed: int = 0) -> dict:
    key = jax.random.key(seed)
    ks = iter(jax.random.split(key, 40))
    nrm = lambda shape, s=1.0: s * jax.random.normal(next(ks), shape, jnp.float32)
    gain = lambda shape: 1.0 + 0.1 * jax.random.normal(next(ks), shape, jnp.float32)
    D = D_MODEL
    return {
        'x_prompt': nrm((BATCH, SEQ, D)),
        'x_sample': nrm((DEC_BATCH, DEC_SEQ, D)),
        'cache_mla_ckv': nrm((DEC_BATCH, N_EVEN, PAST_LEN, MLA_KV_LORA)),
        'cache_mla_krope': nrm((DEC_BATCH, N_EVEN, PAST_LEN, MLA_ROPE)),
        'cache_gqa_k': nrm((DEC_BATCH, N_EVEN, PAST_LEN, GQA_KV_HEADS, GQA_HD)),
        'cache_gqa_v': nrm((DEC_BATCH, N_EVEN, PAST_LEN, GQA_KV_HEADS, GQA_HD)),
        'state_mlstm_C': nrm((DEC_BATCH, N_ODD, 2, MLSTM_HEADS, MLSTM_HD, MLSTM_HD), 0.1),
        'state_mlstm_n': nrm((DEC_BATCH, N_ODD, 2, MLSTM_HEADS, MLSTM_HD), 0.1),
        'state_mlstm_m': nrm((DEC_BATCH, N_ODD, 2, MLSTM_HEADS), 0.5),
        'cache_diff_k': nrm((DEC_BATCH, N_ODD, PAST_LEN, DIFF_HEADS, 2 * DIFF_QK)),
        'cache_diff_v': nrm((DEC_BATCH, N_ODD, PAST_LEN, DIFF_HEADS, DIFF_V)),
        'c': nrm((DEC_BATCH, D)),
        'c_ctx': nrm((D,)),
        'e_norm_g': gain((N_EVEN, D)),
        'e_ada_w': nrm((N_EVEN, D, 3 * D), 0.5 * D ** -0.5),
        'e_ada_b': nrm((N_EVEN, 3 * D), 0.02),
        'e_w_in': nrm((N_EVEN, D, EVEN_IN), D ** -0.5),
        'e_mla_qnorm_g': gain((N_EVEN, MLA_Q_LORA)),
        'e_mla_w_uq': nrm((N_EVEN, MLA_Q_LORA, MLA_HEADS * (MLA_NOPE + MLA_ROPE)), MLA_Q_LORA ** -0.5),
        'e_mla_kvnorm_g': gain((N_EVEN, MLA_KV_LORA)),
        'e_mla_w_ukv': nrm((N_EVEN, MLA_KV_LORA, MLA_HEADS * (MLA_NOPE + MLA_V)), MLA_KV_LORA ** -0.5),
        'e_gqa_qnorm_g': gain((N_EVEN, GQA_HD)),
        'e_gqa_knorm_g': gain((N_EVEN, GQA_HD)),
        'e_w_out': nrm((N_EVEN, MIX_W, D), MIX_W ** -0.5),
        'o_norm_g': gain((N_ODD, D)),
        'o_ada_w': nrm((N_ODD, D, 3 * D), 0.5 * D ** -0.5),
        'o_ada_b': nrm((N_ODD, 3 * D), 0.02),
        'o_w_in': nrm((N_ODD, D, ODD_IN), D ** -0.5),
        'o_mlstm_gate_b': nrm((N_ODD, 4 * MLSTM_HEADS), 0.1)
            + jnp.repeat(jnp.array([0.0, F_BIAS_OFFSET, 0.0, F_BIAS_OFFSET], jnp.float32), MLSTM_HEADS),
        'o_mlstm_norm_g': gain((N_ODD, MLSTM_W)),
        'o_diff_lambda': nrm((N_ODD, 4, DIFF_QK), 0.1),
        'o_diff_norm_g': gain((N_ODD, DIFF_V)),
        'o_w_out': nrm((N_ODD, MIX_W, D), MIX_W ** -0.5),
        'final_norm_g': gain((D,)),
    }


def reference(x_prompt, x_sample, cache_mla_ckv, cache_mla_krope, cache_gqa_k, cache_gqa_v,
              state_mlstm_C, state_mlstm_n, state_mlstm_m, cache_diff_k, cache_diff_v, c,
              c_ctx, e_norm_g, e_ada_w, e_ada_b, e_w_in, e_mla_qnorm_g, e_mla_w_uq, e_mla_kvnorm_g,
              e_mla_w_ukv, e_gqa_qnorm_g, e_gqa_knorm_g, e_w_out, o_norm_g, o_ada_w, o_ada_b, o_w_in,
              o_mlstm_gate_b, o_mlstm_norm_g, o_diff_lambda, o_diff_norm_g, o_w_out, final_norm_g):

    def layer(x, cvec, l, ctx):
        j = l // 2
        if l % 2 == 0:
            shift, scale, gate = adaln(cvec, e_ada_w[j], e_ada_b[j])
            h = rmsnorm(x, e_norm_g[j]) * (1.0 + scale) + shift
            out, st = even_mix(h, e_w_in[j], e_mla_qnorm_g[j], e_mla_w_uq[j], e_mla_kvnorm_g[j],
                               e_mla_w_ukv[j], e_gqa_qnorm_g[j], e_gqa_knorm_g[j], e_w_out[j], ctx)
        else:
            shift, scale, gate = adaln(cvec, o_ada_w[j], o_ada_b[j])
            h = rmsnorm(x, o_norm_g[j]) * (1.0 + scale) + shift
            out, st = odd_mix(h, o_w_in[j], o_mlstm_gate_b[j], o_mlstm_norm_g[j], o_diff_lambda[j],
                              o_diff_norm_g[j], o_w_out[j], l, ctx)
        return x + gate * out, st

    x = x_prompt
    cvec_ctx = c_ctx[None, None, :]
    even_st, odd_st = [], []
    for l in range(DEPTH):
        x, st = layer(x, cvec_ctx, l, None)
        (even_st if l % 2 == 0 else odd_st).append(st)
    y_prompt = rmsnorm(x, final_norm_g)
    st_mla_ckv = jnp.stack([s[0] for s in even_st], axis=1)
    st_mla_krope = jnp.stack([s[1] for s in even_st], axis=1)
    st_gqa_k = jnp.stack([s[2] for s in even_st], axis=1)
    st_gqa_v = jnp.stack([s[3] for s in even_st], axis=1)
    st_mlstm_C = jnp.stack([s[0] for s in odd_st], axis=1)
    st_mlstm_n = jnp.stack([s[1] for s in odd_st], axis=1)
    st_mlstm_m = jnp.stack([s[2] for s in odd_st], axis=1)
    st_diff_k = jnp.stack([s[3] for s in odd_st], axis=1)
    st_diff_v = jnp.stack([s[4] for s in odd_st], axis=1)

    x = x_sample
    cvec = c[:, None, :]
    for l in range(DEPTH):
        j = l // 2
        if l % 2 == 0:
            ctx = (cache_mla_ckv[:, j], cache_mla_krope[:, j], cache_gqa_k[:, j], cache_gqa_v[:, j])
        else:
            ctx = (state_mlstm_C[:, j], state_mlstm_n[:, j], state_mlstm_m[:, j],
                   cache_diff_k[:, j], cache_diff_v[:, j])
        x, _ = layer(x, cvec, l, ctx)
    y_sample = rmsnorm(x, final_norm_g)

    return (y_prompt, y_sample, st_mla_ckv, st_mla_krope, st_gqa_k, st_gqa_v,
            st_mlstm_C, st_mlstm_n, st_mlstm_m, st_diff_k, st_diff_v)
```

```python
import contextlib
import os
import numpy as np
import concourse.bass as bass
import concourse.mybir as mybir
from concourse.bass_utils import run_bass_kernel_spmd

F32 = mybir.dt.float32
BF16 = mybir.dt.bfloat16
AF = mybir.ActivationFunctionType
ALU = mybir.AluOpType
AX = mybir.AxisListType

COMPUTE = ["pe", "act", "dve", "pool"]
ENGS = ["pe", "act", "dve", "pool", "sp"]
EPS = 1e-6


class Sched:
    def __init__(self, nc, n_dma=8, n_swp=88):
        self.nc = nc
        self.ops = {e: [] for e in ENGS}
        self.count = {}
        self.clock = {e: {} for e in ENGS}
        self.vc = {}
        self.res = {}
        self.n_dma = n_dma
        self.dma_rr = 0
        self.tracks = list(COMPUTE) + ["d%d" % i for i in range(n_dma)]
        for t in self.tracks:
            self.count[t] = 0
        self.n_sw = 0
        self.n_swp = n_swp
        self.sw_last = [None] * n_swp
        self.sw_waiters = {}
        self.retired = {}
        self.phys = {}
        self.n_waits = 0
        self.pending = {e: {} for e in ENGS}

    def barrier(self):
        snap = {t: c for t, c in self.count.items() if c > 0 and t not in self.retired}
        for e in ENGS:
            self.pending[e] = dict(snap)

    def _states(self, key):
        tk, sub = key
        d = self.res.setdefault(tk, {})
        if sub is None:
            if None not in d:
                d[None] = {"w": None, "r": {}}
            return list(d.values())
        out = []
        if None in d:
            out.append(d[None])
        if sub not in d:
            d[sub] = {"w": None, "r": {}}
        out.append(d[sub])
        return out

    @staticmethod
    def _norm(k):
        if isinstance(k, str):
            return (k, None)
        return k

    def emit(self, eng, fn, reads=(), writes=(), dma=False):
        reads = [self._norm(k) for k in reads]
        writes = [self._norm(k) for k in writes]
        deps = set()
        for k in reads:
            for st in self._states(k):
                if st["w"] is not None:
                    deps.add(st["w"])
        for k in writes:
            for st in self._states(k):
                if st["w"] is not None:
                    deps.add(st["w"])
                for t, i in st["r"].items():
                    deps.add((t, i))
        if os.environ.get("RR", "1") == "1":
            for k in reads:
                if k[0].startswith("ps"):
                    for st in self._states(k):
                        for t, i in st["r"].items():
                            deps.add((t, i))
        if eng == "pe":
            deps = {d for d in deps if d[0] != "pe"}
        deps = {(self.retired[t] if t in self.retired else (t, i)) for (t, i) in deps}
        if dma and eng == "pool":
            p = self.n_sw % self.n_swp
            assert self.n_sw < self.n_swp, "software-DMA semaphore reuse (sem_clear) faults on hardware"
            prev = self.sw_last[p]
            if prev is not None:
                extra = {(prev, 0)} | set(self.sw_waiters.get(prev, ()))
                self._emit_clear(p, extra)
                self.retired[prev] = ("pool", self.count["pool"] - 1)
            track = "dw%d" % self.n_sw
            self.n_sw += 1
            self.count[track] = 0
            self.phys[track] = p
            self.sw_last[p] = track
            self.sw_waiters[track] = set()
        elif dma:
            track = "d%d" % self.dma_rr
            self.dma_rr = (self.dma_rr + 1) % self.n_dma
            if self.count[track] > 0:
                deps.add((track, self.count[track] - 1))
        else:
            track = eng
        clk = self.clock[eng]
        need = {}
        for t, i in deps:
            if need.get(t, 0) < i + 1:
                need[t] = i + 1
        if self.pending[eng]:
            for t, v in self.pending[eng].items():
                if need.get(t, 0) < v:
                    need[t] = v
            self.pending[eng] = {}
        waits = [(t, v) for t, v in need.items() if clk.get(t, 0) < v]
        for t, v in waits:
            for tt, vv in self.vc[(t, v - 1)].items():
                if clk.get(tt, 0) < vv:
                    clk[tt] = vv
        waits = [(t, v) for (t, v) in waits if not any(
            (t2, v2) != (t, v) and self.vc[(t2, v2 - 1)].get(t, 0) >= v for (t2, v2) in waits)]
        self.n_waits += len(waits)
        idx = self.count[track]
        self.count[track] = idx + 1
        myvc = dict(clk)
        myvc[track] = idx + 1
        self.vc[(track, idx)] = myvc
        self.ops[eng].append((waits, fn, track))
        for t, v in waits:
            if t in self.sw_waiters:
                self.sw_waiters[t].add((track, idx))
        for k in writes:
            self._states(k)
            st = self.res[k[0]][k[1]]
            st["w"] = (track, idx)
            st["r"] = {}
            if k[1] is None:
                for sub, s2 in self.res[k[0]].items():
                    if sub is not None:
                        s2["w"] = (track, idx)
                        s2["r"] = {}
        for k in reads:
            self._states(k)
            self.res[k[0]][k[1]]["r"][track] = idx
        return (track, idx)

    def _emit_clear(self, p, extra_deps):
        clk = self.clock["pool"]
        need = {}
        for t, i in extra_deps:
            if t in self.retired:
                t, i = self.retired[t]
            if need.get(t, 0) < i + 1:
                need[t] = i + 1
        if self.pending["pool"]:
            for t, v in self.pending["pool"].items():
                if need.get(t, 0) < v:
                    need[t] = v
            self.pending["pool"] = {}
        waits = [(t, v) for t, v in need.items() if clk.get(t, 0) < v]
        for t, v in waits:
            for tt, vv in self.vc[(t, v - 1)].items():
                if clk.get(tt, 0) < vv:
                    clk[tt] = vv
        idx = self.count["pool"]
        self.count["pool"] = idx + 1
        myvc = dict(clk)
        myvc["pool"] = idx + 1
        self.vc[("pool", idx)] = myvc
        for t, v in waits:
            if t in self.sw_waiters:
                self.sw_waiters[t].add(("pool", idx))
        self.ops["pool"].append((waits, ("clear", p), "pool"))
        self.n_waits += len(waits)

    def sem_of(self, sems, t):
        if t in self.phys:
            return sems["swp%d" % self.phys[t]]
        return sems[t]

    def sem_names(self):
        return list(COMPUTE) + ["d%d" % i for i in range(self.n_dma)] + ["swp%d" % i for i in range(min(self.n_swp, max(self.n_sw, 1)))]

    def finish(self, sems, block):
        engmap = {"pe": block.tensor, "act": block.scalar, "dve": block.vector,
                  "pool": block.gpsimd, "sp": block.sync}
        final = {t: c for t, c in self.count.items() if c > 0 and t not in self.retired}

        def body(ename):
            def run(e):
                for waits, fn, track in self.ops[ename]:
                    for t, v in waits:
                        e.wait_ge(self.sem_of(sems, t), v * (16 if t[0] == "d" else 1))
                    if isinstance(fn, tuple):
                        e.sem_clear(sems["swp%d" % fn[1]])
                        e.drain().then_inc(sems["pool"], 1)
                        continue
                    inst = fn(e)
                    inst.then_inc(self.sem_of(sems, track), 16 if track[0] == "d" else 1)
                if ename == "sp":
                    for t, c in final.items():
                        e.wait_ge(self.sem_of(sems, t), c * (16 if t[0] == "d" else 1))
            return run

        for ename in ENGS:
            engmap[ename](body(ename))


class TPool:
    def __init__(self, nc, stack):
        self.nc = nc
        self.stack = stack
        self.n = 0
        self.bytes = 0

    def sb(self, name, shape, dtype):
        self.n += 1
        sz = 1
        for s in shape[1:]:
            sz *= s
        self.bytes += sz * (2 if dtype == BF16 else 4)
        return self.stack.enter_context(self.nc.sbuf_tensor("%s_%d" % (name, self.n), list(shape), dtype))

    def ps(self, name, shape, dtype=F32):
        self.n += 1
        return self.stack.enter_context(self.nc.psum_tensor("%s_%d" % (name, self.n), list(shape), dtype))


D = 1024
NT = 1536
E_IN = 2464
O_IN = 4624
GP = dict(name="P", tok0=0, ntok=512, batches=[0], nseg=2, seglen=256, ctx=0, v=0)
GS = dict(name="S", tok0=512, ntok=1024, batches=[1, 2], nseg=1, seglen=1024, ctx=256, v=1)
TK = 1792
SEGS = [(0, 256, 0, 2, 0), (256, 256, 256, 2, 2), (512, 512, 512, 10, 4), (1024, 512, 512, 10, 8)]


def kcol(b):
    return 0 if b == 0 else 768 + (b - 1) * 512

IN_SPECS = [
    ("x_in", [NT, D]), ("c2", [128, 8, 2]),
    ("e_norm_g", [128, 8]), ("e_ada_w", [D, 3 * D]), ("e_ada_b", [128, 24]),
    ("e_w_in", [D, E_IN]), ("e_qn_g", [128, 3]), ("e_w_uq", [384, 768]), ("e_kvn_g", [128, 2]),
    ("e_w_ukv", [256, 1024]), ("e_gq_g", [128, 1]), ("e_gk_g", [128, 1]), ("e_w_out", [D, D]),
    ("o_norm_g", [128, 8]), ("o_ada_w", [D, 3 * D]), ("o_ada_b", [128, 24]),
    ("o_w_in", [D, O_IN]), ("o_gate_b", [4, 4]), ("o_mn_g", [1, 512]), ("o_lambda", [1, 256]),
    ("o_dn_g", [1, 128]), ("o_w_out", [D, D]), ("final_g", [1, D]),
    ("c_ckvT", [256, 256]), ("c_krT", [32, 256]), ("c_gkT", [128, 256]), ("c_gv", [256, 128]),
    ("c_ST", [8, 128, 128]), ("c_n", [8, 128]), ("c_m", [4, 2]), ("c_kdT", [4, 128, 256]),
    ("c_vd", [256, 512]),
    ("k_ident", [128, 128]), ("k_bones", [128, 128]), ("k_psw", [128, 128]),
    ("k_maskf", [64, 64]), ("k_maskb", [64, 64]),
    ("k_cos64", [128, 1024]), ("k_sin64", [128, 1024]), ("k_cos32", [128, 1024]), ("k_sin32", [128, 1024]),
]
OUT_SPECS = [
    ("y", [NT, D]), ("o_ckv", [512, 256]), ("o_kr", [512, 32]), ("o_gk", [512, 128]), ("o_gv", [512, 128]),
    ("o_C", [16, 128, 128]), ("o_n", [16, 128]), ("o_m", [4, 4]), ("o_dk", [512, 512]), ("o_dv", [512, 512]),
]


class Builder:
    def __init__(self, upto=99):
        self.upto = upto
        self.nc = bass.Bass("TRN2", target_bir_lowering=False)
        nc = self.nc
        self.inh = {n: nc.dram_tensor(n, list(s), F32, kind="ExternalInput") for n, s in IN_SPECS}
        self.inp = {n: h.ap() for n, h in self.inh.items()}
        self.outp = {n: nc.dram_tensor(n, list(s), F32, kind="ExternalOutput").ap() for n, s in OUT_SPECS}
        self.xi = 0
        self.si = 0
        self.fi = 0
        self.bg = None
        self.tick_every = int(os.environ.get("TICK", "1"))

    def mm(self, out, lhsT, rhs, start=True, stop=True, r=(), w=()):
        self.S.emit("pe", lambda e: e.matmul(out, lhsT=lhsT, rhs=rhs, start=start, stop=stop,
                                             skip_group_check=True), reads=r, writes=w)

    def tr(self, out, in_, ident, r=(), w=()):
        self.S.emit("pe", lambda e: e.transpose(out=out, in_=in_, identity=ident), reads=r, writes=w)

    def act(self, out, in_, func, r=(), w=(), **kw):
        self.S.emit("act", lambda e: e.activation(out=out, in_=in_, func=func, **kw), reads=r, writes=w)

    def ts(self, out, in0, s1, s2, op0, op1=None, r=(), w=(), eng="dve", accum_out=None):
        if op1 is None:
            self.S.emit(eng, lambda e: e.tensor_scalar(out=out, in0=in0, scalar1=s1, scalar2=None, op0=op0),
                        reads=r, writes=w)
        elif accum_out is not None:
            self.S.emit(eng, lambda e: e.tensor_scalar(out=out, in0=in0, scalar1=s1, scalar2=s2, op0=op0, op1=op1,
                                                       accum_out=accum_out), reads=r, writes=w)
        else:
            self.S.emit(eng, lambda e: e.tensor_scalar(out=out, in0=in0, scalar1=s1, scalar2=s2, op0=op0, op1=op1),
                        reads=r, writes=w)

    def tt(self, out, in0, in1, op, r=(), w=(), eng="dve"):
        self.S.emit(eng, lambda e: e.tensor_tensor(out=out, in0=in0, in1=in1, op=op), reads=r, writes=w)

    def stt(self, out, in0, scalar, in1, op0, op1, r=(), w=()):
        self.S.emit("dve", lambda e: e.scalar_tensor_tensor(out=out, in0=in0, scalar=scalar, in1=in1, op0=op0, op1=op1),
                    reads=r, writes=w)

    def cp(self, out, in_, r=(), w=(), eng="dve"):
        if eng == "act":
            self.S.emit("act", lambda e: e.copy(out=out, in_=in_), reads=r, writes=w)
        else:
            self.S.emit(eng, lambda e: e.tensor_copy(out=out, in_=in_), reads=r, writes=w)

    def recip(self, out, in_, r=(), w=()):
        self.S.emit("dve", lambda e: e.reciprocal(out=out, in_=in_), reads=r, writes=w)

    def memset(self, ap, val, w=(), eng="dve"):
        self.S.emit(eng, lambda e: e.memset(ap, val), writes=w)

    def dma(self, out, in_, r=(), w=(), eng="sp"):
        self.S.emit(eng, lambda e: e.dma_start(out=out, in_=in_), reads=r, writes=w, dma=True)

    def scan(self, out, d0, d1, init, op0, op1, r=(), w=()):
        self.S.emit("dve", lambda e: e.tensor_tensor_scan(out=out, data0=d0, data1=d1, initial=init, op0=op0, op1=op1),
                    reads=r, writes=w)

    def reduce(self, out, in_, op, r=(), w=()):
        self.S.emit("dve", lambda e: e.tensor_reduce(out=out, in_=in_, axis=AX.X, op=op), reads=r, writes=w)

    def dbg(self, n):
        return n <= int(os.environ.get("SUB", "99"))

    @staticmethod
    def hk(b):
        return [("hT", 4 * b + j) for j in range(4)]

    def tick(self):
        if self.bg is not None:
            try:
                next(self.bg)
            except StopIteration:
                self.bg = None

    def run_bg(self):
        while self.bg is not None:
            self.tick()

    def nextX(self):
        i = self.xi
        self.xi = (i + 1) % 2
        return self.psX[i], "psX%d" % i

    def nextS(self):
        i = self.si
        self.si = (i + 1) % 3
        if i < 2:
            return self.psS[i], "psS%d" % i
        return self.psX[2], "psX2"

    def nextF(self):
        i = self.fi
        self.fi = (i + 1) % len(self.fs)
        return self.fs[i], "fs%d" % i

    def setup(self, st):
        nc = self.nc
        self.P = TPool(nc, st)
        self.S = Sched(nc)
        P = self.P
        sb = P.sb
        self.psX = [P.ps("psX%d" % i, [128, 512], F32) for i in range(3)]
        self.psS = [P.ps("psS%d" % i, [128, 512], F32) for i in range(2)]
        self.psO = [P.ps("psO%d" % i, [128, 512], F32) for i in range(2)]
        self.psT = P.ps("psT", [128, 1024], BF16)
        self.hT = sb("hT", [128, 8, NT], BF16)
        self.mixT = sb("mixT", [128, 8, NT], BF16)
        self.ident_f = sb("identf", [128, 128], F32)
        self.ident_b = sb("identb", [128, 128], BF16)
        self.bones = sb("bones", [128, 128], BF16)
        self.psw = sb("psw", [128, 128], BF16)
        self.ones_b = sb("onesb", [128, 128], BF16)
        self.ones_f = sb("onesf", [128, 128], F32)
        self.maskf = sb("maskf", [64, 64], F32)
        self.maskb = sb("maskb", [64, 64], F32)
        self.cos64 = sb("cos64", [128, 1024], BF16)
        self.sin64 = sb("sin64", [128, 1024], BF16)
        self.cos32 = sb("cos32", [128, 1024], BF16)
        self.sin32 = sb("sin32", [128, 1024], BF16)
        self.mhalf = sb("mhalf", [128, 16], F32)
        i = self.inp
        self.dma(self.ident_f[:], i["k_ident"], w=["identf"])
        self.dma(self.ident_b[:], i["k_ident"], w=["identb"], eng="pool")
        self.dma(self.bones[:], i["k_bones"], w=["bones"], eng="pool")
        self.dma(self.psw[:], i["k_psw"], w=["psw"], eng="pool")
        self.dma(self.maskf[:], i["k_maskf"], w=["maskf"])
        self.dma(self.maskb[:], i["k_maskb"], w=["maskb"])
        self.dma(self.cos64[:], i["k_cos64"], w=["cos64"], eng="pool")
        self.dma(self.sin64[:], i["k_sin64"], w=["sin64"], eng="pool")
        self.dma(self.cos32[:], i["k_cos32"], w=["cos32"], eng="pool")
        self.dma(self.sin32[:], i["k_sin32"], w=["sin32"], eng="pool")
        self.memset(self.ones_b[:], 1.0, w=["onesb"])
        self.memset(self.ones_f[:], 1.0, w=["onesf"])
        self.memset(self.mhalf[:], -0.5, w=["mhalf"])
        self.c2 = sb("c2", [128, 16], F32)
        self.sc = sb("sc", [128, 16], BF16)
        self.dma(self.c2[:], i["c2"].rearrange("p a b -> p (a b)"), w=["c2"])
        self.ng = {}
        self.adab = {}
        for L in ("e", "o"):
            self.ng[L] = sb("ng" + L, [128, 8], F32)
            self.adab[L] = sb("adab" + L, [128, 24], F32)
            self.dma(self.ng[L][:], i[L + "_norm_g"], w=["ng" + L])
            self.dma(self.adab[L][:], i[L + "_ada_b"], w=["adab" + L])
        self.qn_g = sb("qng", [128, 3], F32)
        self.kvn_g = sb("kvng", [128, 2], F32)
        self.gq_g = sb("gqg", [128, 1], F32)
        self.gk_g = sb("gkg", [128, 1], F32)
        self.dma(self.qn_g[:], i["e_qn_g"], w=["qng"])
        self.dma(self.kvn_g[:], i["e_kvn_g"], w=["kvng"])
        self.dma(self.gq_g[:], i["e_gq_g"], w=["gqg"])
        self.dma(self.gk_g[:], i["e_gk_g"], w=["gkg"])
        self.fs = [sb("fs%d" % k, [128, 512], F32) for k in range(4)]
        self.ropeb = sb("ropeb", [128, 512], BF16)
        self.junk = sb("junk", [128, D], BF16)
        self.xs = [sb("xs%d" % k, [128, D], BF16) for k in range(2)]
        self.ss = sb("ss", [128, 16], F32)
        self.rden = sb("rden", [128, 8], F32)
        self.rdi = 0
        self.rstd = sb("rstd", [128, 16], F32)
        self.mod = {L: sb("mod" + L, [128, 48], F32) for L in ("e", "o")}
        self.Amod = {L: sb("Amod" + L, [128, 16], F32) for L in ("e", "o")}
        self.diag = [sb("diag%d" % k, [128, 128], F32) for k in range(2)]
        self.tmp16 = sb("tmp16", [128, 16], F32)
        self.act(self.tmp16[:], self.c2[:], AF.Tanh, r=["c2"], w=["tmp16"], scale=0.5)
        self.ts(self.tmp16[:], self.tmp16[:], 0.5, 0.5, ALU.mult, ALU.add, r=["tmp16"], w=["tmp16"])
        self.tt(self.sc[:], self.tmp16[:], self.c2[:], ALU.mult, r=["tmp16", "c2"], w=["sc"])

    def adaln(self, L):
        self.adaslot = [self.P.sb("adas%d" % k, [128, 8, 768], BF16) for k in range(2)]
        wv = self.inp[L + "_ada_w"].rearrange("(kc p) n -> p kc n", p=128)
        ps, pk = self.nextX()
        for blk in range(4):
            slot = self.adaslot[blk % 2]
            sk = "adas%d" % (blk % 2)
            self.dma(slot[:], wv[:, :, blk * 768:(blk + 1) * 768], w=[sk], eng="pool")
            for j in range(6):
                oc = blk * 6 + j
                for kc in range(8):
                    self.mm(ps[:, oc * 2:oc * 2 + 2], slot[:, kc, j * 128:(j + 1) * 128],
                            self.sc[:, kc * 2:kc * 2 + 2], start=(kc == 0), stop=(kc == 7),
                            r=[sk, "sc"], w=[pk])
        mod = self.mod[L]
        self.tt(mod[:].rearrange("p (a b) -> p a b", b=2), ps[:, 0:48].rearrange("p (a b) -> p a b", b=2),
                self.adab[L][:].unsqueeze(2).broadcast_to([128, 24, 2]), ALU.add,
                r=[pk, "adab" + L], w=["mod" + L])
        A = self.Amod[L]
        self.ts(self.tmp16[:], mod[:, 16:32], 1.0, None, ALU.add, r=["mod" + L], w=["tmp16"])
        self.tt(A[:].rearrange("p (a b) -> p a b", b=2), self.tmp16[:].rearrange("p (a b) -> p a b", b=2),
                self.ng[L][:].unsqueeze(2).broadcast_to([128, 8, 2]), ALU.mult,
                r=["tmp16", "ng" + L], w=["Amod" + L])

    def gate_bcast(self, L):
        mod = self.mod[L]
        for v in range(2):
            for half in range(2):
                ps, pk = self.nextX()
                for cc in range(4):
                    c = half * 4 + cc
                    dg = self.diag[cc % 2]
                    dk = "diag%d" % (cc % 2)
                    self.ts(dg[:], self.ident_f[:], mod[:, 32 + c * 2 + v:32 + c * 2 + v + 1], None, ALU.mult,
                            r=["identf", "mod" + L], w=[dk])
                    self.mm(ps[:, cc * 128:(cc + 1) * 128], self.ones_f[:], dg[:], r=["onesf", dk], w=[pk])
                self.cp(self.gate_bc[v][:, half * 512:(half + 1) * 512], ps[:], r=[pk], w=["gbc%d" % v], eng="act")

    def norm_to_hT(self, L, first):
        A = self.Amod[L]
        mod = self.mod[L]
        if first:
            xt_of = lambda t: (self.xstage[t], "xst%d" % t)
            for t in range(12):
                xa, xk = xt_of(t)
                self.dma(xa[:], self.inp["x_in"][t * 128:(t + 1) * 128, :], w=[xk])
        else:
            xt_of = lambda t: (self.x1[t], "x1_%d" % t)
        for t in range(12):
            if not first:
                break
            xa, xk = xt_of(t)
            self.act(self.junk[:], xa[:], AF.Square, r=[xk], w=["junk", ("ss", t)], accum_out=self.ss[:, t:t + 1])
        self.ts(self.rstd[:, 0:12], self.ss[:, 0:12], 1.0 / D, EPS, ALU.mult, ALU.add, r=["ss"], w=["rstd"])
        self.tt(self.rstd[:, 0:12], self.rstd[:, 0:12], self.mhalf[:, 0:12], ALU.pow,
                r=["rstd", "mhalf"], w=["rstd"], eng="pool")
        tbanks = [(self.psT[:], "psT"), (self.psS[0][:].bitcast(BF16), "psS0"), (self.psS[1][:].bitcast(BF16), "psS1")]

        def emit_xs(t):
            xa, xk = xt_of(t)
            xs = self.xs[t % 2]
            sk = "xs%d" % (t % 2)
            if t % 2 == 0:
                self.ts(xs[:], xa[:], self.rstd[:, t:t + 1], None, ALU.mult, r=[xk, ("rstd", t)], w=[sk])
            else:
                self.act(xs[:], xa[:], AF.Copy, r=[xk, ("rstd", t)], w=[sk], scale=self.rstd[:, t:t + 1])
        emit_xs(0)
        for t in range(12):
            b = t // 4
            v = 0 if b == 0 else 1
            xs = self.xs[t % 2]
            sk = "xs%d" % (t % 2)
            pt, ptk = tbanks[t % 3]
            for c in range(8):
                self.tr(pt[:, c * 128:(c + 1) * 128], xs[:, c * 128:(c + 1) * 128], self.ident_b[:],
                        r=[sk, "identb"], w=[ptk])
            if t + 1 < 12:
                emit_xs(t + 1)
            for c in range(8):
                o = self.hT[:, c, t * 128:(t + 1) * 128]
                a_ = A[:, c * 2 + v:c * 2 + v + 1]
                b_ = mod[:, c * 2 + v:c * 2 + v + 1]
                if t % 2 == 0:
                    self.ts(o, pt[:, c * 128:(c + 1) * 128], a_, b_, ALU.mult, ALU.add,
                            r=[ptk, "Amod" + L, "mod" + L], w=[("hT", t)])
                else:
                    self.act(o, pt[:, c * 128:(c + 1) * 128], AF.Identity,
                             r=[ptk, "Amod" + L, "mod" + L], w=[("hT", t)], scale=a_, bias=b_)

    def proj_fm(self, ps_ap, pk, w_fn, wk, rhs_fn, rk, nk):
        for kc in range(nk):
            self.mm(ps_ap, w_fn(kc), rhs_fn(kc), start=(kc == 0), stop=(kc == nk - 1), r=[wk] + list(rk), w=[pk])

    def rms_group(self, projs, n, n_feat, lhs_ones, lk, g_fn, gk, consume):
        last = len(projs) - 1
        psy, yk = self.psX[2], "psX2"
        for c, proj in enumerate(projs):
            ps, pk = self.nextX()
            proj(ps[:, :n], pk)
            yield
            self.cp(self.uf[:, c, :n], ps[:, :n], r=[pk], w=[("uf", c)], eng="act")
            self.act(self.sq[c % 2][:, :n], ps[:, :n], AF.Square, r=[pk], w=["sq%d" % (c % 2)])
            yield
            self.mm(psy[:, :n], lhs_ones, self.sq[c % 2][:, :n], start=(c == 0), stop=(c == last),
                    r=[lk, "sq%d" % (c % 2)], w=[yk])
        yield
        rb, rk = self.rb, "rb"
        self.act(rb[:, :n], psy[:, :n], AF.Ln, r=[yk], w=[rk], scale=1.0 / n_feat, bias=EPS)
        self.act(rb[:, :n], rb[:, :n], AF.Exp, r=[rk], w=[rk], scale=-0.5)
        yield
        for c in range(len(projs)):
            un, uk = self.nextF()
            self.stt(un[:, :n], self.uf[:, c, :n], g_fn(c), rb[:, :n], ALU.mult, ALU.mult,
                     r=[("uf", c), gk, rk], w=[uk])
            yield from consume(c, un, uk)

    def rope_fm(self, u, uk, rows, n, pos0, cos_t, ck, sin_t, sk, out, ok):
        r0, r1 = rows
        self.cp(self.ropeb[r0:r1, :n], u, r=uk, w=["ropeb"])
        f1, k1 = self.nextF()
        self.tt(f1[r0:r1, :n], u, cos_t[r0:r1, pos0:pos0 + n], ALU.mult, r=list(uk) + [ck], w=[k1])
        yield
        ps, pk = self.nextX()
        self.mm(ps[r0:r1, :n], self.psw[r0:r1, r0:r1], self.ropeb[r0:r1, :n], r=["psw", "ropeb"], w=[pk])
        yield
        f2, k2 = self.nextF()
        self.tt(f2[r0:r1, :n], ps[r0:r1, :n], sin_t[r0:r1, pos0:pos0 + n], ALU.mult, r=[pk, sk], w=[k2])
        self.tt(out, f1[r0:r1, :n], f2[r0:r1, :n], ALU.add, r=[k1, k2], w=ok, eng="pool")

    def attention(self, segs, kT, kk, krows, qT, qk, v_fn, vk, dv, scale, out_fn):
        r0, r1 = krows
        pend = None
        pi = 0
        for (q0, nq, t0, ntc, qtile0) in segs:
            pT = self.pT[pi % 2]
            pk = "pT%d" % (pi % 2)
            for tc in range(ntc):
                ps, sk = self.nextS()
                self.mm(ps[:, :nq], kT[r0:r1, t0 + tc * 128:t0 + (tc + 1) * 128], qT[r0:r1, q0:q0 + nq],
                        r=[kk, qk], w=[sk])
                self.act(pT[:, tc, :nq], ps[:, :nq], AF.Exp, r=[sk], w=[(pk, tc)], scale=scale)
                if tc % self.tick_every == self.tick_every - 1:
                    self.tick()
            if pend is not None:
                self._pv(*pend)
            pend = (pT, pk, nq, t0, ntc, qtile0, v_fn, vk, dv, out_fn, pi)
            pi += 1
        if pend is not None:
            self._pv(*pend)

    def _pv(self, pT, pk, nq, t0, ntc, qtile0, v_fn, vk, dv, out_fn, pi):
        w_ = dv + 1

        def reg(qt):
            if dv == 64:
                return self.psO[pi % 2][:, qt * w_:(qt + 1) * w_], "psO%d" % (pi % 2)
            return self.psO[qt // 2][:, (qt % 2) * w_:(qt % 2 + 1) * w_], "psO%d" % (qt // 2)
        for qt in range(nq // 128):
            po, ok = reg(qt)
            for tc in range(ntc):
                self.mm(po, pT[:, tc, qt * 128:(qt + 1) * 128],
                        v_fn(t0 // 128 + tc), start=(tc == 0), stop=(tc == ntc - 1), r=[(pk, tc), vk], w=[ok])
            self.tick()
        if getattr(out_fn, "block", None) is not None:
            out_fn.block(nq // 128, reg, dv)
        for qt in range(nq // 128):
            po, ok = reg(qt)
            out_fn(qtile0 + qt, po, ok)

    def tm_to_mixT(self, src, skey, tile_abs, c0, nchunk):
        b = tile_abs // 4
        for c in range(nchunk):
            self.tr(self.psT[:, c * 128:(c + 1) * 128], src[:, c * 128:(c + 1) * 128], self.ident_b[:],
                    r=[skey, "identb"], w=["psT"])
        self.cp(self.mixT[:, c0:c0 + nchunk, tile_abs * 128:(tile_abs + 1) * 128],
                self.psT[:, 0:nchunk * 128].rearrange("p (c t) -> p c t", t=128),
                r=["psT"], w=[("mixT", b)])

    def fm_to_out(self, src_ap, skey, rows, out_ap):
        self.tr(self.psT[:, 0:rows], src_ap, self.ident_b[0:rows, 0:rows], r=list(skey) + ["identb"], w=["psT"])
        st, sk = self.nextF()
        self.cp(st[:, 0:rows], self.psT[:, 0:rows], r=["psT"], w=[sk])
        self.dma(out_ap, st[:, 0:rows], r=[sk])

    def gates_tm(self, w_fn, wk, dst, dkey, tile_abs, tile_rel):
        b = tile_abs // 4
        ps, pk = self.nextX()
        for kc in range(8):
            self.mm(ps[:], self.hT[:, kc, tile_abs * 128:(tile_abs + 1) * 128], w_fn(kc),
                    start=(kc == 0), stop=(kc == 7), r=[*self.hk(b), wk], w=[pk])
        f, fk = self.nextF()
        self.act(f[:], ps[:], AF.Tanh, r=[pk], w=[fk], scale=0.5)
        self.stt(dst[:, tile_rel, :], f[:], 1.0, ps[:], ALU.add, ALU.mult, r=[fk, pk], w=[(dkey, tile_rel)])

    def alloc_l0(self):
        sb = self.P.sb
        self.wA = sb("wA", [128, 8, 1184], BF16)
        self.wuq = sb("wuq", [128, 3, 768], BF16)
        self.wukv = sb("wukv", [128, 2, 1024], BF16)
        self.cqn = sb("cqn", [128, 3, NT], BF16)
        self.ckvn = sb("ckvn", [128, 2, TK], BF16)
        self.krT = sb("krT", [128, TK], BF16)
        self.qTh = [sb("qTh%d" % k, [128, NT], BF16) for k in range(2)]
        self.kTh = [sb("kTh%d" % k, [128, TK], BF16) for k in range(2)]
        self.vh = [sb("vh%d" % k, [128, 14, 66], BF16) for k in range(2)]
        for k in range(2):
            self.memset(self.vh[k][:, :, 64:65], 1.0, w=["vh%d" % k])

    def alloc_l0_shared(self):
        sb = self.P.sb
        self.uf = sb("uf", [128, 3, 512], F32)
        self.rb = sb("rb", [128, 512], F32)
        self.sq = [sb("sq%d" % k, [128, 512], BF16) for k in range(2)]
        self.wB = sb("wB", [128, 8, 1280], BF16)
        self.ga = sb("ga", [128, 12, 512], BF16)
        self.pT = [sb("pT%d" % k, [128, 10, 512], BF16) for k in range(2)]

    def load_l0_weights(self):
        i = self.inp
        win = i["e_w_in"].rearrange("(kc p) n -> p kc n", p=128)
        self.dma(self.wA[:], win[:, :, 0:1184], w=["wA"], eng="pool")
        self.dma(self.wuq[:], i["e_w_uq"].rearrange("(kc p) n -> p kc n", p=128), w=["wuq"], eng="pool")
        self.dma(self.wukv[:], i["e_w_ukv"].rearrange("(kc p) n -> p kc n", p=128), w=["wukv"], eng="pool")
        self.dma(self.wB[:], win[:, :, 1184:2464], w=["wB"], eng="pool")

    def rms_lane(self, L, projs, n, n_feat, lhs_ones, lk, g_fn, gk, consume):
        last = len(projs) - 1
        ps, pk = L["pb"]
        psy, yk = L["yb"]
        rb, rbk = L["rb"]
        for c, proj in enumerate(projs):
            uf, ufk = L["uf"][c]
            sq, sqk = L["sq"]
            proj(ps[:, :n], pk)
            yield
            self.cp(uf, ps[:, :n], r=[pk], w=ufk, eng="act")
            self.act(sq, ps[:, :n], AF.Square, r=[pk], w=[sqk])
            yield
            self.mm(psy[:, :n], lhs_ones, sq, start=(c == 0), stop=(c == last), r=[lk, sqk], w=[yk])
        yield
        self.act(rb, psy[:, :n], AF.Ln, r=[yk], w=rbk, scale=1.0 / n_feat, bias=EPS)
        self.act(rb, rb, AF.Exp, r=rbk, w=rbk, scale=-0.5)
        yield
        for c in range(len(projs)):
            uf, ufk = L["uf"][c]
            un, unk = L["un"]
            self.stt(un, uf, g_fn(c), rb, ALU.mult, ALU.mult, r=ufk + [gk] + rbk, w=unk)
            yield from consume(c, un, unk)

    def prepA_group(self):
        hT, wA = self.hT, self.wA

        def f32view(k, i):
            ap = self.pT[k][:, 2 * i:2 * i + 2, :].rearrange("p a b -> p (a b)").bitcast(F32)
            return ap, [("pT%d" % k, 2 * i), ("pT%d" % k, 2 * i + 1)]
        lanes = []
        banks = [(self.psX[0], "psX0"), (self.psX[1], "psX1"), (self.psX[2], "psX2"), (self.psS[0], "psS0")]
        for k in range(2):
            L = {"pb": banks[2 * k], "yb": banks[2 * k + 1], "sq": (self.sq[k][:], "sq%d" % k)}
            L["uf"] = [f32view(k, i) for i in range(3)]
            L["rb"] = f32view(k, 3)
            L["un"] = f32view(k, 4)
            lanes.append(L)

        def lane_cq():
            for b in range(3):
                q0 = b * 512
                rhs = lambda kc, b=b: hT[:, kc, b * 512:(b + 1) * 512]
                mk = lambda col, b=b, rhs=rhs: (lambda ps, pk: self.proj_fm(
                    ps, pk, lambda kc: wA[:, kc, col:col + 128], "wA", rhs, [*self.hk(b)], 8))

                def cons(c, un, unk, q0=q0):
                    self.cp(self.cqn[:, c, q0:q0 + 512], un, r=unk, w=["cqn"], eng="pool")
                    yield
                yield from self.rms_lane(lanes[0], [mk(c * 128) for c in range(3)], 512, 384, self.ones_b[:], "onesb",
                                         lambda c: self.qn_g[:, c:c + 1], "qng", cons)

        def lane_ckv():
            for b in range(3):
                isS = b > 0
                d0 = kcol(b)
                rhs = lambda kc, b=b: hT[:, kc, b * 512:(b + 1) * 512]
                mk = lambda col, b=b, rhs=rhs: (lambda ps, pk: self.proj_fm(
                    ps, pk, lambda kc: wA[:, kc, col:col + 128], "wA", rhs, [*self.hk(b)], 8))

                def cons(c, un, unk, d0=d0, isS=isS):
                    self.cp(self.ckvn[:, c, d0:d0 + 512], un, r=unk, w=["ckvn"], eng="pool")
                    yield
                    if not isS:
                        for j in range(4):
                            self.fm_to_out(self.ckvn[:, c, d0 + j * 128:d0 + (j + 1) * 128], ["ckvn"], 128,
                                           self.outp["o_ckv"][j * 128:(j + 1) * 128, c * 128:(c + 1) * 128])
                            yield
                yield from self.rms_lane(lanes[1], [mk(384 + c * 128) for c in range(2)], 512, 256, self.ones_b[:],
                                         "onesb", lambda c: self.kvn_g[:, c:c + 1], "kvng", cons)

        def lane_kr_gates():
            ps, pk = self.psS[1], "psS1"
            ps2, pk2 = self.psO[0], "psO0"
            for b in range(3):
                isS = b > 0
                d0 = kcol(b)
                pos0 = (b - 1) * 512
                rhs = lambda kc, b=b: hT[:, kc, b * 512:(b + 1) * 512]
                self.proj_fm(ps[0:32, :], pk, lambda kc: wA[:, kc, 640:672], "wA", rhs, [*self.hk(b)], 8)
                yield
                if isS:
                    f1, k1 = self.nextF()
                    self.cp(self.ropeb[0:32, :], ps[0:32, :], r=[pk], w=["ropeb"])
                    self.tt(f1[0:32, :], ps[0:32, :], self.cos32[0:32, pos0:pos0 + 512], ALU.mult, r=[pk, "cos32"], w=[k1])
                    yield
                    self.mm(ps2[0:32, :], self.psw[0:32, 0:32], self.ropeb[0:32, :], r=["psw", "ropeb"], w=[pk2])
                    yield
                    f2, k2 = self.nextF()
                    self.tt(f2[0:32, :], ps2[0:32, :], self.sin32[0:32, pos0:pos0 + 512], ALU.mult, r=[pk2, "sin32"], w=[k2])
                    self.tt(self.krT[0:32, d0:d0 + 512], f1[0:32, :], f2[0:32, :], ALU.add, r=[k1, k2], w=["krT"], eng="pool")
                else:
                    self.cp(self.krT[0:32, d0:d0 + 512], ps[0:32, :], r=[pk], w=["krT"])
                    yield
                    for j in range(4):
                        self.fm_to_out(self.krT[0:32, d0 + j * 128:d0 + (j + 1) * 128], ["krT"], 32,
                                       self.outp["o_kr"][j * 128:(j + 1) * 128, :])
                        yield
                for j in range(4):
                    t = b * 4 + j
                    for kc in range(8):
                        self.mm(ps[:], hT[:, kc, t * 128:(t + 1) * 128], wA[:, kc, 672:1184],
                                start=(kc == 0), stop=(kc == 7), r=[*self.hk(b), "wA"], w=[pk])
                    yield
                    f, fk = self.nextF()
                    self.act(f[:], ps[:], AF.Tanh, r=[pk], w=[fk], scale=0.5)
                    yield
                    self.stt(self.ga[:, t, :], f[:], 1.0, ps[:], ALU.add, ALU.mult, r=[fk, pk], w=[("ga", t)])

        gens = [lane_cq(), lane_ckv(), lane_kr_gates()]
        while gens:
            for g_ in list(gens):
                try:
                    next(g_)
                except StopIteration:
                    gens.remove(g_)
            yield
        self.dma(self.ckvn[:, :, 512:768], self.inp["c_ckvT"].rearrange("(c p) t -> p c t", p=128), w=["ckvn"],
                 eng="pool")
        self.dma(self.krT[0:32, 512:768], self.inp["c_krT"], w=["krT"], eng="pool")

    def prepA_head(self, h):
        T = TK
        qT, qk = self.qTh[h % 2], "qTh%d" % (h % 2)
        kT, kk = self.kTh[h % 2], "kTh%d" % (h % 2)
        vh, vk = self.vh[h % 2], "vh%d" % (h % 2)
        for b in range(3):
            q0 = b * 512
            ps, pk = self.nextX()
            self.proj_fm(ps[0:96, :], pk, lambda kc: self.wuq[:, kc, h * 96:(h + 1) * 96], "wuq",
                         lambda kc: self.cqn[:, kc, q0:q0 + 512], ["cqn"], 3)
            yield
            if b > 0:
                self.cp(qT[0:64, q0:q0 + 512], ps[0:64, :], r=[pk], w=[qk])
                yield from self.rope_fm(ps[64:96, :], [pk], (64, 96), 512, (b - 1) * 512, self.cos32, "cos32",
                                        self.sin32, "sin32", qT[64:96, q0:q0 + 512], [qk])
            else:
                self.cp(qT[0:96, q0:q0 + 512], ps[0:96, :], r=[pk], w=[qk])
            yield
        t0 = 0
        while t0 < T:
            n = min(512, T - t0)
            ps, pk = self.nextX()
            self.proj_fm(ps[0:64, :n], pk, lambda kc: self.wukv[:, kc, h * 128:h * 128 + 64], "wukv",
                         lambda kc, t0=t0, n=n: self.ckvn[:, kc, t0:t0 + n], ["ckvn"], 2)
            yield
            self.cp(kT[0:64, t0:t0 + n], ps[0:64, :n], r=[pk], w=[kk])
            t0 += n
        self.cp(kT[64:96, 0:T], self.krT[0:32, 0:T], r=["krT"], w=[kk], eng="pool")
        yield
        ntc = T // 128
        tc0 = 0
        while tc0 < ntc:
            g = min(7, ntc - tc0)
            ps, pk = self.nextX()
            for j in range(g):
                tc = tc0 + j
                for kc in range(2):
                    self.mm(ps[:, j * 64:(j + 1) * 64], self.ckvn[:, kc, tc * 128:(tc + 1) * 128],
                            self.wukv[:, kc, h * 128 + 64:(h + 1) * 128], start=(kc == 0), stop=(kc == 1),
                            r=["ckvn", "wukv"], w=[pk])
            yield
            self.cp(vh[:, tc0:tc0 + g, 0:64], ps[:, 0:g * 64].rearrange("p (a b) -> p a b", b=64), r=[pk], w=[vk])
            tc0 += g

    def phase_A(self):
        self.bg = self.prepA_group()
        self.run_bg()
        scale = 96.0 ** -0.5
        self.bg = self.prepA_head(0)
        self.run_bg()
        for h in range(8):
            if h < 7:
                self.bg = self.prepA_head(h + 1)
            vh = self.vh[h % 2]
            self.attend_std(self.kTh[h % 2], "kTh%d" % (h % 2), (0, 96), self.qTh[h % 2], "qTh%d" % (h % 2),
                            lambda tc, vh=vh: vh[:, tc, 0:65], "vh%d" % (h % 2), 64, scale, self.ga, "ga", h * 64)
            self.run_bg()
        for j in range(12):
            self.tm_to_mixT(self.ga[:, j, :], ("ga", j), j, 0, 4)

    def attend_std(self, kT, kk, krows, qT, qk, v_fn, vk, dv, scale, dst, dkey, col0):
        segs = SEGS

        st_ = {}

        def block(nqt, reg, dv_):
            h_ = self.rdi
            self.rdi = (h_ + 1) % 2
            po0, ok0 = reg(0)
            bank = self.psO[int(ok0[-1])]
            den = bank[:, 0:nqt * (dv_ + 1)].rearrange("p (q w) -> p q w", w=dv_ + 1)[:, :, dv_]
            f = self.rden[:, h_ * 4:h_ * 4 + nqt]
            fk = ("rden", h_)
            self.ts(f, den, 2.0, None, ALU.mult, r=[ok0], w=[fk])
            self.recip(f, f, r=[fk], w=[fk])
            st_["h"] = h_
            st_["q"] = 0

        def out_fn(qtile, po, ok):
            h_ = st_["h"]
            q_ = st_["q"]
            st_["q"] = q_ + 1
            f = self.rden[:, h_ * 4 + q_:h_ * 4 + q_ + 1]
            fk = ("rden", h_)
            self.stt(dst[:, qtile, col0:col0 + dv], po[:, 0:dv], f, dst[:, qtile, col0:col0 + dv],
                     ALU.mult, ALU.mult, r=[ok, fk, (dkey, qtile)], w=[(dkey, qtile)])
        out_fn.block = block
        self.attention(segs, kT, kk, krows, qT, qk, v_fn, vk, dv, scale, out_fn)

    def alloc_l0b(self):
        sb = self.P.sb
        self.qbT = sb("qbT", [128, 4, NT], BF16)
        self.kbT = sb("kbT", [128, TK], BF16)
        self.kbX = sb("kbX", [128, TK], BF16)
        self.vb = sb("vb", [128, 14, 2, 66], BF16)
        self.memset(self.vb[:, :, :, 64:65], 1.0, w=["vb"])
        banks = [(self.psX[0], "psX0"), (self.psX[1], "psX1"), (self.psX[2], "psX2"),
                 (self.psS[0], "psS0"), (self.psS[1], "psS1"), (self.psO[0], "psO0")]
        self.lanesB = []
        for i in range(3):
            L = {"pb": banks[2 * i], "yb": banks[2 * i + 1]}
            for nm, dt_ in (("uf", F32), ("rb", F32), ("un", F32), ("f1", F32), ("f2", F32), ("sq", BF16), ("ropeb", BF16)):
                t_ = sb("lb%d%s" % (i, nm), [128, 512], dt_)
                L[nm] = (t_[:], "lb%d%s" % (i, nm))
            self.lanesB.append(L)

    def prepB_group(self):
        hT, wB = self.hT, self.wB
        lanes = self.lanesB

        def chain(L, b, c):
            isS = b > 0
            tb0 = b * 512
            d0 = kcol(b)
            q0 = b * 512
            pos0 = (b - 1) * 512
            ps, pk = L["pb"]
            psy, yk = L["yb"]
            uf, ufk = L["uf"]
            sq, sqk = L["sq"]
            rb, rbk = L["rb"]
            un, unk = L["un"]
            col = c * 128
            self.proj_fm(ps[:], pk, lambda kc: wB[:, kc, col:col + 128], "wB",
                         lambda kc: hT[:, kc, tb0:tb0 + 512], [*self.hk(b)], 8)
            yield
            self.cp(uf, ps[:], r=[pk], w=[ufk], eng="act")
            self.act(sq, ps[:], AF.Square, r=[pk], w=[sqk])
            yield
            self.mm(psy[:], self.bones[:], sq, r=["bones", sqk], w=[yk])
            yield
            self.act(rb, psy[:], AF.Ln, r=[yk], w=[rbk], scale=1.0 / 64, bias=EPS)
            self.act(rb, rb, AF.Exp, r=[rbk], w=[rbk], scale=-0.5)
            yield
            g = self.gq_g if c < 4 else self.gk_g
            gk = "gqg" if c < 4 else "gkg"
            self.stt(un, uf, g[:, 0:1], rb, ALU.mult, ALU.mult, r=[ufk, gk, rbk], w=[unk])
            if c < 4:
                dst, dk = self.qbT[:, c, q0:q0 + 512], ["qbT"]
            else:
                dst, dk = self.kbT[:, d0:d0 + 512], ["kbT"]
            if isS:
                rpb, rpk = L["ropeb"]
                f1, k1 = L["f1"]
                f2, k2 = L["f2"]
                self.cp(rpb, un, r=[unk], w=[rpk])
                self.tt(f1, un, self.cos64[:, pos0:pos0 + 512], ALU.mult, r=[unk, "cos64"], w=[k1])
                yield
                self.mm(ps[:], self.psw[:], rpb, r=["psw", rpk], w=[pk])
                yield
                self.tt(f2, ps[:], self.sin64[:, pos0:pos0 + 512], ALU.mult, r=[pk, "sin64"], w=[k2])
                self.tt(dst, f1, f2, ALU.add, r=[k1, k2], w=dk, eng="pool")
            else:
                self.cp(dst, un, r=[unk], w=dk, eng="pool")
                if c == 4:
                    yield
                    for j in range(4):
                        self.fm_to_out(self.kbT[:, d0 + j * 128:d0 + (j + 1) * 128], ["kbT"], 128,
                                       self.outp["o_gk"][j * 128:(j + 1) * 128, :])
                        yield
            yield

        def lane_gen(L, items):
            for (b, c) in items:
                yield from chain(L, b, c)

        def vg_gen():
            ps, pk = self.psO[1], "psO1"
            for b in range(3):
                isS = b > 0
                d0 = kcol(b)
                for j in range(4):
                    t = b * 4 + j
                    tc = (d0 // 128) + j
                    for kc in range(8):
                        self.mm(ps[:, 0:128], hT[:, kc, t * 128:(t + 1) * 128], wB[:, kc, 640:768],
                                start=(kc == 0), stop=(kc == 7), r=[*self.hk(b), "wB"], w=[pk])
                    yield
                    self.cp(self.vb[:, tc, :, 0:64], ps[:, 0:128].rearrange("p (g d) -> p g d", d=64), r=[pk], w=["vb"])
                    if not isS:
                        f, fk = self.nextF()
                        self.cp(f[:, 0:128], ps[:, 0:128], r=[pk], w=[fk])
                        self.dma(self.outp["o_gv"][t * 128:(t + 1) * 128, :], f[:, 0:128], r=[fk])
                    for kc in range(8):
                        self.mm(ps[:], hT[:, kc, t * 128:(t + 1) * 128], wB[:, kc, 768:1280],
                                start=(kc == 0), stop=(kc == 7), r=[*self.hk(b), "wB"], w=[pk])
                    yield
                    f, fk = self.nextF()
                    self.act(f[:], ps[:], AF.Tanh, r=[pk], w=[fk], scale=0.5)
                    yield
                    self.stt(self.ga[:, t, :], f[:], 1.0, ps[:], ALU.add, ALU.mult, r=[fk, pk], w=[("ga", t)])

        items = [(b, c) for b in range(3) for c in range(5)]
        gens = [lane_gen(lanes[i], items[i::3]) for i in range(3)] + [vg_gen()]
        while gens:
            for g_ in list(gens):
                try:
                    next(g_)
                except StopIteration:
                    gens.remove(g_)
            yield
        self.dma(self.kbT[:, 512:768], self.inp["c_gkT"], w=["kbT"], eng="pool")
        for g in range(2):
            self.dma(self.vb[:, 4:6, g, 0:64],
                     self.inp["c_gv"].rearrange("(tc p) n -> p tc n", p=128)[:, :, g * 64:(g + 1) * 64],
                     w=["vb"], eng="pool")
        self.cp(self.kbX[0:64, 0:TK], self.kbT[64:128, 0:TK], r=["kbT"], w=["kbX"])
        self.cp(self.kbX[64:128, 0:TK], self.kbT[0:64, 0:TK], r=["kbT"], w=["kbX"])

    def phase_B(self):
        self.bg = self.prepB_group()
        self.run_bg()
        for hq in range(8):
            g = hq // 4
            c = hq // 2
            base = (hq % 2) * 64
            if base == g * 64:
                kT, kk = self.kbT, "kbT"
            else:
                kT, kk = self.kbX, "kbX"
            self.attend_std(kT, kk, (base, base + 64), self.qbT[:, c, :], "qbT",
                            lambda tc, g=g: self.vb[:, tc, g, 0:65], "vb", 64, 0.125, self.ga, "ga", hq * 64)
        for j in range(12):
            self.tm_to_mixT(self.ga[:, j, :], ("ga", j), j, 4, 4)

    def load_wout(self, L):
        for kc in range(8):
            self.dma(self.wout[:, kc, :], self.inp[L + "_w_out"][kc * 128:(kc + 1) * 128, :], w=[("wout", kc)], eng="pool")

    def out_proj(self, first):
        for t in range(12):
            b = t // 4
            v = 0 if b == 0 else 1
            xa = self.x1[t]
            xk = "x1_%d" % t
            if first:
                self.dma(xa[:], self.inp["x_in"][t * 128:(t + 1) * 128, :], w=[xk])
            for half in range(2):
                ps, pk = self.nextX()
                for c in range(8):
                    self.mm(ps[:], self.mixT[:, c, t * 128:(t + 1) * 128], self.wout[:, c, half * 512:(half + 1) * 512],
                            start=(c == 0), stop=(c == 7), r=[("mixT", b), ("wout", c)], w=[pk])
                f, fk = self.nextF()
                self.tt(f[:], ps[:], self.gate_bc[v][:, half * 512:(half + 1) * 512], ALU.mult,
                        r=[pk, "gbc%d" % v], w=[fk])
                self.tt(self.x1[t][:, half * 512:(half + 1) * 512], xa[:, half * 512:(half + 1) * 512], f[:], ALU.add,
                        r=[xk, fk], w=["x1_%d" % t], eng="pool")
            self.act(self.junk[:], self.x1[t][:], AF.Square, r=["x1_%d" % t], w=["junk", ("ss", t)],
                     accum_out=self.ss[:, t:t + 1])

    def dump_x(self):
        for t in range(12):
            self.dma(self.outp["y"][t * 128:(t + 1) * 128, :], self.x1[t][:], r=["x1_%d" % t])

    def final_norm(self):
        fg = self.P.sb("fg", [128, D], F32)
        self.dma(fg[:], self.inh["final_g"].ap().partition_broadcast(128), w=["fg"])
        self.ts(self.rstd[:, 0:12], self.ss[:, 0:12], 1.0 / D, EPS, ALU.mult, ALU.add, r=["ss"], w=["rstd"])
        self.tt(self.rstd[:, 0:12], self.rstd[:, 0:12], self.mhalf[:, 0:12], ALU.pow,
                r=["rstd", "mhalf"], w=["rstd"], eng="pool")
        for t in range(12):
            xa = self.x1[t]
            xk = "x1_%d" % t
            self.stt(xa[:], xa[:], self.rstd[:, t:t + 1], fg[:], ALU.mult, ALU.mult, r=[xk, ("rstd", t), "fg"], w=[xk])
            self.dma(self.outp["y"][t * 128:(t + 1) * 128, :], xa[:], r=[xk])


def build_program(upto=99):
    B = Builder(upto)
    nc = B.nc
    stage = [0]

    def go():
        stage[0] += 1
        return not (upto < 0 and stage[0] > -upto)

    with contextlib.ExitStack() as st:
        B.setup(st)
        with contextlib.ExitStack() as s0:
            B.P.stack = s0
            B.xstage = [B.P.sb("xst%d" % k, [128, D], F32) for k in range(12)]
            if go():
                B.adaln("e")
            if go():
                B.norm_to_hT("e", True)
                B.adaln("o")
        B.S.barrier()
        with contextlib.ExitStack() as s1:
            B.P.stack = s1
            B.alloc_l0_shared()
            with contextlib.ExitStack() as s1a:
                B.P.stack = s1a
                B.alloc_l0()
                if go():
                    B.load_l0_weights()
                if go():
                    B.phase_A()
                go()
            B.S.barrier()
            go()
            with contextlib.ExitStack() as s1b:
                B.P.stack = s1b
                B.alloc_l0b()
                if go():
                    B.phase_B()
                go()
        B.S.barrier()
        B.P.stack = st
        B.x1 = [B.P.sb("x1_%d" % i, [128, D], F32) for i in range(12)]
        with contextlib.ExitStack() as so:
            B.P.stack = so
            B.gate_bc = [B.P.sb("gbc%d" % v, [128, D], F32) for v in range(2)]
            B.wout = B.P.sb("wout", [128, 8, D], BF16)
            if go():
                B.load_wout("e")
                B.gate_bcast("e")
                B.out_proj(True)
        B.S.barrier()
        B.P.stack = st
        if upto == 0:
            B.dump_x()
        elif upto > 0:
            B.layer1()
            B.final_norm()
        print("ops:", {e: len(v) for e, v in B.S.ops.items()}, "waits:", B.S.n_waits)
        sems = {t: st.enter_context(nc.semaphore("s_" + t)) for t in B.S.sem_names()}
        print("semaphores:", len(sems), "sw dmas:", B.S.n_sw)
        with nc.Block() as block:
            B.S.finish(sems, block)
    return nc


def _consts():
    k = {}
    k["k_ident"] = np.eye(128, dtype=np.float32)
    bo = np.zeros((128, 128), np.float32)
    bo[:64, :64] = 1
    bo[64:, 64:] = 1
    k["k_bones"] = bo
    ps = np.zeros((128, 128), np.float32)
    for i in range(128):
        ps[i, i ^ 1] = 1
    k["k_psw"] = ps
    s_ = np.arange(64)[:, None]
    t_ = np.arange(64)[None, :]
    k["k_maskf"] = (s_ <= t_).astype(np.float32)
    k["k_maskb"] = (s_ >= t_).astype(np.float32)

    def rope_tab(rot):
        n = 1024
        rows = (np.arange(n) // 64).astype(np.float32)
        cols = (np.arange(n) % 64).astype(np.float32)
        half = rot // 2
        inv = (1.0 / (10000.0 ** (np.arange(0, half, 2, dtype=np.float32) / half))).astype(np.float32)
        ang = np.concatenate([rows[:, None] * inv, cols[:, None] * inv], axis=-1)
        cos = np.cos(ang).astype(np.float32)
        sin = np.sin(ang).astype(np.float32)
        cosT = np.repeat(cos, 2, axis=1).T
        sinT = np.repeat(sin, 2, axis=1).T.copy()
        sinT[0::2] *= -1.0
        return np.ascontiguousarray(cosT), np.ascontiguousarray(sinT)

    c64, s64 = rope_tab(64)
    k["k_cos64"] = np.ascontiguousarray(np.tile(c64, (2, 1)))
    k["k_sin64"] = np.ascontiguousarray(np.tile(s64, (2, 1)))
    c32, s32 = rope_tab(32)
    z = np.zeros((128, 1024), np.float32)
    z[0:32] = c32
    z[64:96] = c32
    k["k_cos32"] = z
    z = np.zeros((128, 1024), np.float32)
    z[0:32] = s32
    z[64:96] = s32
    k["k_sin32"] = z
    return k


def _fm(vec, nch):
    return np.ascontiguousarray(np.asarray(vec, np.float32).reshape(nch, 128).T)


_NC_CACHE = {}


def kernel(**inp):
    f = lambda a: np.asarray(a, np.float32)
    consts = _consts()
    shared = dict(consts)
    shared["e_norm_g"] = _fm(inp["e_norm_g"][0], 8)
    shared["e_ada_w"] = f(inp["e_ada_w"][0])
    shared["e_ada_b"] = _fm(inp["e_ada_b"][0], 24)
    shared["e_w_in"] = f(inp["e_w_in"][0])
    shared["e_qn_g"] = _fm(inp["e_mla_qnorm_g"][0], 3)
    shared["e_w_uq"] = f(inp["e_mla_w_uq"][0])
    shared["e_kvn_g"] = _fm(inp["e_mla_kvnorm_g"][0], 2)
    shared["e_w_ukv"] = f(inp["e_mla_w_ukv"][0])
    shared["e_gq_g"] = np.ascontiguousarray(np.tile(f(inp["e_gqa_qnorm_g"][0]), 2)[:, None])
    shared["e_gk_g"] = np.ascontiguousarray(np.tile(f(inp["e_gqa_knorm_g"][0]), 2)[:, None])
    shared["e_w_out"] = f(inp["e_w_out"][0])
    shared["o_norm_g"] = _fm(inp["o_norm_g"][0], 8)
    shared["o_ada_w"] = f(inp["o_ada_w"][0])
    shared["o_ada_b"] = _fm(inp["o_ada_b"][0], 24)
    shared["o_w_in"] = f(inp["o_w_in"][0])
    shared["o_gate_b"] = np.ascontiguousarray(f(inp["o_mlstm_gate_b"][0]).reshape(4, 4).T)
    shared["o_mn_g"] = f(inp["o_mlstm_norm_g"][0])[None, :]
    shared["o_lambda"] = f(inp["o_diff_lambda"][0]).reshape(1, 256)
    shared["o_dn_g"] = f(inp["o_diff_norm_g"][0])[None, :]
    shared["o_w_out"] = f(inp["o_w_out"][0])
    shared["final_g"] = f(inp["final_norm_g"])[None, :]
    xp = f(inp["x_prompt"])
    xs = f(inp["x_sample"])
    in_maps = []
    for core in range(8):
        sq = core // 4
        m = dict(shared)
        m["x_in"] = np.ascontiguousarray(np.concatenate([xp[2 * core], xp[2 * core + 1], xs[sq]], axis=0))
        cv = np.stack([f(inp["c_ctx"]), f(inp["c"][sq])], axis=-1)
        m["c2"] = np.ascontiguousarray(cv.reshape(8, 128, 2).transpose(1, 0, 2))
        m["c_ckvT"] = np.ascontiguousarray(f(inp["cache_mla_ckv"][sq, 0]).T)
        m["c_krT"] = np.ascontiguousarray(f(inp["cache_mla_krope"][sq, 0]).T)
        m["c_gkT"] = np.ascontiguousarray(f(inp["cache_gqa_k"][sq, 0]).reshape(256, 128).T)
        m["c_gv"] = np.ascontiguousarray(f(inp["cache_gqa_v"][sq, 0]).reshape(256, 128))
        m["c_ST"] = np.ascontiguousarray(f(inp["state_mlstm_C"][sq, 0]).reshape(8, 128, 128).transpose(0, 2, 1))
        m["c_n"] = np.ascontiguousarray(f(inp["state_mlstm_n"][sq, 0]).reshape(8, 128))
        m["c_m"] = np.ascontiguousarray(f(inp["state_mlstm_m"][sq, 0]).reshape(2, 4).T)
        m["c_kdT"] = np.ascontiguousarray(f(inp["cache_diff_k"][sq, 0]).transpose(1, 2, 0))
        m["c_vd"] = np.ascontiguousarray(f(inp["cache_diff_v"][sq, 0]).reshape(256, 512))
        in_maps.append(m)
    if "nc" not in _NC_CACHE:
        _NC_CACHE["nc"] = build_program()
    nc = _NC_CACHE["nc"]
    if _NC_CACHE.get("debug_cores"):
        ncore = _NC_CACHE["debug_cores"]
        res = run_bass_kernel_spmd(nc, in_maps[:ncore], core_ids=list(range(ncore)))
        return res.results
    res = run_bass_kernel_spmd(nc, in_maps, core_ids=list(range(8)))
    R = res.results
    y_prompt = np.stack([R[c]["y"][s * 256:(s + 1) * 256] for c in range(8) for s in range(2)], axis=0)
    y_sample = np.stack([R[0]["y"][512:], R[4]["y"][512:]], axis=0)

    def gather(name, shape):
        return np.stack([R[c][name][s * 256:(s + 1) * 256].reshape(shape) for c in range(8) for s in range(2)],
                        axis=0)[:, None]
    st_ckv = gather("o_ckv", (256, 256))
    st_kr = gather("o_kr", (256, 32))
    st_gk = gather("o_gk", (256, 2, 64))
    st_gv = gather("o_gv", (256, 2, 64))
    st_dk = gather("o_dk", (256, 4, 128))
    st_dv = gather("o_dv", (256, 4, 128))
    st_C = np.stack([R[c]["o_C"][s * 8:(s + 1) * 8].reshape(2, 4, 128, 128) for c in range(8) for s in range(2)],
                    axis=0)[:, None]
    st_n = np.stack([R[c]["o_n"][s * 8:(s + 1) * 8].reshape(2, 4, 128) for c in range(8) for s in range(2)],
                    axis=0)[:, None]
    st_m = np.stack([R[c]["o_m"][:, s * 2:(s + 1) * 2].T.reshape(2, 4) for c in range(8) for s in range(2)],
                    axis=0)[:, None]
    outs = (y_prompt, y_sample, st_ckv, st_kr, st_gk, st_gv, st_C, st_n, st_m, st_dk, st_dv)
    return tuple(np.ascontiguousarray(o, dtype=np.float32) for o in outs)


LAM_INIT = 0.8 - 0.6 * float(np.exp(-0.3 * 1))
TILE_SEQS = [(0, 2, False), (2, 4, False), (4, 12, True)]


def _l1_setup(self):
    sb = self.P.sb
    i = self.inp
    self.wG = sb("wG", [128, 8, 16], BF16)
    self.dma(self.wG[:], i["o_w_in"].rearrange("(kc p) n -> p kc n", p=128)[:, :, 2048:2064], w=["wG"], eng="pool")
    self.gb = sb("gb", [4, 4], F32)
    self.dma(self.gb[:], i["o_gate_b"], w=["gb"])
    self.cm = sb("cm", [4, 2], F32)
    self.dma(self.cm[:], i["c_m"], w=["cm"])
    self.scal = sb("scal", [128, 192], F32)
    self.decbc = sb("decbc", [128, 192], F32)
    self.mout = sb("mout", [4, 4], F32)
    self.mng = sb("mng", [128, 512], F32)
    self.dma(self.mng[:], self.inh["o_mn_g"].ap().partition_broadcast(128), w=["mng"])
    self.ts(self.mng[:], self.mng[:], 0.5, None, ALU.mult, r=["mng"], w=["mng"])
    self.dng = sb("dng", [128, 128], F32)
    self.dma(self.dng[:], self.inh["o_dn_g"].ap().partition_broadcast(128), w=["dng"])
    self.ts(self.dng[:], self.dng[:], 0.5 * (1.0 - LAM_INIT), None, ALU.mult, r=["dng"], w=["dng"])
    lam = sb("lam", [128, 256], F32)
    self.dma(lam[:], self.inh["o_lambda"].ap().partition_broadcast(128), w=["lam"])
    self.nlam = sb("nlam", [128, 4], F32)
    pr = sb("lampr", [128, 128], F32)
    self.tt(pr[:].rearrange("p (a b) -> p a b", b=64), lam[:].rearrange("p (a b) -> p a b", b=128)[:, :, 0:64],
            lam[:].rearrange("p (a b) -> p a b", b=128)[:, :, 64:128], ALU.mult, r=["lam"], w=["lampr"])
    self.reduce(self.nlam[:, 0:2], pr[:].rearrange("p (a b) -> p a b", b=64), ALU.add, r=["lampr"], w=["nlam"])
    self.act(self.nlam[:, 0:2], self.nlam[:, 0:2], AF.Exp, r=["nlam"], w=["nlam"])
    self.tt(self.nlam[:, 2:3], self.nlam[:, 1:2], self.nlam[:, 0:1], ALU.subtract, r=["nlam"], w=["nlam"])
    self.ts(self.nlam[:, 2:3], self.nlam[:, 2:3], -LAM_INIT, None, ALU.add, r=["nlam"], w=["nlam"])
    self.mask2 = {}
    for nm in ("maskf", "maskb"):
        m2 = sb(nm + "2", [128, 64], F32)
        self.dma(m2[0:64, :], i["k_" + nm], w=[nm + "2"])
        self.dma(m2[64:128, :], i["k_" + nm], w=[nm + "2"])
        self.mask2[nm] = m2
    self.wL = [sb("wL%d" % k, [128, 8, 640], BF16) for k in range(2)]


def _load_head_weights(self, slot, kind, h):
    win = self.inp["o_w_in"].rearrange("(kc p) n -> p kc n", p=128)
    w = self.wL[slot]
    wk = "wL%d" % slot
    if os.environ.get("MERGEW", "0") == "1":
        o0 = 0 if kind == "C" else 2576
        self.dma(w[:, :, 0:512].rearrange("p kc (g c) -> p kc g c", c=128),
                 win[:, :, o0:o0 + 2048].rearrange("p kc (g c) -> p kc g c", c=512)[:, :, :, h * 128:(h + 1) * 128],
                 w=[wk], eng="pool")
        if kind == "C":
            self.dma(w[:, :, 512:640], win[:, :, 2064 + h * 128:2064 + (h + 1) * 128], w=[wk], eng="pool")
        return
    if kind == "C":
        offs = [0, 512, 1024, 1536, 2064]
    else:
        offs = [2576, 3088, 3600, 4112]
    for j, o in enumerate(offs):
        self.dma(w[:, :, j * 128:(j + 1) * 128], win[:, :, o + h * 128:o + (h + 1) * 128], w=[wk], eng="pool")


def _gate_chain(self, G, d, k):
    sb = self.P.sb
    ntok = G["ntok"]
    nch = ntok // 64
    tok0 = G["tok0"]
    tile0 = tok0 // 128
    ch0 = tok0 // 64
    tl = self.gts[k]
    bank, bkey = self.gbanks[k]
    K_ = str(k)
    small = tl["small"]
    sm = lambda k: small[:, k * 16:k * 16 + nch]
    dexp = tl["dexp"]

    if True:
        for gi, nm in ((d * 2, "ig"), (d * 2 + 1, "lf")):
            for bi, b in enumerate(G["batches"]):
                ps, pk = bank, bkey
                for kc in range(8):
                    self.mm(ps[0:4, :], self.wG[:, kc, gi * 4:(gi + 1) * 4], self.hT[:, kc, b * 512:(b + 1) * 512],
                            start=(kc == 0), stop=(kc == 7), r=["wG", *self.hk(b)], w=[pk])
                yield
                self.ts(tl[nm][:, bi * 512:(bi + 1) * 512], ps[0:4, :], self.gb[:, gi:gi + 1], None, ALU.add,
                        r=[pk, "gb"], w=["g_" + nm + K_])
        t_ = tl["lf"]
        yield
        self.act(t_[:, :ntok], t_[:, :ntok], AF.Exp, r=["g_lf" + K_], w=["g_lf" + K_], scale=-1.0)
        self.act(t_[:, :ntok], t_[:, :ntok], AF.Ln, r=["g_lf" + K_], w=["g_lf" + K_], bias=1.0)
        yield
        self.ts(t_[:, :ntok], t_[:, :ntok], -1.0, None, ALU.mult, r=["g_lf" + K_], w=["g_lf" + K_])
    rm = self.g_rm
    c3 = lambda ap: ap.rearrange("p (c t) -> p c t", t=64)
    if True:
        ig_t, lf_t, cum_t = tl["ig"], tl["lf"], tl["cum"]
        ik, lk, ck = "g_ig" + K_, "g_lf" + K_, "g_cum" + K_
        self.scan(cum_t[:, :ntok], rm[:, :ntok], lf_t[:, :ntok], 0.0, ALU.mult, ALU.add, r=["g_rm", lk], w=[ck])
        tot, A, mseq, mprev, Gm, dec = sm(0 + d), sm(2 + d), sm(4 + d), sm(6 + d), sm(8 + d), sm(10 + d)
        self.cp(tot, c3(cum_t[:, :ntok])[:, :, 63], r=[ck], w=["g_small" + K_])
        if d == 1:
            self.tt(c3(cum_t[:, :ntok]), tot.unsqueeze(2).broadcast_to([4, nch, 64]), c3(cum_t[:, :ntok]), ALU.subtract,
                    r=["g_small" + K_, ck], w=[ck])
            self.tt(cum_t[:, :ntok], cum_t[:, :ntok], lf_t[:, :ntok], ALU.add, r=[ck, lk], w=[ck])
        self.tt(ig_t[:, :ntok], ig_t[:, :ntok], cum_t[:, :ntok], ALU.subtract, r=[ik, ck], w=[ik])
        self.reduce(A, c3(ig_t[:, :ntok]), ALU.max, r=[ik], w=["g_small" + K_])
        for (t0, t1, has_ctx) in TILE_SEQS:
            c0, c1 = t0 * 2 - ch0, t1 * 2 - ch0
            if c0 < 0 or c1 > nch:
                continue
            if has_ctx:
                init = self.cm[:, d:d + 1]
                self_k = ["cm"]
            else:
                init = 0.0
                self_k = []
            if d == 0:
                self.scan(mseq[:, c0:c1], A[:, c0:c1], tot[:, c0:c1], init, ALU.max, ALU.add,
                          r=["g_small" + K_] + self_k, w=["g_small" + K_])
                if has_ctx:
                    self.cp(mprev[:, c0:c0 + 1], init, r=["cm"], w=["g_small" + K_])
                else:
                    self.memset(mprev[:, c0:c0 + 1], 0.0, w=["g_small" + K_])
                self.cp(mprev[:, c0 + 1:c1], mseq[:, c0:c1 - 1], r=["g_small" + K_], w=["g_small" + K_])
            else:
                self.scan(mseq[:, c0:c1][:, ::-1], A[:, c0:c1][:, ::-1], tot[:, c0:c1][:, ::-1], init, ALU.max, ALU.add,
                          r=["g_small" + K_] + self_k, w=["g_small" + K_])
                if has_ctx:
                    self.cp(mprev[:, c1 - 1:c1], init, r=["cm"], w=["g_small" + K_])
                else:
                    self.memset(mprev[:, c1 - 1:c1], 0.0, w=["g_small" + K_])
                self.cp(mprev[:, c0:c1 - 1], mseq[:, c0 + 1:c1], r=["g_small" + K_], w=["g_small" + K_])
            if not has_ctx:
                s_idx = t0 // 2
                last = mseq[:, c1 - 1:c1] if d == 0 else mseq[:, c0:c0 + 1]
                self.cp(self.mout[:, s_idx * 2 + d:s_idx * 2 + d + 1], last, r=["g_small" + K_], w=["mout"])
        self.tt(Gm, mprev, A, ALU.max, r=["g_small" + K_], w=["g_small" + K_])
        self.tt(dec, mprev, Gm, ALU.subtract, r=["g_small" + K_], w=["g_small" + K_])
        yield
        self.act(dec, dec, AF.Exp, r=["g_small" + K_], w=["g_small" + K_])
        yield
        Gbc = Gm.unsqueeze(2).broadcast_to([4, nch, 64])
        self.tt(c3(ig_t[:, :ntok]), c3(ig_t[:, :ntok]), Gbc, ALU.subtract, r=[ik, "g_small" + K_], w=[ik])
        yield
        self.act(ig_t[:, :ntok], ig_t[:, :ntok], AF.Exp, r=[ik], w=[ik])
        self.tt(c3(cum_t[:, :ntok]), c3(cum_t[:, :ntok]), Gbc, ALU.add, r=[ck, "g_small" + K_], w=[ck])
        yield
        self.act(cum_t[:, :ntok], cum_t[:, :ntok], AF.Exp, r=[ck], w=[ck], scale=-1.0)
        yield
        ps, pk = bank, bkey
        ntile = ntok // 128
        for tl_i in range(ntile):
            for q, src, sk in ((0, ig_t, ik), (1, cum_t, ck)):
                col = (tl_i * 2 + q) * 4
                self.mm(ps[:, col:col + 4], src[0:4, tl_i * 128:(tl_i + 1) * 128], self.ident_f[0:4, 0:4],
                        r=[sk, "identf"], w=[pk])
        yield
        for tl_i in range(ntile):
            base = (((tile0 + tl_i) * 2 + d) * 2) * 4
            self.cp(self.scal[:, base:base + 8], ps[:, tl_i * 8:tl_i * 8 + 8], r=[pk], w=["scal"])
        self.tt(dexp[:, :nch, :], dec.unsqueeze(2).broadcast_to([4, nch, 4]),
                self.ident_f[0:4, 0:4].unsqueeze(1).broadcast_to([4, nch, 4]), ALU.mult,
                r=["g_small" + K_, "identf"], w=["g_dexp" + K_])
        yield
        ps2, pk2 = bank, bkey
        self.mm(ps2[:, 0:nch * 4], self.ones_f[0:4, :], dexp[:, :nch, :].rearrange("p c h -> p (c h)"),
                r=["onesf", "g_dexp" + K_], w=[pk2])
        yield
        self.cp(self.decbc[:, (d * 24 + ch0) * 4:(d * 24 + ch0 + nch) * 4], ps2[:, 0:nch * 4], r=[pk2], w=["decbc"])


def _gate_prep_all(self):
    sb = self.P.sb
    self.g_rm = sb("g_rm", [4, 1024], F32)
    self.memset(self.g_rm[:], 1.0, w=["g_rm"])
    self.memset(self.g_rm[:].rearrange("p (c t) -> p c t", t=64)[:, :, 0:1], 0.0, w=["g_rm"])
    self.gts = []
    for k in range(4):
        t_ = {nm: sb("g_%s%d" % (nm, k), [4, 1024], F32) for nm in ("ig", "lf", "cum")}
        t_["small"] = sb("g_small%d" % k, [4, 256], F32)
        t_["dexp"] = sb("g_dexp%d" % k, [4, 16, 4], F32)
        self.gts.append(t_)
    self.gbanks = [(self.psX[0], "psX0"), (self.psX[1], "psX1"), (self.psX[2], "psX2"), (self.psS[0], "psS0")]
    gens = [self.gate_chain(G, d, gi * 2 + d) for gi, G in enumerate((GP, GS)) for d in range(2)]
    while gens:
        for g_ in list(gens):
            try:
                next(g_)
            except StopIteration:
                gens.remove(g_)


def _phase_C_head(self, h, slot):
    w = self.wL[slot]
    wk = "wL%d" % slot
    hT = self.hT
    qT, kT, ktm, vau, og, zg, hbuf = self.c_qT, self.c_kT, self.c_ktm, self.c_vau, self.c_og, self.c_zg, self.c_hbuf
    dscale = 128.0 ** -0.5
    for b in range(3):
        rhs = lambda kc: hT[:, kc, b * 512:(b + 1) * 512]
        ps, pk = self.nextX()
        self.proj_fm(ps[:], pk, lambda kc: w[:, kc, 0:128], wk, rhs, [*self.hk(b)], 8)
        self.cp(qT[:, b * 512:(b + 1) * 512], ps[:], r=[pk], w=["c_qT"], eng="act")
        ps, pk = self.nextX()
        self.proj_fm(ps[:], pk, lambda kc: w[:, kc, 128:256], wk, rhs, [*self.hk(b)], 8)
        self.ts(kT[:, b * 512:(b + 1) * 512], ps[:], dscale, None, ALU.mult, r=[pk], w=["c_kT"])
    for t in range(12):
        b = t // 4
        ps, pk = self.nextX()
        for kc in range(8):
            self.mm(ps[:], hT[:, kc, t * 128:(t + 1) * 128], w[:, kc, 128:640], start=(kc == 0), stop=(kc == 7),
                    r=[*self.hk(b), wk], w=[pk])
        self.ts(ktm[:, t, :], ps[:, 0:128], dscale, None, ALU.mult, r=[pk], w=[("c_ktm", t)])
        self.cp(vau[:, t, 0:128], ps[:, 128:256], r=[pk], w=[("c_vau", t)], eng="act")
        f, fk = self.nextF()
        self.act(f[:, 0:256], ps[:, 256:512], AF.Tanh, r=[pk], w=[fk], scale=0.5)
        self.ts(og[:, t, :], f[:, 0:128], 0.5, 0.5, ALU.mult, ALU.add, r=[fk], w=[("c_og", t)])
        self.stt(zg[:, t, :], f[:, 128:256], 1.0, ps[:, 384:512], ALU.add, ALU.mult, r=[fk, pk], w=[("c_zg", t)])
    cs = int(os.environ.get("CS", "9"))
    if cs < 2:
        return
    self.mlstm_pre(h)
    self.memset(hbuf[:], 0.0, w=["c_hbuf"], eng="pool")
    chains = []
    for si, (t0, t1, has_ctx) in enumerate(TILE_SEQS):
        for d in range(2):
            ci = si * 2 + d
            Sf = self.c_S[ci]
            Sk = "c_S%d" % ci
            if has_ctx:
                self.dma(Sf[:, 0:128], self.inp["c_ST"][d * 4 + h], w=[Sk])
                self.dma(Sf[:, 128:129], self.inp["c_n"][d * 4 + h].rearrange("(k o) -> k o", o=1), w=[Sk])
            else:
                self.memset(Sf[:], 0.0, w=[Sk])
            chunks = list(range(t0 * 2, t1 * 2))
            if d == 1:
                chunks = chunks[::-1]
            chains.append(dict(ci=ci, d=d, si=si, chunks=chunks, has_ctx=has_ctx))
    nsteps = max(len(c["chunks"]) for c in chains)
    nsteps = min(nsteps, int(os.environ.get("CSTEP", "99")))
    for step in range(nsteps):
        live = [ch for ch in chains if step < len(ch["chunks"])]
        for stage in range(4):
            for ch in live:
                self._mlstm_step(h, ch, ch["chunks"][step], stage)
    if cs < 3:
        return
    for ch in chains:
        if ch["has_ctx"]:
            continue
        ci, d, si = ch["ci"], ch["d"], ch["si"]
        Sf, Sk = self.c_S[ci], "c_S%d" % ci
        idx = (si * 2 + d) * 4 + h
        ps, pk = self.nextX()
        self.S.emit("pe", lambda e, ps=ps, Sf=Sf: e.transpose(out=ps[:, 0:128], in_=Sf[:, 0:128], identity=self.ident_f[:]),
                    reads=[Sk, "identf"], writes=[pk])
        f, fk = self.nextF()
        self.cp(f[:, 0:128], ps[:, 0:128], r=[pk], w=[fk])
        self.dma(self.outp["o_C"][idx], f[:, 0:128], r=[fk])
        self.dma(self.outp["o_n"][idx].rearrange("(k o) -> k o", o=1), Sf[:, 128:129], r=[Sk])
    if cs < 4:
        return
    for t in range(12):
        self.tt(hbuf[:, t, :], hbuf[:, t, :], og[:, t, :], ALU.mult, r=[("c_hbuf", t), ("c_og", t)], w=[("c_hbuf", t)])
        self.act(self.junk[:, 0:128], hbuf[:, t, :], AF.Square, r=[("c_hbuf", t)], w=["junk", ("ss", t)],
                 accum_out=self.ss[:, t:t + 1])
        self.tt(zg[:, t, :], zg[:, t, :], self.mng[:, h * 128:(h + 1) * 128], ALU.mult, r=[("c_zg", t), "mng"],
                w=[("c_zg", t)], eng="pool")
    self.ts(self.rstd[:, 0:12], self.ss[:, 0:12], 1.0 / 128, EPS, ALU.mult, ALU.add, r=["ss"], w=["rstd"])
    self.tt(self.rstd[:, 0:12], self.rstd[:, 0:12], self.mhalf[:, 0:12], ALU.pow, r=["rstd", "mhalf"], w=["rstd"],
            eng="pool")
    for t in range(12):
        xs = self.xs[t % 2]
        xk = "xs%d" % (t % 2)
        self.stt(xs[:, 0:128], hbuf[:, t, :], self.rstd[:, t:t + 1], zg[:, t, :], ALU.mult, ALU.mult,
                 r=[("c_hbuf", t), ("rstd", t), ("c_zg", t)], w=[xk])
        self.tr(self.psT[:, 0:128], xs[:, 0:128], self.ident_b[:], r=[xk, "identb"], w=["psT"])
        self.cp(self.mixT[:, h, t * 128:(t + 1) * 128], self.psT[:, 0:128], r=["psT"], w=[("mixT", t // 4)])


def _mlstm_pre(self, h):
    qT, kT, ktm = self.c_qT, self.c_kT, self.c_ktm
    for t in range(12):
        ps, pk = self.nextY()
        for half in range(2):
            c = t * 2 + half
            cols = slice(c * 64, (c + 1) * 64)
            p0 = half * 64
            self.mm(ps[p0:p0 + 64, 0:64], kT[:, cols], qT[:, cols], r=["c_kT", "c_qT"], w=[pk])
        for d in range(2):
            col = ((t * 2 + d) * 2 + 0) * 4 + h
            e_ap = self.scal[:, col:col + 1]
            mask = self.mask2["maskf" if d == 0 else "maskb"]
            mk = "maskf2" if d == 0 else "maskb2"
            self.stt(self.c_smta[:, t, d, :], ps[:, 0:64], e_ap, mask[:, :], ALU.mult, ALU.mult,
                     r=[pk, "scal", mk], w=[("c_smta", t)])
            self.act(self.c_kpa[:, t, d, :], ktm[:, t, :], AF.Copy, r=[("c_ktm", t), "scal"], w=[("c_kpa", t)],
                     scale=e_ap)


def _mlstm_step(self, h, ch, c, stage):
    ci, d = ch["ci"], ch["d"]
    t = c // 2
    p0 = (c % 2) * 64
    p1 = p0 + 64
    cols = slice(c * 64, (c + 1) * 64)
    qT, vau, hbuf = self.c_qT, self.c_vau, self.c_hbuf
    Sf, Sk = self.c_S[ci], "c_S%d" % ci
    Sb, Sbk = self.c_Sb[ci], "c_Sb%d" % ci
    dd, ddk = self.c_dd[ci], "c_dd%d" % ci
    thr_ap = self.scal[p0:p1, ((t * 2 + d) * 2 + 1) * 4 + h:((t * 2 + d) * 2 + 1) * 4 + h + 1]
    dec_ap = self.decbc[:, (d * 24 + c) * 4 + h:(d * 24 + c) * 4 + h + 1]
    if stage == 0:
        self.act(Sb[:, 0:129], Sf[:, 0:129], AF.Copy, r=[Sk, "decbc"], w=[Sbk], scale=dec_ap)
    elif stage == 1:
        ps2, pk2 = self.nextY()
        ch["ps2"] = (ps2, pk2)
        self.mm(ps2[:, 256:385], self.c_kpa[p0:p1, t, d, :], vau[p0:p1, t, 0:129], r=[("c_kpa", t), ("c_vau", t)], w=[pk2])
        self.mm(ps2[p0:p1, 0:129], qT[:, cols], Sb[:, 0:129], start=True, stop=False, r=["c_qT", Sbk], w=[pk2])
        self.mm(ps2[p0:p1, 0:129], self.c_smta[p0:p1, t, d, :], vau[p0:p1, t, 0:129], start=False, stop=True,
                r=[("c_smta", t), ("c_vau", t)], w=[pk2])
    elif stage == 2:
        ps2, pk2 = ch["ps2"]
        self.stt(Sf[:, 0:129], Sf[:, 0:129], dec_ap, ps2[:, 256:385], ALU.mult, ALU.add, r=[Sk, "decbc", pk2], w=[Sk])
        self.act(dd[p0:p1, 0:1], ps2[p0:p1, 128:129], AF.Abs, r=[pk2], w=[ddk])
    else:
        ps2, pk2 = ch["ps2"]
        self.ts(dd[p0:p1, 0:1], dd[p0:p1, 0:1], thr_ap, None, ALU.max, r=[ddk, "scal"], w=[ddk])
        self.recip(dd[p0:p1, 0:1], dd[p0:p1, 0:1], r=[ddk], w=[ddk])
        self.stt(hbuf[p0:p1, t, :], ps2[p0:p1, 0:128], dd[p0:p1, 0:1], hbuf[p0:p1, t, :], ALU.mult, ALU.add,
                 r=[pk2, ddk, ("c_hbuf", t)], w=[("c_hbuf", t)])


def _nextY(self):
    i = self.yi
    self.yi = (i + 1) % 7
    if i < 3:
        return self.psX[i], "psX%d" % i
    if i < 5:
        return self.psS[i - 3], "psS%d" % (i - 3)
    return self.psO[i - 5], "psO%d" % (i - 5)


def _alloc_C(self):
    sb = self.P.sb
    self.c_qT = sb("c_qT", [128, NT], BF16)
    self.c_kT = sb("c_kT", [128, NT], BF16)
    self.c_ktm = sb("c_ktm", [128, 12, 128], BF16)
    self.c_vau = sb("c_vau", [128, 12, 130], BF16)
    self.c_og = sb("c_og", [128, 12, 128], BF16)
    self.c_zg = sb("c_zg", [128, 12, 128], BF16)
    self.c_hbuf = sb("c_hbuf", [128, 12, 128], F32)
    self.c_S = [sb("c_S%d" % k, [128, 130], F32) for k in range(6)]
    self.c_Sb = [sb("c_Sb%d" % k, [128, 130], BF16) for k in range(6)]
    self.c_smta = sb("c_smta", [128, 12, 2, 64], BF16)
    self.c_kpa = sb("c_kpa", [128, 12, 2, 128], BF16)
    self.c_dd = [sb("c_dd%d" % k, [128, 2], F32) for k in range(6)]
    self.memset(self.c_vau[:, :, 128:129], 1.0, w=["c_vau"])
    self.yi = 0


def _alloc_D(self):
    sb = self.P.sb
    self.d_qT = [sb("d_qT%d" % k, [128, NT], BF16) for k in range(2)]
    self.d_kT = [sb("d_kT%d" % k, [128, TK], BF16) for k in range(2)]
    self.d_v = [sb("d_v%d" % k, [128, 14, 130], BF16) for k in range(2)]
    self.d_g = [sb("d_g%d" % k, [128, 12, 128], BF16) for k in range(2)]
    self.d_a = sb("d_a", [128, 12, 128], F32)
    self.pT = [sb("pT%d" % k, [128, 10, 512], BF16) for k in range(2)]
    for k in range(2):
        self.memset(self.d_v[k][:, :, 128:129], 1.0, w=["d_v%d" % k])


def _prepD(self, h, slot):
    w = self.wL[slot]
    wk = "wL%d" % slot
    hT = self.hT
    qT, kT, dv_, dg = self.d_qT[slot], self.d_kT[slot], self.d_v[slot], self.d_g[slot]
    qk, kk, vk, gk = "d_qT%d" % slot, "d_kT%d" % slot, "d_v%d" % slot, "d_g%d" % slot
    for b in range(3):
        isS = b > 0
        d0 = kcol(b)
        q0 = b * 512
        pos0 = (b - 1) * 512
        rhs = lambda kc, b=b: hT[:, kc, b * 512:(b + 1) * 512]
        for which in range(2):
            ps, pk = self.nextX()
            self.proj_fm(ps[:], pk, lambda kc: w[:, kc, which * 128:(which + 1) * 128], wk, rhs, [*self.hk(b)], 8)
            yield
            if which == 0:
                dst, dk = qT[:, q0:q0 + 512], [qk]
            else:
                dst, dk = kT[:, d0:d0 + 512], [kk]
            if isS:
                yield from self.rope_fm(ps[:], [pk], (0, 128), 512, pos0, self.cos64, "cos64", self.sin64, "sin64", dst, dk)
            else:
                self.cp(dst, ps[:], r=[pk], w=dk)
                yield
                if which == 1:
                    for j in range(4):
                        self.fm_to_out(kT[:, d0 + j * 128:d0 + (j + 1) * 128], [kk], 128,
                                       self.outp["o_dk"][j * 128:(j + 1) * 128, h * 128:(h + 1) * 128])
                        yield
        for j in range(4):
            t = b * 4 + j
            tc = d0 // 128 + j
            ps, pk = self.nextX()
            for kc in range(8):
                self.mm(ps[:, 0:256], hT[:, kc, t * 128:(t + 1) * 128], w[:, kc, 256:512], start=(kc == 0), stop=(kc == 7),
                        r=[*self.hk(b), wk], w=[pk])
            yield
            self.cp(dv_[:, tc, 0:128], ps[:, 0:128], r=[pk], w=[vk])
            f, fk = self.nextF()
            if not isS:
                self.cp(f[:, 256:384], ps[:, 0:128], r=[pk], w=[fk], eng="act")
                self.dma(self.outp["o_dv"][t * 128:(t + 1) * 128, h * 128:(h + 1) * 128], f[:, 256:384], r=[fk])
            self.act(f[:, 0:128], ps[:, 128:256], AF.Tanh, r=[pk], w=[fk], scale=0.5)
            yield
            self.stt(dg[:, t, :], f[:, 0:128], 1.0, ps[:, 128:256], ALU.add, ALU.mult, r=[fk, pk], w=[(gk, t)])
            self.tt(dg[:, t, :], dg[:, t, :], self.dng[:], ALU.mult, r=[(gk, t), "dng"], w=[(gk, t)], eng="pool")
    self.dma(kT[:, 512:768], self.inp["c_kdT"][h], w=[kk], eng="pool")
    self.dma(dv_[:, 4:6, 0:128],
             self.inp["c_vd"].rearrange("(tc p) n -> p tc n", p=128)[:, :, h * 128:(h + 1) * 128], w=[vk],
             eng="pool")


def _phase_D_head(self, h, slot):
    qT, kT, dv_, dg, da = self.d_qT[slot], self.d_kT[slot], self.d_v[slot], self.d_g[slot], self.d_a
    qk, kk, vk, gk = "d_qT%d" % slot, "d_kT%d" % slot, "d_v%d" % slot, "d_g%d" % slot
    for sub in range(2):
        def out_fn(qtile, po, ok, sub=sub):
            j = self.rdi
            self.rdi = (j + 1) % 8
            f, fk = self.rden[:, j:j + 1], ("rden", j)
            self.recip(f, po[:, 128:129], r=[ok], w=[fk])
            if sub == 0:
                self.ts(da[:, qtile, :], po[:, 0:128], f, None, ALU.mult, r=[ok, fk], w=[("d_a", qtile)])
            else:
                self.tt(f, f, self.nlam[:, 2:3], ALU.mult, r=[fk, "nlam"], w=[fk])
                self.stt(da[:, qtile, :], po[:, 0:128], f, da[:, qtile, :], ALU.mult, ALU.add,
                         r=[ok, fk, ("d_a", qtile)], w=[("d_a", qtile)])
        self.attention(SEGS, kT, kk, (sub * 64, sub * 64 + 64), qT, qk,
                       lambda tc: dv_[:, tc, 0:129], vk, 128, 0.125, out_fn)
    self.run_bg()
    for t in range(12):
        self.act(self.junk[:, 0:128], da[:, t, :], AF.Square, r=[("d_a", t)], w=["junk", ("ss", t)],
                 accum_out=self.ss[:, t:t + 1])
    self.ts(self.rstd[:, 0:12], self.ss[:, 0:12], 1.0 / 128, EPS, ALU.mult, ALU.add, r=["ss"], w=["rstd"])
    self.tt(self.rstd[:, 0:12], self.rstd[:, 0:12], self.mhalf[:, 0:12], ALU.pow, r=["rstd", "mhalf"],
            w=["rstd"], eng="pool")
    for t in range(12):
        xs = self.xs[t % 2]
        xk = "xs%d" % (t % 2)
        self.stt(xs[:, 0:128], da[:, t, :], self.rstd[:, t:t + 1], dg[:, t, :], ALU.mult, ALU.mult,
                 r=[("d_a", t), ("rstd", t), (gk, t)], w=[xk])
        self.tr(self.psT[:, 0:128], xs[:, 0:128], self.ident_b[:], r=[xk, "identb"], w=["psT"])
        self.cp(self.mixT[:, 4 + h, t * 128:(t + 1) * 128], self.psT[:, 0:128], r=["psT"], w=[("mixT", t // 4)])


def _layer1(self):
    lim = int(os.environ.get("L1S", "99"))
    nhc = int(os.environ.get("L1HC", "4"))
    nhd = int(os.environ.get("L1HD", "4"))
    st_parent = self.P.stack
    self.norm_to_hT("o", False)
    with contextlib.ExitStack() as s2:
        self.P.stack = s2
        self.l1_setup()
        self.load_head_weights(0, "C", 0)
        with contextlib.ExitStack() as s2g:
            self.P.stack = s2g
            if lim >= 2:
                self.gate_prep_all()
        self.S.barrier()
        if lim >= 2:
            self.dma(self.outp["o_m"], self.mout[:], r=["mout"])
        with contextlib.ExitStack() as s2c:
            self.P.stack = s2c
            self.alloc_C()
            for h in range(4):
                if lim < 3 or h >= nhc:
                    break
                if h < 3:
                    self.load_head_weights((h + 1) % 2, "C", h + 1)
                else:
                    self.load_head_weights(0, "D", 0)
                self.phase_C_head(h, h % 2)
        self.S.barrier()
        with contextlib.ExitStack() as s2d:
            self.P.stack = s2d
            self.alloc_D()
            if lim >= 4:
                self.load_head_weights(1, "D", 1)
                self.bg = self.prepD(0, 0)
                self.run_bg()
            for h in range(4):
                if lim < 4 or h >= nhd:
                    break
                if h + 2 < 4:
                    self.load_head_weights(h % 2, "D", h + 2)
                if h < 3:
                    self.bg = self.prepD(h + 1, (h + 1) % 2)
                self.phase_D_head(h, h % 2)
    self.S.barrier()
    with contextlib.ExitStack() as so:
        self.P.stack = so
        self.gate_bc = [self.P.sb("gbc%d" % v, [128, D], F32) for v in range(2)]
        self.wout = self.P.sb("wout", [128, 8, D], BF16)
        if lim >= 5:
            self.load_wout("o")
            self.gate_bcast("o")
            self.out_proj(False)
    self.S.barrier()
    self.P.stack = st_parent


Builder.l1_setup = _l1_setup
Builder.load_head_weights = _load_head_weights
Builder.gate_chain = _gate_chain
Builder.gate_prep_all = _gate_prep_all
Builder.phase_C_head = _phase_C_head
Builder._mlstm_step = _mlstm_step
Builder.mlstm_pre = _mlstm_pre
Builder.nextY = _nextY
Builder.alloc_C = _alloc_C
Builder.alloc_D = _alloc_D
Builder.phase_D_head = _phase_D_head
Builder.prepD = _prepD
Builder.layer1 = _layer1
```

```python
import contextlib
import os
import numpy as np
import concourse.bass as bass
import concourse.mybir as mybir
from concourse.bass_utils import run_bass_kernel_spmd

F32 = mybir.dt.float32
BF16 = mybir.dt.bfloat16
AF = mybir.ActivationFunctionType
ALU = mybir.AluOpType
AX = mybir.AxisListType

COMPUTE = ["pe", "act", "dve", "pool"]
ENGS = ["pe", "act", "dve", "pool", "sp"]
EPS = 1e-6


class Sched:
    def __init__(self, nc, n_dma=8, n_swp=88):
        self.nc = nc
        self.ops = {e: [] for e in ENGS}
        self.count = {}
        self.clock = {e: {} for e in ENGS}
        self.vc = {}
        self.res = {}
        self.n_dma = n_dma
        self.dma_rr = 0
        self.tracks = list(COMPUTE) + ["d%d" % i for i in range(n_dma)]
        for t in self.tracks:
            self.count[t] = 0
        self.n_sw = 0
        self.n_swp = n_swp
        self.sw_last = [None] * n_swp
        self.sw_waiters = {}
        self.retired = {}
        self.phys = {}
        self.n_waits = 0
        self.pending = {e: {} for e in ENGS}

    def barrier(self):
        snap = {t: c for t, c in self.count.items() if c > 0 and t not in self.retired}
        for e in ENGS:
            self.pending[e] = dict(snap)

    def _states(self, key):
        tk, sub = key
        d = self.res.setdefault(tk, {})
        if sub is None:
            if None not in d:
                d[None] = {"w": None, "r": {}}
            return list(d.values())
        out = []
        if None in d:
            out.append(d[None])
        if sub not in d:
            d[sub] = {"w": None, "r": {}}
        out.append(d[sub])
        return out

    @staticmethod
    def _norm(k):
        if isinstance(k, str):
            return (k, None)
        return k

    def emit(self, eng, fn, reads=(), writes=(), dma=False):
        reads = [self._norm(k) for k in reads]
        writes = [self._norm(k) for k in writes]
        deps = set()
        for k in reads:
            for st in self._states(k):
                if st["w"] is not None:
                    deps.add(st["w"])
        for k in writes:
            for st in self._states(k):
                if st["w"] is not None:
                    deps.add(st["w"])
                for t, i in st["r"].items():
                    deps.add((t, i))
        if os.environ.get("RR", "1") == "1":
            for k in reads:
                if k[0].startswith("ps"):
                    for st in self._states(k):
                        for t, i in st["r"].items():
                            deps.add((t, i))
        if eng == "pe":
            deps = {d for d in deps if d[0] != "pe"}
        deps = {(self.retired[t] if t in self.retired else (t, i)) for (t, i) in deps}
        if dma and eng == "pool":
            p = self.n_sw % self.n_swp
            assert self.n_sw < self.n_swp, "software-DMA semaphore reuse (sem_clear) faults on hardware"
            prev = self.sw_last[p]
            if prev is not None:
                extra = {(prev, 0)} | set(self.sw_waiters.get(prev, ()))
                self._emit_clear(p, extra)
                self.retired[prev] = ("pool", self.count["pool"] - 1)
            track = "dw%d" % self.n_sw
            self.n_sw += 1
            self.count[track] = 0
            self.phys[track] = p
            self.sw_last[p] = track
            self.sw_waiters[track] = set()
        elif dma:
            track = "d%d" % self.dma_rr
            self.dma_rr = (self.dma_rr + 1) % self.n_dma
            if self.count[track] > 0:
                deps.add((track, self.count[track] - 1))
        else:
            track = eng
        clk = self.clock[eng]
        need = {}
        for t, i in deps:
            if need.get(t, 0) < i + 1:
                need[t] = i + 1
        if self.pending[eng]:
            for t, v in self.pending[eng].items():
                if need.get(t, 0) < v:
                    need[t] = v
            self.pending[eng] = {}
        waits = [(t, v) for t, v in need.items() if clk.get(t, 0) < v]
        for t, v in waits:
            for tt, vv in self.vc[(t, v - 1)].items():
                if clk.get(tt, 0) < vv:
                    clk[tt] = vv
        waits = [(t, v) for (t, v) in waits if not any(
            (t2, v2) != (t, v) and self.vc[(t2, v2 - 1)].get(t, 0) >= v for (t2, v2) in waits)]
        self.n_waits += len(waits)
        idx = self.count[track]
        self.count[track] = idx + 1
        myvc = dict(clk)
        myvc[track] = idx + 1
        self.vc[(track, idx)] = myvc
        self.ops[eng].append((waits, fn, track))
        for t, v in waits:
            if t in self.sw_waiters:
                self.sw_waiters[t].add((track, idx))
        for k in writes:
            self._states(k)
            st = self.res[k[0]][k[1]]
            st["w"] = (track, idx)
            st["r"] = {}
            if k[1] is None:
                for sub, s2 in self.res[k[0]].items():
                    if sub is not None:
                        s2["w"] = (track, idx)
                        s2["r"] = {}
        for k in reads:
            self._states(k)
            self.res[k[0]][k[1]]["r"][track] = idx
        return (track, idx)

    def _emit_clear(self, p, extra_deps):
        clk = self.clock["pool"]
        need = {}
        for t, i in extra_deps:
            if t in self.retired:
                t, i = self.retired[t]
            if need.get(t, 0) < i + 1:
                need[t] = i + 1
        if self.pending["pool"]:
            for t, v in self.pending["pool"].items():
                if need.get(t, 0) < v:
                    need[t] = v
            self.pending["pool"] = {}
        waits = [(t, v) for t, v in need.items() if clk.get(t, 0) < v]
        for t, v in waits:
            for tt, vv in self.vc[(t, v - 1)].items():
                if clk.get(tt, 0) < vv:
                    clk[tt] = vv
        idx = self.count["pool"]
        self.count["pool"] = idx + 1
        myvc = dict(clk)
        myvc["pool"] = idx + 1
        self.vc[("pool", idx)] = myvc
        for t, v in waits:
            if t in self.sw_waiters:
                self.sw_waiters[t].add(("pool", idx))
        self.ops["pool"].append((waits, ("clear", p), "pool"))
        self.n_waits += len(waits)

    def sem_of(self, sems, t):
        if t in self.phys:
            return sems["swp%d" % self.phys[t]]
        return sems[t]

    def sem_names(self):
        return list(COMPUTE) + ["d%d" % i for i in range(self.n_dma)] + ["swp%d" % i for i in range(min(self.n_swp, max(self.n_sw, 1)))]

    def finish(self, sems, block):
        engmap = {"pe": block.tensor, "act": block.scalar, "dve": block.vector,
                  "pool": block.gpsimd, "sp": block.sync}
        final = {t: c for t, c in self.count.items() if c > 0 and t not in self.retired}

        def body(ename):
            def run(e):
                for waits, fn, track in self.ops[ename]:
                    for t, v in waits:
                        e.wait_ge(self.sem_of(sems, t), v * (16 if t[0] == "d" else 1))
                    if isinstance(fn, tuple):
                        e.sem_clear(sems["swp%d" % fn[1]])
                        e.drain().then_inc(sems["pool"], 1)
                        continue
                    inst = fn(e)
                    inst.then_inc(self.sem_of(sems, track), 16 if track[0] == "d" else 1)
                if ename == "sp":
                    for t, c in final.items():
                        e.wait_ge(self.sem_of(sems, t), c * (16 if t[0] == "d" else 1))
            return run

        for ename in ENGS:
            engmap[ename](body(ename))


class TPool:
    def __init__(self, nc, stack):
        self.nc = nc
        self.stack = stack
        self.n = 0
        self.bytes = 0

    def sb(self, name, shape, dtype):
        self.n += 1
        sz = 1
        for s in shape[1:]:
            sz *= s
        self.bytes += sz * (2 if dtype == BF16 else 4)
        return self.stack.enter_context(self.nc.sbuf_tensor("%s_%d" % (name, self.n), list(shape), dtype))

    def ps(self, name, shape, dtype=F32):
        self.n += 1
        return self.stack.enter_context(self.nc.psum_tensor("%s_%d" % (name, self.n), list(shape), dtype))


D = 1024
NT = 1536
E_IN = 2464
O_IN = 4624
GP = dict(name="P", tok0=0, ntok=512, batches=[0], nseg=2, seglen=256, ctx=0, v=0)
GS = dict(name="S", tok0=512, ntok=1024, batches=[1, 2], nseg=1, seglen=1024, ctx=256, v=1)
TK = 1792
SEGS = [(0, 256, 0, 2, 0), (256, 256, 256, 2, 2), (512, 512, 512, 10, 4), (1024, 512, 512, 10, 8)]


def kcol(b):
    return 0 if b == 0 else 768 + (b - 1) * 512

IN_SPECS = [
    ("x_in", [NT, D]), ("c2", [128, 8, 2]),
    ("e_norm_g", [128, 8]), ("e_ada_w", [D, 3 * D]), ("e_ada_b", [128, 24]),
    ("e_w_in", [D, E_IN]), ("e_qn_g", [128, 3]), ("e_w_uq", [384, 768]), ("e_kvn_g", [128, 2]),
    ("e_w_ukv", [256, 1024]), ("e_gq_g", [128, 1]), ("e_gk_g", [128, 1]), ("e_w_out", [D, D]),
    ("o_norm_g", [128, 8]), ("o_ada_w", [D, 3 * D]), ("o_ada_b", [128, 24]),
    ("o_w_in", [D, O_IN]), ("o_gate_b", [4, 4]), ("o_mn_g", [1, 512]), ("o_lambda", [1, 256]),
    ("o_dn_g", [1, 128]), ("o_w_out", [D, D]), ("final_g", [1, D]),
    ("c_ckvT", [256, 256]), ("c_krT", [32, 256]), ("c_gkT", [128, 256]), ("c_gv", [256, 128]),
    ("c_ST", [8, 128, 128]), ("c_n", [8, 128]), ("c_m", [4, 2]), ("c_kdT", [4, 128, 256]),
    ("c_vd", [256, 512]),
    ("k_ident", [128, 128]), ("k_bones", [128, 128]), ("k_psw", [128, 128]),
    ("k_maskf", [64, 64]), ("k_maskb", [64, 64]),
    ("k_cos64", [128, 1024]), ("k_sin64", [128, 1024]), ("k_cos32", [128, 1024]), ("k_sin32", [128, 1024]),
]
OUT_SPECS = [
    ("y", [NT, D]), ("o_ckv", [512, 256]), ("o_kr", [512, 32]), ("o_gk", [512, 128]), ("o_gv", [512, 128]),
    ("o_C", [16, 128, 128]), ("o_n", [16, 128]), ("o_m", [4, 4]), ("o_dk", [512, 512]), ("o_dv", [512, 512]),
]


class Builder:
    def __init__(self, upto=99):
        self.upto = upto
        self.nc = bass.Bass("TRN2", target_bir_lowering=False)
        nc = self.nc
        self.inh = {n: nc.dram_tensor(n, list(s), F32, kind="ExternalInput") for n, s in IN_SPECS}
        self.inp = {n: h.ap() for n, h in self.inh.items()}
        self.outp = {n: nc.dram_tensor(n, list(s), F32, kind="ExternalOutput").ap() for n, s in OUT_SPECS}
        self.xi = 0
        self.si = 0
        self.fi = 0
        self.bg = None

    def mm(self, out, lhsT, rhs, start=True, stop=True, r=(), w=()):
        self.S.emit("pe", lambda e: e.matmul(out, lhsT=lhsT, rhs=rhs, start=start, stop=stop,
                                             skip_group_check=True), reads=r, writes=w)

    def tr(self, out, in_, ident, r=(), w=()):
        self.S.emit("pe", lambda e: e.transpose(out=out, in_=in_, identity=ident), reads=r, writes=w)

    def act(self, out, in_, func, r=(), w=(), **kw):
        self.S.emit("act", lambda e: e.activation(out=out, in_=in_, func=func, **kw), reads=r, writes=w)

    def ts(self, out, in0, s1, s2, op0, op1=None, r=(), w=(), eng="dve", accum_out=None):
        if op1 is None:
            self.S.emit(eng, lambda e: e.tensor_scalar(out=out, in0=in0, scalar1=s1, scalar2=None, op0=op0),
                        reads=r, writes=w)
        elif accum_out is not None:
            self.S.emit(eng, lambda e: e.tensor_scalar(out=out, in0=in0, scalar1=s1, scalar2=s2, op0=op0, op1=op1,
                                                       accum_out=accum_out), reads=r, writes=w)
        else:
            self.S.emit(eng, lambda e: e.tensor_scalar(out=out, in0=in0, scalar1=s1, scalar2=s2, op0=op0, op1=op1),
                        reads=r, writes=w)

    def tt(self, out, in0, in1, op, r=(), w=(), eng="dve"):
        self.S.emit(eng, lambda e: e.tensor_tensor(out=out, in0=in0, in1=in1, op=op), reads=r, writes=w)

    def stt(self, out, in0, scalar, in1, op0, op1, r=(), w=()):
        self.S.emit("dve", lambda e: e.scalar_tensor_tensor(out=out, in0=in0, scalar=scalar, in1=in1, op0=op0, op1=op1),
                    reads=r, writes=w)

    def cp(self, out, in_, r=(), w=(), eng="dve"):
        if eng == "act":
            self.S.emit("act", lambda e: e.copy(out=out, in_=in_), reads=r, writes=w)
        else:
            self.S.emit(eng, lambda e: e.tensor_copy(out=out, in_=in_), reads=r, writes=w)

    def recip(self, out, in_, r=(), w=()):
        self.S.emit("dve", lambda e: e.reciprocal(out=out, in_=in_), reads=r, writes=w)

    def memset(self, ap, val, w=(), eng="dve"):
        self.S.emit(eng, lambda e: e.memset(ap, val), writes=w)

    def dma(self, out, in_, r=(), w=(), eng="sp"):
        self.S.emit(eng, lambda e: e.dma_start(out=out, in_=in_), reads=r, writes=w, dma=True)

    def scan(self, out, d0, d1, init, op0, op1, r=(), w=()):
        self.S.emit("dve", lambda e: e.tensor_tensor_scan(out=out, data0=d0, data1=d1, initial=init, op0=op0, op1=op1),
                    reads=r, writes=w)

    def reduce(self, out, in_, op, r=(), w=()):
        self.S.emit("dve", lambda e: e.tensor_reduce(out=out, in_=in_, axis=AX.X, op=op), reads=r, writes=w)

    def dbg(self, n):
        return n <= int(os.environ.get("SUB", "99"))

    @staticmethod
    def hk(b):
        return [("hT", 4 * b + j) for j in range(4)]

    def tick(self):
        if self.bg is not None:
            try:
                next(self.bg)
            except StopIteration:
                self.bg = None

    def run_bg(self):
        while self.bg is not None:
            self.tick()

    def nextX(self):
        i = self.xi
        self.xi = (i + 1) % 2
        return self.psX[i], "psX%d" % i

    def nextS(self):
        i = self.si
        self.si = (i + 1) % 3
        if i < 2:
            return self.psS[i], "psS%d" % i
        return self.psX[2], "psX2"

    def nextF(self):
        i = self.fi
        self.fi = (i + 1) % len(self.fs)
        return self.fs[i], "fs%d" % i

    def setup(self, st):
        nc = self.nc
        self.P = TPool(nc, st)
        self.S = Sched(nc)
        P = self.P
        sb = P.sb
        self.psX = [P.ps("psX%d" % i, [128, 512], F32) for i in range(3)]
        self.psS = [P.ps("psS%d" % i, [128, 512], F32) for i in range(2)]
        self.psO = [P.ps("psO%d" % i, [128, 512], F32) for i in range(2)]
        self.psT = P.ps("psT", [128, 1024], BF16)
        self.hT = sb("hT", [128, 8, NT], BF16)
        self.mixT = sb("mixT", [128, 8, NT], BF16)
        self.ident_f = sb("identf", [128, 128], F32)
        self.ident_b = sb("identb", [128, 128], BF16)
        self.bones = sb("bones", [128, 128], BF16)
        self.psw = sb("psw", [128, 128], BF16)
        self.ones_b = sb("onesb", [128, 128], BF16)
        self.ones_f = sb("onesf", [128, 128], F32)
        self.maskf = sb("maskf", [64, 64], F32)
        self.maskb = sb("maskb", [64, 64], F32)
        self.cos64 = sb("cos64", [128, 1024], BF16)
        self.sin64 = sb("sin64", [128, 1024], BF16)
        self.cos32 = sb("cos32", [128, 1024], BF16)
        self.sin32 = sb("sin32", [128, 1024], BF16)
        self.mhalf = sb("mhalf", [128, 16], F32)
        i = self.inp
        self.dma(self.ident_f[:], i["k_ident"], w=["identf"])
        self.dma(self.ident_b[:], i["k_ident"], w=["identb"], eng="pool")
        self.dma(self.bones[:], i["k_bones"], w=["bones"], eng="pool")
        self.dma(self.psw[:], i["k_psw"], w=["psw"], eng="pool")
        self.dma(self.maskf[:], i["k_maskf"], w=["maskf"])
        self.dma(self.maskb[:], i["k_maskb"], w=["maskb"])
        self.dma(self.cos64[:], i["k_cos64"], w=["cos64"], eng="pool")
        self.dma(self.sin64[:], i["k_sin64"], w=["sin64"], eng="pool")
        self.dma(self.cos32[:], i["k_cos32"], w=["cos32"], eng="pool")
        self.dma(self.sin32[:], i["k_sin32"], w=["sin32"], eng="pool")
        self.memset(self.ones_b[:], 1.0, w=["onesb"])
        self.memset(self.ones_f[:], 1.0, w=["onesf"])
        self.memset(self.mhalf[:], -0.5, w=["mhalf"])
        self.c2 = sb("c2", [128, 16], F32)
        self.sc = sb("sc", [128, 16], BF16)
        self.dma(self.c2[:], i["c2"].rearrange("p a b -> p (a b)"), w=["c2"])
        self.ng = {}
        self.adab = {}
        for L in ("e", "o"):
            self.ng[L] = sb("ng" + L, [128, 8], F32)
            self.adab[L] = sb("adab" + L, [128, 24], F32)
            self.dma(self.ng[L][:], i[L + "_norm_g"], w=["ng" + L])
            self.dma(self.adab[L][:], i[L + "_ada_b"], w=["adab" + L])
        self.qn_g = sb("qng", [128, 3], F32)
        self.kvn_g = sb("kvng", [128, 2], F32)
        self.gq_g = sb("gqg", [128, 1], F32)
        self.gk_g = sb("gkg", [128, 1], F32)
        self.dma(self.qn_g[:], i["e_qn_g"], w=["qng"])
        self.dma(self.kvn_g[:], i["e_kvn_g"], w=["kvng"])
        self.dma(self.gq_g[:], i["e_gq_g"], w=["gqg"])
        self.dma(self.gk_g[:], i["e_gk_g"], w=["gkg"])
        self.fs = [sb("fs%d" % k, [128, 512], F32) for k in range(4)]
        self.ropeb = sb("ropeb", [128, 512], BF16)
        self.junk = sb("junk", [128, D], BF16)
        self.xs = [sb("xs%d" % k, [128, D], BF16) for k in range(2)]
        self.ss = sb("ss", [128, 16], F32)
        self.rden = sb("rden", [128, 8], F32)
        self.rdi = 0
        self.rstd = sb("rstd", [128, 16], F32)
        self.mod = {L: sb("mod" + L, [128, 48], F32) for L in ("e", "o")}
        self.Amod = {L: sb("Amod" + L, [128, 16], F32) for L in ("e", "o")}
        self.diag = [sb("diag%d" % k, [128, 128], F32) for k in range(2)]
        self.tmp16 = sb("tmp16", [128, 16], F32)
        self.act(self.tmp16[:], self.c2[:], AF.Tanh, r=["c2"], w=["tmp16"], scale=0.5)
        self.ts(self.tmp16[:], self.tmp16[:], 0.5, 0.5, ALU.mult, ALU.add, r=["tmp16"], w=["tmp16"])
        self.tt(self.sc[:], self.tmp16[:], self.c2[:], ALU.mult, r=["tmp16", "c2"], w=["sc"])

    def adaln(self, L):
        self.adaslot = [self.P.sb("adas%d" % k, [128, 8, 768], BF16) for k in range(2)]
        wv = self.inp[L + "_ada_w"].rearrange("(kc p) n -> p kc n", p=128)
        ps, pk = self.nextX()
        for blk in range(4):
            slot = self.adaslot[blk % 2]
            sk = "adas%d" % (blk % 2)
            self.dma(slot[:], wv[:, :, blk * 768:(blk + 1) * 768], w=[sk], eng="pool")
            for j in range(6):
                oc = blk * 6 + j
                for kc in range(8):
                    self.mm(ps[:, oc * 2:oc * 2 + 2], slot[:, kc, j * 128:(j + 1) * 128],
                            self.sc[:, kc * 2:kc * 2 + 2], start=(kc == 0), stop=(kc == 7),
                            r=[sk, "sc"], w=[pk])
        mod = self.mod[L]
        self.tt(mod[:].rearrange("p (a b) -> p a b", b=2), ps[:, 0:48].rearrange("p (a b) -> p a b", b=2),
                self.adab[L][:].unsqueeze(2).broadcast_to([128, 24, 2]), ALU.add,
                r=[pk, "adab" + L], w=["mod" + L])
        A = self.Amod[L]
        self.ts(self.tmp16[:], mod[:, 16:32], 1.0, None, ALU.add, r=["mod" + L], w=["tmp16"])
        self.tt(A[:].rearrange("p (a b) -> p a b", b=2), self.tmp16[:].rearrange("p (a b) -> p a b", b=2),
                self.ng[L][:].unsqueeze(2).broadcast_to([128, 8, 2]), ALU.mult,
                r=["tmp16", "ng" + L], w=["Amod" + L])

    def gate_bcast(self, L):
        mod = self.mod[L]
        for v in range(2):
            for half in range(2):
                ps, pk = self.nextX()
                for cc in range(4):
                    c = half * 4 + cc
                    dg = self.diag[cc % 2]
                    dk = "diag%d" % (cc % 2)
                    self.ts(dg[:], self.ident_f[:], mod[:, 32 + c * 2 + v:32 + c * 2 + v + 1], None, ALU.mult,
                            r=["identf", "mod" + L], w=[dk])
                    self.mm(ps[:, cc * 128:(cc + 1) * 128], self.ones_f[:], dg[:], r=["onesf", dk], w=[pk])
                self.cp(self.gate_bc[v][:, half * 512:(half + 1) * 512], ps[:], r=[pk], w=["gbc%d" % v], eng="act")

    def norm_to_hT(self, L, first):
        A = self.Amod[L]
        mod = self.mod[L]
        if first:
            xt_of = lambda t: (self.xstage[t], "xst%d" % t)
            for t in range(12):
                xa, xk = xt_of(t)
                self.dma(xa[:], self.inp["x_in"][t * 128:(t + 1) * 128, :], w=[xk])
        else:
            xt_of = lambda t: (self.x1[t], "x1_%d" % t)
        for t in range(12):
            if not first:
                break
            xa, xk = xt_of(t)
            self.act(self.junk[:], xa[:], AF.Square, r=[xk], w=["junk", ("ss", t)], accum_out=self.ss[:, t:t + 1])
        self.ts(self.rstd[:, 0:12], self.ss[:, 0:12], 1.0 / D, EPS, ALU.mult, ALU.add, r=["ss"], w=["rstd"])
        self.tt(self.rstd[:, 0:12], self.rstd[:, 0:12], self.mhalf[:, 0:12], ALU.pow,
                r=["rstd", "mhalf"], w=["rstd"], eng="pool")
        tbanks = [(self.psT[:], "psT"), (self.psS[0][:].bitcast(BF16), "psS0"), (self.psS[1][:].bitcast(BF16), "psS1")]

        def emit_xs(t):
            xa, xk = xt_of(t)
            xs = self.xs[t % 2]
            sk = "xs%d" % (t % 2)
            if t % 2 == 0:
                self.ts(xs[:], xa[:], self.rstd[:, t:t + 1], None, ALU.mult, r=[xk, ("rstd", t)], w=[sk])
            else:
                self.act(xs[:], xa[:], AF.Copy, r=[xk, ("rstd", t)], w=[sk], scale=self.rstd[:, t:t + 1])
        emit_xs(0)
        for t in range(12):
            b = t // 4
            v = 0 if b == 0 else 1
            xs = self.xs[t % 2]
            sk = "xs%d" % (t % 2)
            pt, ptk = tbanks[t % 3]
            for c in range(8):
                self.tr(pt[:, c * 128:(c + 1) * 128], xs[:, c * 128:(c + 1) * 128], self.ident_b[:],
                        r=[sk, "identb"], w=[ptk])
            if t + 1 < 12:
                emit_xs(t + 1)
            for c in range(8):
                o = self.hT[:, c, t * 128:(t + 1) * 128]
                a_ = A[:, c * 2 + v:c * 2 + v + 1]
                b_ = mod[:, c * 2 + v:c * 2 + v + 1]
                if t % 2 == 0:
                    self.ts(o, pt[:, c * 128:(c + 1) * 128], a_, b_, ALU.mult, ALU.add,
                            r=[ptk, "Amod" + L, "mod" + L], w=[("hT", t)])
                else:
                    self.act(o, pt[:, c * 128:(c + 1) * 128], AF.Identity,
                             r=[ptk, "Amod" + L, "mod" + L], w=[("hT", t)], scale=a_, bias=b_)

    def proj_fm(self, ps_ap, pk, w_fn, wk, rhs_fn, rk, nk):
        for kc in range(nk):
            self.mm(ps_ap, w_fn(kc), rhs_fn(kc), start=(kc == 0), stop=(kc == nk - 1), r=[wk] + list(rk), w=[pk])

    def rms_group(self, projs, n, n_feat, lhs_ones, lk, g_fn, gk, consume):
        last = len(projs) - 1
        psy, yk = self.psX[2], "psX2"
        for c, proj in enumerate(projs):
            ps, pk = self.nextX()
            proj(ps[:, :n], pk)
            yield
            self.cp(self.uf[:, c, :n], ps[:, :n], r=[pk], w=[("uf", c)], eng="act")
            self.act(self.sq[c % 2][:, :n], ps[:, :n], AF.Square, r=[pk], w=["sq%d" % (c % 2)])
            yield
            self.mm(psy[:, :n], lhs_ones, self.sq[c % 2][:, :n], start=(c == 0), stop=(c == last),
                    r=[lk, "sq%d" % (c % 2)], w=[yk])
        yield
        rb, rk = self.rb, "rb"
        self.act(rb[:, :n], psy[:, :n], AF.Ln, r=[yk], w=[rk], scale=1.0 / n_feat, bias=EPS)
        self.act(rb[:, :n], rb[:, :n], AF.Exp, r=[rk], w=[rk], scale=-0.5)
        yield
        for c in range(len(projs)):
            un, uk = self.nextF()
            self.stt(un[:, :n], self.uf[:, c, :n], g_fn(c), rb[:, :n], ALU.mult, ALU.mult,
                     r=[("uf", c), gk, rk], w=[uk])
            yield from consume(c, un, uk)

    def rope_fm(self, u, uk, rows, n, pos0, cos_t, ck, sin_t, sk, out, ok):
        r0, r1 = rows
        self.cp(self.ropeb[r0:r1, :n], u, r=uk, w=["ropeb"])
        f1, k1 = self.nextF()
        self.tt(f1[r0:r1, :n], u, cos_t[r0:r1, pos0:pos0 + n], ALU.mult, r=list(uk) + [ck], w=[k1])
        yield
        ps, pk = self.nextX()
        self.mm(ps[r0:r1, :n], self.psw[r0:r1, r0:r1], self.ropeb[r0:r1, :n], r=["psw", "ropeb"], w=[pk])
        yield
        f2, k2 = self.nextF()
        self.tt(f2[r0:r1, :n], ps[r0:r1, :n], sin_t[r0:r1, pos0:pos0 + n], ALU.mult, r=[pk, sk], w=[k2])
        self.tt(out, f1[r0:r1, :n], f2[r0:r1, :n], ALU.add, r=[k1, k2], w=ok, eng="pool")

    def attention(self, segs, kT, kk, krows, qT, qk, v_fn, vk, dv, scale, out_fn):
        r0, r1 = krows
        pend = None
        pi = 0
        for (q0, nq, t0, ntc, qtile0) in segs:
            pT = self.pT[pi % 2]
            pk = "pT%d" % (pi % 2)
            for tc in range(ntc):
                ps, sk = self.nextS()
                self.mm(ps[:, :nq], kT[r0:r1, t0 + tc * 128:t0 + (tc + 1) * 128], qT[r0:r1, q0:q0 + nq],
                        r=[kk, qk], w=[sk])
                self.act(pT[:, tc, :nq], ps[:, :nq], AF.Exp, r=[sk], w=[(pk, tc)], scale=scale)
                if tc % 2 == 1:
                    self.tick()
            if pend is not None:
                self._pv(*pend)
            pend = (pT, pk, nq, t0, ntc, qtile0, v_fn, vk, dv, out_fn, pi)
            pi += 1
        if pend is not None:
            self._pv(*pend)

    def _pv(self, pT, pk, nq, t0, ntc, qtile0, v_fn, vk, dv, out_fn, pi):
        w_ = dv + 1

        def reg(qt):
            if dv == 64:
                return self.psO[pi % 2][:, qt * w_:(qt + 1) * w_], "psO%d" % (pi % 2)
            return self.psO[qt // 2][:, (qt % 2) * w_:(qt % 2 + 1) * w_], "psO%d" % (qt // 2)
        for qt in range(nq // 128):
            po, ok = reg(qt)
            for tc in range(ntc):
                self.mm(po, pT[:, tc, qt * 128:(qt + 1) * 128],
                        v_fn(t0 // 128 + tc), start=(tc == 0), stop=(tc == ntc - 1), r=[(pk, tc), vk], w=[ok])
            self.tick()
        if getattr(out_fn, "block", None) is not None:
            out_fn.block(nq // 128, reg, dv)
        for qt in range(nq // 128):
            po, ok = reg(qt)
            out_fn(qtile0 + qt, po, ok)

    def tm_to_mixT(self, src, skey, tile_abs, c0, nchunk):
        b = tile_abs // 4
        for c in range(nchunk):
            self.tr(self.psT[:, c * 128:(c + 1) * 128], src[:, c * 128:(c + 1) * 128], self.ident_b[:],
                    r=[skey, "identb"], w=["psT"])
        self.cp(self.mixT[:, c0:c0 + nchunk, tile_abs * 128:(tile_abs + 1) * 128],
                self.psT[:, 0:nchunk * 128].rearrange("p (c t) -> p c t", t=128),
                r=["psT"], w=[("mixT", b)])

    def fm_to_out(self, src_ap, skey, rows, out_ap):
        self.tr(self.psT[:, 0:rows], src_ap, self.ident_b[0:rows, 0:rows], r=list(skey) + ["identb"], w=["psT"])
        st, sk = self.nextF()
        self.cp(st[:, 0:rows], self.psT[:, 0:rows], r=["psT"], w=[sk])
        self.dma(out_ap, st[:, 0:rows], r=[sk])

    def gates_tm(self, w_fn, wk, dst, dkey, tile_abs, tile_rel):
        b = tile_abs // 4
        ps, pk = self.nextX()
        for kc in range(8):
            self.mm(ps[:], self.hT[:, kc, tile_abs * 128:(tile_abs + 1) * 128], w_fn(kc),
                    start=(kc == 0), stop=(kc == 7), r=[*self.hk(b), wk], w=[pk])
        f, fk = self.nextF()
        self.act(f[:], ps[:], AF.Tanh, r=[pk], w=[fk], scale=0.5)
        self.stt(dst[:, tile_rel, :], f[:], 1.0, ps[:], ALU.add, ALU.mult, r=[fk, pk], w=[(dkey, tile_rel)])

    def alloc_l0(self):
        sb = self.P.sb
        self.wA = sb("wA", [128, 8, 1184], BF16)
        self.wuq = sb("wuq", [128, 3, 768], BF16)
        self.wukv = sb("wukv", [128, 2, 1024], BF16)
        self.cqn = sb("cqn", [128, 3, NT], BF16)
        self.ckvn = sb("ckvn", [128, 2, TK], BF16)
        self.krT = sb("krT", [128, TK], BF16)
        self.qTh = [sb("qTh%d" % k, [128, NT], BF16) for k in range(2)]
        self.kTh = [sb("kTh%d" % k, [128, TK], BF16) for k in range(2)]
        self.vh = [sb("vh%d" % k, [128, 14, 66], BF16) for k in range(2)]
        for k in range(2):
            self.memset(self.vh[k][:, :, 64:65], 1.0, w=["vh%d" % k])

    def alloc_l0_shared(self):
        sb = self.P.sb
        self.uf = sb("uf", [128, 3, 512], F32)
        self.rb = sb("rb", [128, 512], F32)
        self.sq = [sb("sq%d" % k, [128, 512], BF16) for k in range(2)]
        self.wB = sb("wB", [128, 8, 1280], BF16)
        self.ga = sb("ga", [128, 12, 512], BF16)
        self.pT = [sb("pT%d" % k, [128, 10, 512], BF16) for k in range(2)]

    def load_l0_weights(self):
        i = self.inp
        win = i["e_w_in"].rearrange("(kc p) n -> p kc n", p=128)
        self.dma(self.wA[:], win[:, :, 0:1184], w=["wA"], eng="pool")
        self.dma(self.wuq[:], i["e_w_uq"].rearrange("(kc p) n -> p kc n", p=128), w=["wuq"], eng="pool")
        self.dma(self.wukv[:], i["e_w_ukv"].rearrange("(kc p) n -> p kc n", p=128), w=["wukv"], eng="pool")
        self.dma(self.wB[:], win[:, :, 1184:2464], w=["wB"], eng="pool")

    def rms_lane(self, L, projs, n, n_feat, lhs_ones, lk, g_fn, gk, consume):
        last = len(projs) - 1
        ps, pk = L["pb"]
        psy, yk = L["yb"]
        rb, rbk = L["rb"]
        for c, proj in enumerate(projs):
            uf, ufk = L["uf"][c]
            sq, sqk = L["sq"]
            proj(ps[:, :n], pk)
            yield
            self.cp(uf, ps[:, :n], r=[pk], w=ufk, eng="act")
            self.act(sq, ps[:, :n], AF.Square, r=[pk], w=[sqk])
            yield
            self.mm(psy[:, :n], lhs_ones, sq, start=(c == 0), stop=(c == last), r=[lk, sqk], w=[yk])
        yield
        self.act(rb, psy[:, :n], AF.Ln, r=[yk], w=rbk, scale=1.0 / n_feat, bias=EPS)
        self.act(rb, rb, AF.Exp, r=rbk, w=rbk, scale=-0.5)
        yield
        for c in range(len(projs)):
            uf, ufk = L["uf"][c]
            un, unk = L["un"]
            self.stt(un, uf, g_fn(c), rb, ALU.mult, ALU.mult, r=ufk + [gk] + rbk, w=unk)
            yield from consume(c, un, unk)

    def prepA_group(self):
        hT, wA = self.hT, self.wA

        def f32view(k, i):
            ap = self.pT[k][:, 2 * i:2 * i + 2, :].rearrange("p a b -> p (a b)").bitcast(F32)
            return ap, [("pT%d" % k, 2 * i), ("pT%d" % k, 2 * i + 1)]
        lanes = []
        banks = [(self.psX[0], "psX0"), (self.psX[1], "psX1"), (self.psX[2], "psX2"), (self.psS[0], "psS0")]
        for k in range(2):
            L = {"pb": banks[2 * k], "yb": banks[2 * k + 1], "sq": (self.sq[k][:], "sq%d" % k)}
            L["uf"] = [f32view(k, i) for i in range(3)]
            L["rb"] = f32view(k, 3)
            L["un"] = f32view(k, 4)
            lanes.append(L)

        def lane_cq():
            for b in range(3):
                q0 = b * 512
                rhs = lambda kc, b=b: hT[:, kc, b * 512:(b + 1) * 512]
                mk = lambda col, b=b, rhs=rhs: (lambda ps, pk: self.proj_fm(
                    ps, pk, lambda kc: wA[:, kc, col:col + 128], "wA", rhs, [*self.hk(b)], 8))

                def cons(c, un, unk, q0=q0):
                    self.cp(self.cqn[:, c, q0:q0 + 512], un, r=unk, w=["cqn"], eng="pool")
                    yield
                yield from self.rms_lane(lanes[0], [mk(c * 128) for c in range(3)], 512, 384, self.ones_b[:], "onesb",
                                         lambda c: self.qn_g[:, c:c + 1], "qng", cons)

        def lane_ckv():
            for b in range(3):
                isS = b > 0
                d0 = kcol(b)
                rhs = lambda kc, b=b: hT[:, kc, b * 512:(b + 1) * 512]
                mk = lambda col, b=b, rhs=rhs: (lambda ps, pk: self.proj_fm(
                    ps, pk, lambda kc: wA[:, kc, col:col + 128], "wA", rhs, [*self.hk(b)], 8))

                def cons(c, un, unk, d0=d0, isS=isS):
                    self.cp(self.ckvn[:, c, d0:d0 + 512], un, r=unk, w=["ckvn"], eng="pool")
                    yield
                    if not isS:
                        for j in range(4):
                            self.fm_to_out(self.ckvn[:, c, d0 + j * 128:d0 + (j + 1) * 128], ["ckvn"], 128,
                                           self.outp["o_ckv"][j * 128:(j + 1) * 128, c * 128:(c + 1) * 128])
                            yield
                yield from self.rms_lane(lanes[1], [mk(384 + c * 128) for c in range(2)], 512, 256, self.ones_b[:],
                                         "onesb", lambda c: self.kvn_g[:, c:c + 1], "kvng", cons)

        def lane_kr_gates():
            ps, pk = self.psS[1], "psS1"
            ps2, pk2 = self.psO[0], "psO0"
            for b in range(3):
                isS = b > 0
                d0 = kcol(b)
                pos0 = (b - 1) * 512
                rhs = lambda kc, b=b: hT[:, kc, b * 512:(b + 1) * 512]
                self.proj_fm(ps[0:32, :], pk, lambda kc: wA[:, kc, 640:672], "wA", rhs, [*self.hk(b)], 8)
                yield
                if isS:
                    f1, k1 = self.nextF()
                    self.cp(self.ropeb[0:32, :], ps[0:32, :], r=[pk], w=["ropeb"])
                    self.tt(f1[0:32, :], ps[0:32, :], self.cos32[0:32, pos0:pos0 + 512], ALU.mult, r=[pk, "cos32"], w=[k1])
                    yield
                    self.mm(ps2[0:32, :], self.psw[0:32, 0:32], self.ropeb[0:32, :], r=["psw", "ropeb"], w=[pk2])
                    yield
                    f2, k2 = self.nextF()
                    self.tt(f2[0:32, :], ps2[0:32, :], self.sin32[0:32, pos0:pos0 + 512], ALU.mult, r=[pk2, "sin32"], w=[k2])
                    self.tt(self.krT[0:32, d0:d0 + 512], f1[0:32, :], f2[0:32, :], ALU.add, r=[k1, k2], w=["krT"], eng="pool")
                else:
                    self.cp(self.krT[0:32, d0:d0 + 512], ps[0:32, :], r=[pk], w=["krT"])
                    yield
                    for j in range(4):
                        self.fm_to_out(self.krT[0:32, d0 + j * 128:d0 + (j + 1) * 128], ["krT"], 32,
                                       self.outp["o_kr"][j * 128:(j + 1) * 128, :])
                        yield
                for j in range(4):
                    t = b * 4 + j
                    for kc in range(8):
                        self.mm(ps[:], hT[:, kc, t * 128:(t + 1) * 128], wA[:, kc, 672:1184],
                                start=(kc == 0), stop=(kc == 7), r=[*self.hk(b), "wA"], w=[pk])
                    yield
                    f, fk = self.nextF()
                    self.act(f[:], ps[:], AF.Tanh, r=[pk], w=[fk], scale=0.5)
                    yield
                    self.stt(self.ga[:, t, :], f[:], 1.0, ps[:], ALU.add, ALU.mult, r=[fk, pk], w=[("ga", t)])

        gens = [lane_cq(), lane_ckv(), lane_kr_gates()]
        while gens:
            for g_ in list(gens):
                try:
                    next(g_)
                except StopIteration:
                    gens.remove(g_)
            yield
        self.dma(self.ckvn[:, :, 512:768], self.inp["c_ckvT"].rearrange("(c p) t -> p c t", p=128), w=["ckvn"],
                 eng="pool")
        self.dma(self.krT[0:32, 512:768], self.inp["c_krT"], w=["krT"], eng="pool")

    def prepA_head(self, h):
        T = TK
        qT, qk = self.qTh[h % 2], "qTh%d" % (h % 2)
        kT, kk = self.kTh[h % 2], "kTh%d" % (h % 2)
        vh, vk = self.vh[h % 2], "vh%d" % (h % 2)
        for b in range(3):
            q0 = b * 512
            ps, pk = self.nextX()
            self.proj_fm(ps[0:96, :], pk, lambda kc: self.wuq[:, kc, h * 96:(h + 1) * 96], "wuq",
                         lambda kc: self.cqn[:, kc, q0:q0 + 512], ["cqn"], 3)
            yield
            if b > 0:
                self.cp(qT[0:64, q0:q0 + 512], ps[0:64, :], r=[pk], w=[qk])
                yield from self.rope_fm(ps[64:96, :], [pk], (64, 96), 512, (b - 1) * 512, self.cos32, "cos32",
                                        self.sin32, "sin32", qT[64:96, q0:q0 + 512], [qk])
            else:
                self.cp(qT[0:96, q0:q0 + 512], ps[0:96, :], r=[pk], w=[qk])
            yield
        t0 = 0
        while t0 < T:
            n = min(512, T - t0)
            ps, pk = self.nextX()
            self.proj_fm(ps[0:64, :n], pk, lambda kc: self.wukv[:, kc, h * 128:h * 128 + 64], "wukv",
                         lambda kc, t0=t0, n=n: self.ckvn[:, kc, t0:t0 + n], ["ckvn"], 2)
            yield
            self.cp(kT[0:64, t0:t0 + n], ps[0:64, :n], r=[pk], w=[kk])
            t0 += n
        self.cp(kT[64:96, 0:T], self.krT[0:32, 0:T], r=["krT"], w=[kk], eng="pool")
        yield
        ntc = T // 128
        tc0 = 0
        while tc0 < ntc:
            g = min(7, ntc - tc0)
            ps, pk = self.nextX()
            for j in range(g):
                tc = tc0 + j
                for kc in range(2):
                    self.mm(ps[:, j * 64:(j + 1) * 64], self.ckvn[:, kc, tc * 128:(tc + 1) * 128],
                            self.wukv[:, kc, h * 128 + 64:(h + 1) * 128], start=(kc == 0), stop=(kc == 1),
                            r=["ckvn", "wukv"], w=[pk])
            yield
            self.cp(vh[:, tc0:tc0 + g, 0:64], ps[:, 0:g * 64].rearrange("p (a b) -> p a b", b=64), r=[pk], w=[vk])
            tc0 += g

    def phase_A(self):
        self.bg = self.prepA_group()
        self.run_bg()
        scale = 96.0 ** -0.5
        self.bg = self.prepA_head(0)
        self.run_bg()
        for h in range(8):
            if h < 7:
                self.bg = self.prepA_head(h + 1)
            vh = self.vh[h % 2]
            self.attend_std(self.kTh[h % 2], "kTh%d" % (h % 2), (0, 96), self.qTh[h % 2], "qTh%d" % (h % 2),
                            lambda tc, vh=vh: vh[:, tc, 0:65], "vh%d" % (h % 2), 64, scale, self.ga, "ga", h * 64)
            self.run_bg()
        for j in range(12):
            self.tm_to_mixT(self.ga[:, j, :], ("ga", j), j, 0, 4)

    def attend_std(self, kT, kk, krows, qT, qk, v_fn, vk, dv, scale, dst, dkey, col0):
        segs = SEGS

        st_ = {}

        def block(nqt, reg, dv_):
            h_ = self.rdi
            self.rdi = (h_ + 1) % 2
            po0, ok0 = reg(0)
            bank = self.psO[int(ok0[-1])]
            den = bank[:, 0:nqt * (dv_ + 1)].rearrange("p (q w) -> p q w", w=dv_ + 1)[:, :, dv_]
            f = self.rden[:, h_ * 4:h_ * 4 + nqt]
            fk = ("rden", h_)
            self.ts(f, den, 2.0, None, ALU.mult, r=[ok0], w=[fk])
            self.recip(f, f, r=[fk], w=[fk])
            st_["h"] = h_
            st_["q"] = 0

        def out_fn(qtile, po, ok):
            h_ = st_["h"]
            q_ = st_["q"]
            st_["q"] = q_ + 1
            f = self.rden[:, h_ * 4 + q_:h_ * 4 + q_ + 1]
            fk = ("rden", h_)
            self.stt(dst[:, qtile, col0:col0 + dv], po[:, 0:dv], f, dst[:, qtile, col0:col0 + dv],
                     ALU.mult, ALU.mult, r=[ok, fk, (dkey, qtile)], w=[(dkey, qtile)])
        out_fn.block = block
        self.attention(segs, kT, kk, krows, qT, qk, v_fn, vk, dv, scale, out_fn)

    def alloc_l0b(self):
        sb = self.P.sb
        self.qbT = sb("qbT", [128, 4, NT], BF16)
        self.kbT = sb("kbT", [128, TK], BF16)
        self.kbX = sb("kbX", [128, TK], BF16)
        self.vb = sb("vb", [128, 14, 2, 66], BF16)
        self.memset(self.vb[:, :, :, 64:65], 1.0, w=["vb"])
        banks = [(self.psX[0], "psX0"), (self.psX[1], "psX1"), (self.psX[2], "psX2"),
                 (self.psS[0], "psS0"), (self.psS[1], "psS1"), (self.psO[0], "psO0")]
        self.lanesB = []
        for i in range(3):
            L = {"pb": banks[2 * i], "yb": banks[2 * i + 1]}
            for nm, dt_ in (("uf", F32), ("rb", F32), ("un", F32), ("f1", F32), ("f2", F32), ("sq", BF16), ("ropeb", BF16)):
                t_ = sb("lb%d%s" % (i, nm), [128, 512], dt_)
                L[nm] = (t_[:], "lb%d%s" % (i, nm))
            self.lanesB.append(L)

    def prepB_group(self):
        hT, wB = self.hT, self.wB
        lanes = self.lanesB

        def chain(L, b, c):
            isS = b > 0
            tb0 = b * 512
            d0 = kcol(b)
            q0 = b * 512
            pos0 = (b - 1) * 512
            ps, pk = L["pb"]
            psy, yk = L["yb"]
            uf, ufk = L["uf"]
            sq, sqk = L["sq"]
            rb, rbk = L["rb"]
            un, unk = L["un"]
            col = c * 128
            self.proj_fm(ps[:], pk, lambda kc: wB[:, kc, col:col + 128], "wB",
                         lambda kc: hT[:, kc, tb0:tb0 + 512], [*self.hk(b)], 8)
            yield
            self.cp(uf, ps[:], r=[pk], w=[ufk], eng="act")
            self.act(sq, ps[:], AF.Square, r=[pk], w=[sqk])
            yield
            self.mm(psy[:], self.bones[:], sq, r=["bones", sqk], w=[yk])
            yield
            self.act(rb, psy[:], AF.Ln, r=[yk], w=[rbk], scale=1.0 / 64, bias=EPS)
            self.act(rb, rb, AF.Exp, r=[rbk], w=[rbk], scale=-0.5)
            yield
            g = self.gq_g if c < 4 else self.gk_g
            gk = "gqg" if c < 4 else "gkg"
            self.stt(un, uf, g[:, 0:1], rb, ALU.mult, ALU.mult, r=[ufk, gk, rbk], w=[unk])
            if c < 4:
                dst, dk = self.qbT[:, c, q0:q0 + 512], ["qbT"]
            else:
                dst, dk = self.kbT[:, d0:d0 + 512], ["kbT"]
            if isS:
                rpb, rpk = L["ropeb"]
                f1, k1 = L["f1"]
                f2, k2 = L["f2"]
                self.cp(rpb, un, r=[unk], w=[rpk])
                self.tt(f1, un, self.cos64[:, pos0:pos0 + 512], ALU.mult, r=[unk, "cos64"], w=[k1])
                yield
                self.mm(ps[:], self.psw[:], rpb, r=["psw", rpk], w=[pk])
                yield
                self.tt(f2, ps[:], self.sin64[:, pos0:pos0 + 512], ALU.mult, r=[pk, "sin64"], w=[k2])
                self.tt(dst, f1, f2, ALU.add, r=[k1, k2], w=dk, eng="pool")
            else:
                self.cp(dst, un, r=[unk], w=dk, eng="pool")
                if c == 4:
                    yield
                    for j in range(4):
                        self.fm_to_out(self.kbT[:, d0 + j * 128:d0 + (j + 1) * 128], ["kbT"], 128,
                                       self.outp["o_gk"][j * 128:(j + 1) * 128, :])
                        yield
            yield

        def lane_gen(L, items):
            for (b, c) in items:
                yield from chain(L, b, c)

        def vg_gen():
            ps, pk = self.psO[1], "psO1"
            for b in range(3):
                isS = b > 0
                d0 = kcol(b)
                for j in range(4):
                    t = b * 4 + j
                    tc = (d0 // 128) + j
                    for kc in range(8):
                        self.mm(ps[:, 0:128], hT[:, kc, t * 128:(t + 1) * 128], wB[:, kc, 640:768],
                                start=(kc == 0), stop=(kc == 7), r=[*self.hk(b), "wB"], w=[pk])
                    yield
                    self.cp(self.vb[:, tc, :, 0:64], ps[:, 0:128].rearrange("p (g d) -> p g d", d=64), r=[pk], w=["vb"])
                    if not isS:
                        f, fk = self.nextF()
                        self.cp(f[:, 0:128], ps[:, 0:128], r=[pk], w=[fk])
                        self.dma(self.outp["o_gv"][t * 128:(t + 1) * 128, :], f[:, 0:128], r=[fk])
                    for kc in range(8):
                        self.mm(ps[:], hT[:, kc, t * 128:(t + 1) * 128], wB[:, kc, 768:1280],
                                start=(kc == 0), stop=(kc == 7), r=[*self.hk(b), "wB"], w=[pk])
                    yield
                    f, fk = self.nextF()
                    self.act(f[:], ps[:], AF.Tanh, r=[pk], w=[fk], scale=0.5)
                    yield
                    self.stt(self.ga[:, t, :], f[:], 1.0, ps[:], ALU.add, ALU.mult, r=[fk, pk], w=[("ga", t)])

        items = [(b, c) for b in range(3) for c in range(5)]
        gens = [lane_gen(lanes[i], items[i::3]) for i in range(3)] + [vg_gen()]
        while gens:
            for g_ in list(gens):
                try:
                    next(g_)
                except StopIteration:
                    gens.remove(g_)
            yield
        self.dma(self.kbT[:, 512:768], self.inp["c_gkT"], w=["kbT"], eng="pool")
        for g in range(2):
            self.dma(self.vb[:, 4:6, g, 0:64],
                     self.inp["c_gv"].rearrange("(tc p) n -> p tc n", p=128)[:, :, g * 64:(g + 1) * 64],
                     w=["vb"], eng="pool")
        self.cp(self.kbX[0:64, 0:TK], self.kbT[64:128, 0:TK], r=["kbT"], w=["kbX"])
        self.cp(self.kbX[64:128, 0:TK], self.kbT[0:64, 0:TK], r=["kbT"], w=["kbX"])

    def phase_B(self):
        self.bg = self.prepB_group()
        self.run_bg()
        for hq in range(8):
            g = hq // 4
            c = hq // 2
            base = (hq % 2) * 64
            if base == g * 64:
                kT, kk = self.kbT, "kbT"
            else:
                kT, kk = self.kbX, "kbX"
            self.attend_std(kT, kk, (base, base + 64), self.qbT[:, c, :], "qbT",
                            lambda tc, g=g: self.vb[:, tc, g, 0:65], "vb", 64, 0.125, self.ga, "ga", hq * 64)
        for j in range(12):
            self.tm_to_mixT(self.ga[:, j, :], ("ga", j), j, 4, 4)

    def load_wout(self, L):
        for kc in range(8):
            self.dma(self.wout[:, kc, :], self.inp[L + "_w_out"][kc * 128:(kc + 1) * 128, :], w=[("wout", kc)], eng="pool")

    def out_proj(self, first):
        for t in range(12):
            b = t // 4
            v = 0 if b == 0 else 1
            xa = self.x1[t]
            xk = "x1_%d" % t
            if first:
                self.dma(xa[:], self.inp["x_in"][t * 128:(t + 1) * 128, :], w=[xk])
            for half in range(2):
                ps, pk = self.nextX()
                for c in range(8):
                    self.mm(ps[:], self.mixT[:, c, t * 128:(t + 1) * 128], self.wout[:, c, half * 512:(half + 1) * 512],
                            start=(c == 0), stop=(c == 7), r=[("mixT", b), ("wout", c)], w=[pk])
                f, fk = self.nextF()
                self.tt(f[:], ps[:], self.gate_bc[v][:, half * 512:(half + 1) * 512], ALU.mult,
                        r=[pk, "gbc%d" % v], w=[fk])
                self.tt(self.x1[t][:, half * 512:(half + 1) * 512], xa[:, half * 512:(half + 1) * 512], f[:], ALU.add,
                        r=[xk, fk], w=["x1_%d" % t], eng="pool")
            self.act(self.junk[:], self.x1[t][:], AF.Square, r=["x1_%d" % t], w=["junk", ("ss", t)],
                     accum_out=self.ss[:, t:t + 1])

    def dump_x(self):
        for t in range(12):
            self.dma(self.outp["y"][t * 128:(t + 1) * 128, :], self.x1[t][:], r=["x1_%d" % t])

    def final_norm(self):
        fg = self.P.sb("fg", [128, D], F32)
        self.dma(fg[:], self.inh["final_g"].ap().partition_broadcast(128), w=["fg"])
        self.ts(self.rstd[:, 0:12], self.ss[:, 0:12], 1.0 / D, EPS, ALU.mult, ALU.add, r=["ss"], w=["rstd"])
        self.tt(self.rstd[:, 0:12], self.rstd[:, 0:12], self.mhalf[:, 0:12], ALU.pow,
                r=["rstd", "mhalf"], w=["rstd"], eng="pool")
        for t in range(12):
            xa = self.x1[t]
            xk = "x1_%d" % t
            self.stt(xa[:], xa[:], self.rstd[:, t:t + 1], fg[:], ALU.mult, ALU.mult, r=[xk, ("rstd", t), "fg"], w=[xk])
            self.dma(self.outp["y"][t * 128:(t + 1) * 128, :], xa[:], r=[xk])


def build_program(upto=99):
    B = Builder(upto)
    nc = B.nc
    stage = [0]

    def go():
        stage[0] += 1
        return not (upto < 0 and stage[0] > -upto)

    with contextlib.ExitStack() as st:
        B.setup(st)
        with contextlib.ExitStack() as s0:
            B.P.stack = s0
            B.xstage = [B.P.sb("xst%d" % k, [128, D], F32) for k in range(12)]
            if go():
                B.adaln("e")
            if go():
                B.norm_to_hT("e", True)
                B.adaln("o")
        B.S.barrier()
        with contextlib.ExitStack() as s1:
            B.P.stack = s1
            B.alloc_l0_shared()
            with contextlib.ExitStack() as s1a:
                B.P.stack = s1a
                B.alloc_l0()
                if go():
                    B.load_l0_weights()
                if go():
                    B.phase_A()
                go()
            B.S.barrier()
            go()
            with contextlib.ExitStack() as s1b:
                B.P.stack = s1b
                B.alloc_l0b()
                if go():
                    B.phase_B()
                go()
        B.S.barrier()
        B.P.stack = st
        B.x1 = [B.P.sb("x1_%d" % i, [128, D], F32) for i in range(12)]
        with contextlib.ExitStack() as so:
            B.P.stack = so
            B.gate_bc = [B.P.sb("gbc%d" % v, [128, D], F32) for v in range(2)]
            B.wout = B.P.sb("wout", [128, 8, D], BF16)
            if go():
                B.load_wout("e")
                B.gate_bcast("e")
                B.out_proj(True)
        B.S.barrier()
        B.P.stack = st
        if upto == 0:
            B.dump_x()
        elif upto > 0:
            B.layer1()
            B.final_norm()
        print("ops:", {e: len(v) for e, v in B.S.ops.items()}, "waits:", B.S.n_waits)
        sems = {t: st.enter_context(nc.semaphore("s_" + t)) for t in B.S.sem_names()}
        print("semaphores:", len(sems), "sw dmas:", B.S.n_sw)
        with nc.Block() as block:
            B.S.finish(sems, block)
    return nc


def _consts():
    k = {}
    k["k_ident"] = np.eye(128, dtype=np.float32)
    bo = np.zeros((128, 128), np.float32)
    bo[:64, :64] = 1
    bo[64:, 64:] = 1
    k["k_bones"] = bo
    ps = np.zeros((128, 128), np.float32)
    for i in range(128):
        ps[i, i ^ 1] = 1
    k["k_psw"] = ps
    s_ = np.arange(64)[:, None]
    t_ = np.arange(64)[None, :]
    k["k_maskf"] = (s_ <= t_).astype(np.float32)
    k["k_maskb"] = (s_ >= t_).astype(np.float32)

    def rope_tab(rot):
        n = 1024
        rows = (np.arange(n) // 64).astype(np.float32)
        cols = (np.arange(n) % 64).astype(np.float32)
        half = rot // 2
        inv = (1.0 / (10000.0 ** (np.arange(0, half, 2, dtype=np.float32) / half))).astype(np.float32)
        ang = np.concatenate([rows[:, None] * inv, cols[:, None] * inv], axis=-1)
        cos = np.cos(ang).astype(np.float32)
        sin = np.sin(ang).astype(np.float32)
        cosT = np.repeat(cos, 2, axis=1).T
        sinT = np.repeat(sin, 2, axis=1).T.copy()
        sinT[0::2] *= -1.0
        return np.ascontiguousarray(cosT), np.ascontiguousarray(sinT)

    c64, s64 = rope_tab(64)
    k["k_cos64"] = np.ascontiguousarray(np.tile(c64, (2, 1)))
    k["k_sin64"] = np.ascontiguousarray(np.tile(s64, (2, 1)))
    c32, s32 = rope_tab(32)
    z = np.zeros((128, 1024), np.float32)
    z[0:32] = c32
    z[64:96] = c32
    k["k_cos32"] = z
    z = np.zeros((128, 1024), np.float32)
    z[0:32] = s32
    z[64:96] = s32
    k["k_sin32"] = z
    return k


def _fm(vec, nch):
    return np.ascontiguousarray(np.asarray(vec, np.float32).reshape(nch, 128).T)


_NC_CACHE = {}


def kernel(**inp):
    f = lambda a: np.asarray(a, np.float32)
    consts = _consts()
    shared = dict(consts)
    shared["e_norm_g"] = _fm(inp["e_norm_g"][0], 8)
    shared["e_ada_w"] = f(inp["e_ada_w"][0])
    shared["e_ada_b"] = _fm(inp["e_ada_b"][0], 24)
    shared["e_w_in"] = f(inp["e_w_in"][0])
    shared["e_qn_g"] = _fm(inp["e_mla_qnorm_g"][0], 3)
    shared["e_w_uq"] = f(inp["e_mla_w_uq"][0])
    shared["e_kvn_g"] = _fm(inp["e_mla_kvnorm_g"][0], 2)
    shared["e_w_ukv"] = f(inp["e_mla_w_ukv"][0])
    shared["e_gq_g"] = np.ascontiguousarray(np.tile(f(inp["e_gqa_qnorm_g"][0]), 2)[:, None])
    shared["e_gk_g"] = np.ascontiguousarray(np.tile(f(inp["e_gqa_knorm_g"][0]), 2)[:, None])
    shared["e_w_out"] = f(inp["e_w_out"][0])
    shared["o_norm_g"] = _fm(inp["o_norm_g"][0], 8)
    shared["o_ada_w"] = f(inp["o_ada_w"][0])
    shared["o_ada_b"] = _fm(inp["o_ada_b"][0], 24)
    shared["o_w_in"] = f(inp["o_w_in"][0])
    shared["o_gate_b"] = np.ascontiguousarray(f(inp["o_mlstm_gate_b"][0]).reshape(4, 4).T)
    shared["o_mn_g"] = f(inp["o_mlstm_norm_g"][0])[None, :]
    shared["o_lambda"] = f(inp["o_diff_lambda"][0]).reshape(1, 256)
    shared["o_dn_g"] = f(inp["o_diff_norm_g"][0])[None, :]
    shared["o_w_out"] = f(inp["o_w_out"][0])
    shared["final_g"] = f(inp["final_norm_g"])[None, :]
    xp = f(inp["x_prompt"])
    xs = f(inp["x_sample"])
    in_maps = []
    for core in range(8):
        sq = core // 4
        m = dict(shared)
        m["x_in"] = np.ascontiguousarray(np.concatenate([xp[2 * core], xp[2 * core + 1], xs[sq]], axis=0))
        cv = np.stack([f(inp["c_ctx"]), f(inp["c"][sq])], axis=-1)
        m["c2"] = np.ascontiguousarray(cv.reshape(8, 128, 2).transpose(1, 0, 2))
        m["c_ckvT"] = np.ascontiguousarray(f(inp["cache_mla_ckv"][sq, 0]).T)
        m["c_krT"] = np.ascontiguousarray(f(inp["cache_mla_krope"][sq, 0]).T)
        m["c_gkT"] = np.ascontiguousarray(f(inp["cache_gqa_k"][sq, 0]).reshape(256, 128).T)
        m["c_gv"] = np.ascontiguousarray(f(inp["cache_gqa_v"][sq, 0]).reshape(256, 128))
        m["c_ST"] = np.ascontiguousarray(f(inp["state_mlstm_C"][sq, 0]).reshape(8, 128, 128).transpose(0, 2, 1))
        m["c_n"] = np.ascontiguousarray(f(inp["state_mlstm_n"][sq, 0]).reshape(8, 128))
        m["c_m"] = np.ascontiguousarray(f(inp["state_mlstm_m"][sq, 0]).reshape(2, 4).T)
        m["c_kdT"] = np.ascontiguousarray(f(inp["cache_diff_k"][sq, 0]).transpose(1, 2, 0))
        m["c_vd"] = np.ascontiguousarray(f(inp["cache_diff_v"][sq, 0]).reshape(256, 512))
        in_maps.append(m)
    if "nc" not in _NC_CACHE:
        _NC_CACHE["nc"] = build_program()
    nc = _NC_CACHE["nc"]
    if _NC_CACHE.get("debug_cores"):
        ncore = _NC_CACHE["debug_cores"]
        res = run_bass_kernel_spmd(nc, in_maps[:ncore], core_ids=list(range(ncore)))
        return res.results
    res = run_bass_kernel_spmd(nc, in_maps, core_ids=list(range(8)))
    R = res.results
    y_prompt = np.stack([R[c]["y"][s * 256:(s + 1) * 256] for c in range(8) for s in range(2)], axis=0)
    y_sample = np.stack([R[0]["y"][512:], R[4]["y"][512:]], axis=0)

    def gather(name, shape):
        return np.stack([R[c][name][s * 256:(s + 1) * 256].reshape(shape) for c in range(8) for s in range(2)],
                        axis=0)[:, None]
    st_ckv = gather("o_ckv", (256, 256))
    st_kr = gather("o_kr", (256, 32))
    st_gk = gather("o_gk", (256, 2, 64))
    st_gv = gather("o_gv", (256, 2, 64))
    st_dk = gather("o_dk", (256, 4, 128))
    st_dv = gather("o_dv", (256, 4, 128))
    st_C = np.stack([R[c]["o_C"][s * 8:(s + 1) * 8].reshape(2, 4, 128, 128) for c in range(8) for s in range(2)],
                    axis=0)[:, None]
    st_n = np.stack([R[c]["o_n"][s * 8:(s + 1) * 8].reshape(2, 4, 128) for c in range(8) for s in range(2)],
                    axis=0)[:, None]
    st_m = np.stack([R[c]["o_m"][:, s * 2:(s + 1) * 2].T.reshape(2, 4) for c in range(8) for s in range(2)],
                    axis=0)[:, None]
    outs = (y_prompt, y_sample, st_ckv, st_kr, st_gk, st_gv, st_C, st_n, st_m, st_dk, st_dv)
    return tuple(np.ascontiguousarray(o, dtype=np.float32) for o in outs)


LAM_INIT = 0.8 - 0.6 * float(np.exp(-0.3 * 1))
TILE_SEQS = [(0, 2, False), (2, 4, False), (4, 12, True)]


def _l1_setup(self):
    sb = self.P.sb
    i = self.inp
    self.wG = sb("wG", [128, 8, 16], BF16)
    self.dma(self.wG[:], i["o_w_in"].rearrange("(kc p) n -> p kc n", p=128)[:, :, 2048:2064], w=["wG"], eng="pool")
    self.gb = sb("gb", [4, 4], F32)
    self.dma(self.gb[:], i["o_gate_b"], w=["gb"])
    self.cm = sb("cm", [4, 2], F32)
    self.dma(self.cm[:], i["c_m"], w=["cm"])
    self.scal = sb("scal", [128, 192], F32)
    self.decbc = sb("decbc", [128, 192], F32)
    self.mout = sb("mout", [4, 4], F32)
    self.mng = sb("mng", [128, 512], F32)
    self.dma(self.mng[:], self.inh["o_mn_g"].ap().partition_broadcast(128), w=["mng"])
    self.ts(self.mng[:], self.mng[:], 0.5, None, ALU.mult, r=["mng"], w=["mng"])
    self.dng = sb("dng", [128, 128], F32)
    self.dma(self.dng[:], self.inh["o_dn_g"].ap().partition_broadcast(128), w=["dng"])
    self.ts(self.dng[:], self.dng[:], 0.5 * (1.0 - LAM_INIT), None, ALU.mult, r=["dng"], w=["dng"])
    lam = sb("lam", [128, 256], F32)
    self.dma(lam[:], self.inh["o_lambda"].ap().partition_broadcast(128), w=["lam"])
    self.nlam = sb("nlam", [128, 4], F32)
    pr = sb("lampr", [128, 128], F32)
    self.tt(pr[:].rearrange("p (a b) -> p a b", b=64), lam[:].rearrange("p (a b) -> p a b", b=128)[:, :, 0:64],
            lam[:].rearrange("p (a b) -> p a b", b=128)[:, :, 64:128], ALU.mult, r=["lam"], w=["lampr"])
    self.reduce(self.nlam[:, 0:2], pr[:].rearrange("p (a b) -> p a b", b=64), ALU.add, r=["lampr"], w=["nlam"])
    self.act(self.nlam[:, 0:2], self.nlam[:, 0:2], AF.Exp, r=["nlam"], w=["nlam"])
    self.tt(self.nlam[:, 2:3], self.nlam[:, 1:2], self.nlam[:, 0:1], ALU.subtract, r=["nlam"], w=["nlam"])
    self.ts(self.nlam[:, 2:3], self.nlam[:, 2:3], -LAM_INIT, None, ALU.add, r=["nlam"], w=["nlam"])
    self.mask2 = {}
    for nm in ("maskf", "maskb"):
        m2 = sb(nm + "2", [128, 64], F32)
        self.dma(m2[0:64, :], i["k_" + nm], w=[nm + "2"])
        self.dma(m2[64:128, :], i["k_" + nm], w=[nm + "2"])
        self.mask2[nm] = m2
    self.wL = [sb("wL%d" % k, [128, 8, 640], BF16) for k in range(2)]


def _load_head_weights(self, slot, kind, h):
    win = self.inp["o_w_in"].rearrange("(kc p) n -> p kc n", p=128)
    w = self.wL[slot]
    wk = "wL%d" % slot
    if os.environ.get("MERGEW", "0") == "1":
        o0 = 0 if kind == "C" else 2576
        self.dma(w[:, :, 0:512].rearrange("p kc (g c) -> p kc g c", c=128),
                 win[:, :, o0:o0 + 2048].rearrange("p kc (g c) -> p kc g c", c=512)[:, :, :, h * 128:(h + 1) * 128],
                 w=[wk], eng="pool")
        if kind == "C":
            self.dma(w[:, :, 512:640], win[:, :, 2064 + h * 128:2064 + (h + 1) * 128], w=[wk], eng="pool")
        return
    if kind == "C":
        offs = [0, 512, 1024, 1536, 2064]
    else:
        offs = [2576, 3088, 3600, 4112]
    for j, o in enumerate(offs):
        self.dma(w[:, :, j * 128:(j + 1) * 128], win[:, :, o + h * 128:o + (h + 1) * 128], w=[wk], eng="pool")


def _gate_chain(self, G, d, k):
    sb = self.P.sb
    ntok = G["ntok"]
    nch = ntok // 64
    tok0 = G["tok0"]
    tile0 = tok0 // 128
    ch0 = tok0 // 64
    tl = self.gts[k]
    bank, bkey = self.gbanks[k]
    K_ = str(k)
    small = tl["small"]
    sm = lambda k: small[:, k * 16:k * 16 + nch]
    dexp = tl["dexp"]

    if True:
        for gi, nm in ((d * 2, "ig"), (d * 2 + 1, "lf")):
            for bi, b in enumerate(G["batches"]):
                ps, pk = bank, bkey
                for kc in range(8):
                    self.mm(ps[0:4, :], self.wG[:, kc, gi * 4:(gi + 1) * 4], self.hT[:, kc, b * 512:(b + 1) * 512],
                            start=(kc == 0), stop=(kc == 7), r=["wG", *self.hk(b)], w=[pk])
                yield
                self.ts(tl[nm][:, bi * 512:(bi + 1) * 512], ps[0:4, :], self.gb[:, gi:gi + 1], None, ALU.add,
                        r=[pk, "gb"], w=["g_" + nm + K_])
        t_ = tl["lf"]
        yield
        self.act(t_[:, :ntok], t_[:, :ntok], AF.Exp, r=["g_lf" + K_], w=["g_lf" + K_], scale=-1.0)
        self.act(t_[:, :ntok], t_[:, :ntok], AF.Ln, r=["g_lf" + K_], w=["g_lf" + K_], bias=1.0)
        yield
        self.ts(t_[:, :ntok], t_[:, :ntok], -1.0, None, ALU.mult, r=["g_lf" + K_], w=["g_lf" + K_])
    rm = self.g_rm
    c3 = lambda ap: ap.rearrange("p (c t) -> p c t", t=64)
    if True:
        ig_t, lf_t, cum_t = tl["ig"], tl["lf"], tl["cum"]
        ik, lk, ck = "g_ig" + K_, "g_lf" + K_, "g_cum" + K_
        self.scan(cum_t[:, :ntok], rm[:, :ntok], lf_t[:, :ntok], 0.0, ALU.mult, ALU.add, r=["g_rm", lk], w=[ck])
        tot, A, mseq, mprev, Gm, dec = sm(0 + d), sm(2 + d), sm(4 + d), sm(6 + d), sm(8 + d), sm(10 + d)
        self.cp(tot, c3(cum_t[:, :ntok])[:, :, 63], r=[ck], w=["g_small" + K_])
        if d == 1:
            self.tt(c3(cum_t[:, :ntok]), tot.unsqueeze(2).broadcast_to([4, nch, 64]), c3(cum_t[:, :ntok]), ALU.subtract,
                    r=["g_small" + K_, ck], w=[ck])
            self.tt(cum_t[:, :ntok], cum_t[:, :ntok], lf_t[:, :ntok], ALU.add, r=[ck, lk], w=[ck])
        self.tt(ig_t[:, :ntok], ig_t[:, :ntok], cum_t[:, :ntok], ALU.subtract, r=[ik, ck], w=[ik])
        self.reduce(A, c3(ig_t[:, :ntok]), ALU.max, r=[ik], w=["g_small" + K_])
        for (t0, t1, has_ctx) in TILE_SEQS:
            c0, c1 = t0 * 2 - ch0, t1 * 2 - ch0
            if c0 < 0 or c1 > nch:
                continue
            if has_ctx:
                init = self.cm[:, d:d + 1]
                self_k = ["cm"]
            else:
                init = 0.0
                self_k = []
            if d == 0:
                self.scan(mseq[:, c0:c1], A[:, c0:c1], tot[:, c0:c1], init, ALU.max, ALU.add,
                          r=["g_small" + K_] + self_k, w=["g_small" + K_])
                if has_ctx:
                    self.cp(mprev[:, c0:c0 + 1], init, r=["cm"], w=["g_small" + K_])
                else:
                    self.memset(mprev[:, c0:c0 + 1], 0.0, w=["g_small" + K_])
                self.cp(mprev[:, c0 + 1:c1], mseq[:, c0:c1 - 1], r=["g_small" + K_], w=["g_small" + K_])
            else:
                self.scan(mseq[:, c0:c1][:, ::-1], A[:, c0:c1][:, ::-1], tot[:, c0:c1][:, ::-1], init, ALU.max, ALU.add,
                          r=["g_small" + K_] + self_k, w=["g_small" + K_])
                if has_ctx:
                    self.cp(mprev[:, c1 - 1:c1], init, r=["cm"], w=["g_small" + K_])
                else:
                    self.memset(mprev[:, c1 - 1:c1], 0.0, w=["g_small" + K_])
                self.cp(mprev[:, c0:c1 - 1], mseq[:, c0 + 1:c1], r=["g_small" + K_], w=["g_small" + K_])
            if not has_ctx:
                s_idx = t0 // 2
                last = mseq[:, c1 - 1:c1] if d == 0 else mseq[:, c0:c0 + 1]
                self.cp(self.mout[:, s_idx * 2 + d:s_idx * 2 + d + 1], last, r=["g_small" + K_], w=["mout"])
        self.tt(Gm, mprev, A, ALU.max, r=["g_small" + K_], w=["g_small" + K_])
        self.tt(dec, mprev, Gm, ALU.subtract, r=["g_small" + K_], w=["g_small" + K_])
        yield
        self.act(dec, dec, AF.Exp, r=["g_small" + K_], w=["g_small" + K_])
        yield
        Gbc = Gm.unsqueeze(2).broadcast_to([4, nch, 64])
        self.tt(c3(ig_t[:, :ntok]), c3(ig_t[:, :ntok]), Gbc, ALU.subtract, r=[ik, "g_small" + K_], w=[ik])
        yield
        self.act(ig_t[:, :ntok], ig_t[:, :ntok], AF.Exp, r=[ik], w=[ik])
        self.tt(c3(cum_t[:, :ntok]), c3(cum_t[:, :ntok]), Gbc, ALU.add, r=[ck, "g_small" + K_], w=[ck])
        yield
        self.act(cum_t[:, :ntok], cum_t[:, :ntok], AF.Exp, r=[ck], w=[ck], scale=-1.0)
        yield
        ps, pk = bank, bkey
        ntile = ntok // 128
        for tl_i in range(ntile):
            for q, src, sk in ((0, ig_t, ik), (1, cum_t, ck)):
                col = (tl_i * 2 + q) * 4
                self.mm(ps[:, col:col + 4], src[0:4, tl_i * 128:(tl_i + 1) * 128], self.ident_f[0:4, 0:4],
                        r=[sk, "identf"], w=[pk])
        yield
        for tl_i in range(ntile):
            base = (((tile0 + tl_i) * 2 + d) * 2) * 4
            self.cp(self.scal[:, base:base + 8], ps[:, tl_i * 8:tl_i * 8 + 8], r=[pk], w=["scal"])
        self.tt(dexp[:, :nch, :], dec.unsqueeze(2).broadcast_to([4, nch, 4]),
                self.ident_f[0:4, 0:4].unsqueeze(1).broadcast_to([4, nch, 4]), ALU.mult,
                r=["g_small" + K_, "identf"], w=["g_dexp" + K_])
        yield
        ps2, pk2 = bank, bkey
        self.mm(ps2[:, 0:nch * 4], self.ones_f[0:4, :], dexp[:, :nch, :].rearrange("p c h -> p (c h)"),
                r=["onesf", "g_dexp" + K_], w=[pk2])
        yield
        self.cp(self.decbc[:, (d * 24 + ch0) * 4:(d * 24 + ch0 + nch) * 4], ps2[:, 0:nch * 4], r=[pk2], w=["decbc"])


def _gate_prep_all(self):
    sb = self.P.sb
    self.g_rm = sb("g_rm", [4, 1024], F32)
    self.memset(self.g_rm[:], 1.0, w=["g_rm"])
    self.memset(self.g_rm[:].rearrange("p (c t) -> p c t", t=64)[:, :, 0:1], 0.0, w=["g_rm"])
    self.gts = []
    for k in range(4):
        t_ = {nm: sb("g_%s%d" % (nm, k), [4, 1024], F32) for nm in ("ig", "lf", "cum")}
        t_["small"] = sb("g_small%d" % k, [4, 256], F32)
        t_["dexp"] = sb("g_dexp%d" % k, [4, 16, 4], F32)
        self.gts.append(t_)
    self.gbanks = [(self.psX[0], "psX0"), (self.psX[1], "psX1"), (self.psX[2], "psX2"), (self.psS[0], "psS0")]
    gens = [self.gate_chain(G, d, gi * 2 + d) for gi, G in enumerate((GP, GS)) for d in range(2)]
    while gens:
        for g_ in list(gens):
            try:
                next(g_)
            except StopIteration:
                gens.remove(g_)


def _phase_C_head(self, h, slot):
    w = self.wL[slot]
    wk = "wL%d" % slot
    hT = self.hT
    qT, kT, ktm, vau, og, zg, hbuf = self.c_qT, self.c_kT, self.c_ktm, self.c_vau, self.c_og, self.c_zg, self.c_hbuf
    dscale = 128.0 ** -0.5
    for b in range(3):
        rhs = lambda kc: hT[:, kc, b * 512:(b + 1) * 512]
        ps, pk = self.nextX()
        self.proj_fm(ps[:], pk, lambda kc: w[:, kc, 0:128], wk, rhs, [*self.hk(b)], 8)
        self.cp(qT[:, b * 512:(b + 1) * 512], ps[:], r=[pk], w=["c_qT"], eng="act")
        ps, pk = self.nextX()
        self.proj_fm(ps[:], pk, lambda kc: w[:, kc, 128:256], wk, rhs, [*self.hk(b)], 8)
        self.ts(kT[:, b * 512:(b + 1) * 512], ps[:], dscale, None, ALU.mult, r=[pk], w=["c_kT"])
    for t in range(12):
        b = t // 4
        ps, pk = self.nextX()
        for kc in range(8):
            self.mm(ps[:], hT[:, kc, t * 128:(t + 1) * 128], w[:, kc, 128:640], start=(kc == 0), stop=(kc == 7),
                    r=[*self.hk(b), wk], w=[pk])
        self.ts(ktm[:, t, :], ps[:, 0:128], dscale, None, ALU.mult, r=[pk], w=[("c_ktm", t)])
        self.cp(vau[:, t, 0:128], ps[:, 128:256], r=[pk], w=[("c_vau", t)], eng="act")
        f, fk = self.nextF()
        self.act(f[:, 0:256], ps[:, 256:512], AF.Tanh, r=[pk], w=[fk], scale=0.5)
        self.ts(og[:, t, :], f[:, 0:128], 0.5, 0.5, ALU.mult, ALU.add, r=[fk], w=[("c_og", t)])
        self.stt(zg[:, t, :], f[:, 128:256], 1.0, ps[:, 384:512], ALU.add, ALU.mult, r=[fk, pk], w=[("c_zg", t)])
    cs = int(os.environ.get("CS", "9"))
    if cs < 2:
        return
    self.mlstm_pre(h)
    self.memset(hbuf[:], 0.0, w=["c_hbuf"], eng="pool")
    chains = []
    for si, (t0, t1, has_ctx) in enumerate(TILE_SEQS):
        for d in range(2):
            ci = si * 2 + d
            Sf = self.c_S[ci]
            Sk = "c_S%d" % ci
            if has_ctx:
                self.dma(Sf[:, 0:128], self.inp["c_ST"][d * 4 + h], w=[Sk])
                self.dma(Sf[:, 128:129], self.inp["c_n"][d * 4 + h].rearrange("(k o) -> k o", o=1), w=[Sk])
            else:
                self.memset(Sf[:], 0.0, w=[Sk])
            chunks = list(range(t0 * 2, t1 * 2))
            if d == 1:
                chunks = chunks[::-1]
            chains.append(dict(ci=ci, d=d, si=si, chunks=chunks, has_ctx=has_ctx))
    nsteps = max(len(c["chunks"]) for c in chains)
    nsteps = min(nsteps, int(os.environ.get("CSTEP", "99")))
    for step in range(nsteps):
        live = [ch for ch in chains if step < len(ch["chunks"])]
        for stage in range(4):
            for ch in live:
                self._mlstm_step(h, ch, ch["chunks"][step], stage)
    if cs < 3:
        return
    for ch in chains:
        if ch["has_ctx"]:
            continue
        ci, d, si = ch["ci"], ch["d"], ch["si"]
        Sf, Sk = self.c_S[ci], "c_S%d" % ci
        idx = (si * 2 + d) * 4 + h
        ps, pk = self.nextX()
        self.S.emit("pe", lambda e, ps=ps, Sf=Sf: e.transpose(out=ps[:, 0:128], in_=Sf[:, 0:128], identity=self.ident_f[:]),
                    reads=[Sk, "identf"], writes=[pk])
        f, fk = self.nextF()
        self.cp(f[:, 0:128], ps[:, 0:128], r=[pk], w=[fk])
        self.dma(self.outp["o_C"][idx], f[:, 0:128], r=[fk])
        self.dma(self.outp["o_n"][idx].rearrange("(k o) -> k o", o=1), Sf[:, 128:129], r=[Sk])
    if cs < 4:
        return
    for t in range(12):
        self.tt(hbuf[:, t, :], hbuf[:, t, :], og[:, t, :], ALU.mult, r=[("c_hbuf", t), ("c_og", t)], w=[("c_hbuf", t)])
        self.act(self.junk[:, 0:128], hbuf[:, t, :], AF.Square, r=[("c_hbuf", t)], w=["junk", ("ss", t)],
                 accum_out=self.ss[:, t:t + 1])
        self.tt(zg[:, t, :], zg[:, t, :], self.mng[:, h * 128:(h + 1) * 128], ALU.mult, r=[("c_zg", t), "mng"],
                w=[("c_zg", t)], eng="pool")
    self.ts(self.rstd[:, 0:12], self.ss[:, 0:12], 1.0 / 128, EPS, ALU.mult, ALU.add, r=["ss"], w=["rstd"])
    self.tt(self.rstd[:, 0:12], self.rstd[:, 0:12], self.mhalf[:, 0:12], ALU.pow, r=["rstd", "mhalf"], w=["rstd"],
            eng="pool")
    for t in range(12):
        xs = self.xs[t % 2]
        xk = "xs%d" % (t % 2)
        self.stt(xs[:, 0:128], hbuf[:, t, :], self.rstd[:, t:t + 1], zg[:, t, :], ALU.mult, ALU.mult,
                 r=[("c_hbuf", t), ("rstd", t), ("c_zg", t)], w=[xk])
        self.tr(self.psT[:, 0:128], xs[:, 0:128], self.ident_b[:], r=[xk, "identb"], w=["psT"])
        self.cp(self.mixT[:, h, t * 128:(t + 1) * 128], self.psT[:, 0:128], r=["psT"], w=[("mixT", t // 4)])


def _mlstm_pre(self, h):
    qT, kT, ktm = self.c_qT, self.c_kT, self.c_ktm
    for t in range(12):
        ps, pk = self.nextY()
        for half in range(2):
            c = t * 2 + half
            cols = slice(c * 64, (c + 1) * 64)
            p0 = half * 64
            self.mm(ps[p0:p0 + 64, 0:64], kT[:, cols], qT[:, cols], r=["c_kT", "c_qT"], w=[pk])
        for d in range(2):
            col = ((t * 2 + d) * 2 + 0) * 4 + h
            e_ap = self.scal[:, col:col + 1]
            mask = self.mask2["maskf" if d == 0 else "maskb"]
            mk = "maskf2" if d == 0 else "maskb2"
            self.stt(self.c_smta[:, t, d, :], ps[:, 0:64], e_ap, mask[:, :], ALU.mult, ALU.mult,
                     r=[pk, "scal", mk], w=[("c_smta", t)])
            self.act(self.c_kpa[:, t, d, :], ktm[:, t, :], AF.Copy, r=[("c_ktm", t), "scal"], w=[("c_kpa", t)],
                     scale=e_ap)


def _mlstm_step(self, h, ch, c, stage):
    ci, d = ch["ci"], ch["d"]
    t = c // 2
    p0 = (c % 2) * 64
    p1 = p0 + 64
    cols = slice(c * 64, (c + 1) * 64)
    qT, vau, hbuf = self.c_qT, self.c_vau, self.c_hbuf
    Sf, Sk = self.c_S[ci], "c_S%d" % ci
    Sb, Sbk = self.c_Sb[ci], "c_Sb%d" % ci
    dd, ddk = self.c_dd[ci], "c_dd%d" % ci
    thr_ap = self.scal[p0:p1, ((t * 2 + d) * 2 + 1) * 4 + h:((t * 2 + d) * 2 + 1) * 4 + h + 1]
    dec_ap = self.decbc[:, (d * 24 + c) * 4 + h:(d * 24 + c) * 4 + h + 1]
    if stage == 0:
        self.act(Sb[:, 0:129], Sf[:, 0:129], AF.Copy, r=[Sk, "decbc"], w=[Sbk], scale=dec_ap)
    elif stage == 1:
        ps2, pk2 = self.nextY()
        ch["ps2"] = (ps2, pk2)
        self.mm(ps2[:, 256:385], self.c_kpa[p0:p1, t, d, :], vau[p0:p1, t, 0:129], r=[("c_kpa", t), ("c_vau", t)], w=[pk2])
        self.mm(ps2[p0:p1, 0:129], qT[:, cols], Sb[:, 0:129], start=True, stop=False, r=["c_qT", Sbk], w=[pk2])
        self.mm(ps2[p0:p1, 0:129], self.c_smta[p0:p1, t, d, :], vau[p0:p1, t, 0:129], start=False, stop=True,
                r=[("c_smta", t), ("c_vau", t)], w=[pk2])
    elif stage == 2:
        ps2, pk2 = ch["ps2"]
        self.stt(Sf[:, 0:129], Sf[:, 0:129], dec_ap, ps2[:, 256:385], ALU.mult, ALU.add, r=[Sk, "decbc", pk2], w=[Sk])
        self.act(dd[p0:p1, 0:1], ps2[p0:p1, 128:129], AF.Abs, r=[pk2], w=[ddk])
    else:
        ps2, pk2 = ch["ps2"]
        self.ts(dd[p0:p1, 0:1], dd[p0:p1, 0:1], thr_ap, None, ALU.max, r=[ddk, "scal"], w=[ddk])
        self.recip(dd[p0:p1, 0:1], dd[p0:p1, 0:1], r=[ddk], w=[ddk])
        self.stt(hbuf[p0:p1, t, :], ps2[p0:p1, 0:128], dd[p0:p1, 0:1], hbuf[p0:p1, t, :], ALU.mult, ALU.add,
                 r=[pk2, ddk, ("c_hbuf", t)], w=[("c_hbuf", t)])


def _nextY(self):
    i = self.yi
    self.yi = (i + 1) % 7
    if i < 3:
        return self.psX[i], "psX%d" % i
    if i < 5:
        return self.psS[i - 3], "psS%d" % (i - 3)
    return self.psO[i - 5], "psO%d" % (i - 5)


def _alloc_C(self):
    sb = self.P.sb
    self.c_qT = sb("c_qT", [128, NT], BF16)
    self.c_kT = sb("c_kT", [128, NT], BF16)
    self.c_ktm = sb("c_ktm", [128, 12, 128], BF16)
    self.c_vau = sb("c_vau", [128, 12, 130], BF16)
    self.c_og = sb("c_og", [128, 12, 128], BF16)
    self.c_zg = sb("c_zg", [128, 12, 128], BF16)
    self.c_hbuf = sb("c_hbuf", [128, 12, 128], F32)
    self.c_S = [sb("c_S%d" % k, [128, 130], F32) for k in range(6)]
    self.c_Sb = [sb("c_Sb%d" % k, [128, 130], BF16) for k in range(6)]
    self.c_smta = sb("c_smta", [128, 12, 2, 64], BF16)
    self.c_kpa = sb("c_kpa", [128, 12, 2, 128], BF16)
    self.c_dd = [sb("c_dd%d" % k, [128, 2], F32) for k in range(6)]
    self.memset(self.c_vau[:, :, 128:129], 1.0, w=["c_vau"])
    self.yi = 0


def _alloc_D(self):
    sb = self.P.sb
    self.d_qT = [sb("d_qT%d" % k, [128, NT], BF16) for k in range(2)]
    self.d_kT = [sb("d_kT%d" % k, [128, TK], BF16) for k in range(2)]
    self.d_v = [sb("d_v%d" % k, [128, 14, 130], BF16) for k in range(2)]
    self.d_g = [sb("d_g%d" % k, [128, 12, 128], BF16) for k in range(2)]
    self.d_a = sb("d_a", [128, 12, 128], F32)
    self.pT = [sb("pT%d" % k, [128, 10, 512], BF16) for k in range(2)]
    for k in range(2):
        self.memset(self.d_v[k][:, :, 128:129], 1.0, w=["d_v%d" % k])


def _prepD(self, h, slot):
    w = self.wL[slot]
    wk = "wL%d" % slot
    hT = self.hT
    qT, kT, dv_, dg = self.d_qT[slot], self.d_kT[slot], self.d_v[slot], self.d_g[slot]
    qk, kk, vk, gk = "d_qT%d" % slot, "d_kT%d" % slot, "d_v%d" % slot, "d_g%d" % slot
    for b in range(3):
        isS = b > 0
        d0 = kcol(b)
        q0 = b * 512
        pos0 = (b - 1) * 512
        rhs = lambda kc, b=b: hT[:, kc, b * 512:(b + 1) * 512]
        for which in range(2):
            ps, pk = self.nextX()
            self.proj_fm(ps[:], pk, lambda kc: w[:, kc, which * 128:(which + 1) * 128], wk, rhs, [*self.hk(b)], 8)
            yield
            if which == 0:
                dst, dk = qT[:, q0:q0 + 512], [qk]
            else:
                dst, dk = kT[:, d0:d0 + 512], [kk]
            if isS:
                yield from self.rope_fm(ps[:], [pk], (0, 128), 512, pos0, self.cos64, "cos64", self.sin64, "sin64", dst, dk)
            else:
                self.cp(dst, ps[:], r=[pk], w=dk)
                yield
                if which == 1:
                    for j in range(4):
                        self.fm_to_out(kT[:, d0 + j * 128:d0 + (j + 1) * 128], [kk], 128,
                                       self.outp["o_dk"][j * 128:(j + 1) * 128, h * 128:(h + 1) * 128])
                        yield
        for j in range(4):
            t = b * 4 + j
            tc = d0 // 128 + j
            ps, pk = self.nextX()
            for kc in range(8):
                self.mm(ps[:, 0:256], hT[:, kc, t * 128:(t + 1) * 128], w[:, kc, 256:512], start=(kc == 0), stop=(kc == 7),
                        r=[*self.hk(b), wk], w=[pk])
            yield
            self.cp(dv_[:, tc, 0:128], ps[:, 0:128], r=[pk], w=[vk])
            f, fk = self.nextF()
            if not isS:
                self.cp(f[:, 256:384], ps[:, 0:128], r=[pk], w=[fk], eng="act")
                self.dma(self.outp["o_dv"][t * 128:(t + 1) * 128, h * 128:(h + 1) * 128], f[:, 256:384], r=[fk])
            self.act(f[:, 0:128], ps[:, 128:256], AF.Tanh, r=[pk], w=[fk], scale=0.5)
            yield
            self.stt(dg[:, t, :], f[:, 0:128], 1.0, ps[:, 128:256], ALU.add, ALU.mult, r=[fk, pk], w=[(gk, t)])
            self.tt(dg[:, t, :], dg[:, t, :], self.dng[:], ALU.mult, r=[(gk, t), "dng"], w=[(gk, t)], eng="pool")
    self.dma(kT[:, 512:768], self.inp["c_kdT"][h], w=[kk], eng="pool")
    self.dma(dv_[:, 4:6, 0:128],
             self.inp["c_vd"].rearrange("(tc p) n -> p tc n", p=128)[:, :, h * 128:(h + 1) * 128], w=[vk],
             eng="pool")


def _phase_D_head(self, h, slot):
    qT, kT, dv_, dg, da = self.d_qT[slot], self.d_kT[slot], self.d_v[slot], self.d_g[slot], self.d_a
    qk, kk, vk, gk = "d_qT%d" % slot, "d_kT%d" % slot, "d_v%d" % slot, "d_g%d" % slot
    for sub in range(2):
        st_ = {}

        def block(nqt, reg, dv_, sub=sub):
            h_ = self.rdi % 2
            self.rdi = (h_ + 1) % 2
            for bnk in range((nqt + 1) // 2):
                n_ = min(2, nqt - 2 * bnk)
                den = self.psO[bnk][:, 0:n_ * 129].rearrange("p (q w) -> p q w", w=129)[:, :, 128]
                f = self.rden[:, h_ * 4 + 2 * bnk:h_ * 4 + 2 * bnk + n_]
                fk = ("rden", (h_, bnk))
                self.recip(f, den, r=["psO%d" % bnk], w=[fk])
                if sub == 1:
                    self.ts(f, f, self.nlam[:, 2:3], None, ALU.mult, r=[fk, "nlam"], w=[fk])
            st_["h"] = h_
            st_["q"] = 0

        def out_fn(qtile, po, ok, sub=sub):
            h_, q_ = st_["h"], st_["q"]
            st_["q"] = q_ + 1
            f = self.rden[:, h_ * 4 + q_:h_ * 4 + q_ + 1]
            fk = ("rden", (h_, q_ // 2))
            if sub == 0:
                self.ts(da[:, qtile, :], po[:, 0:128], f, None, ALU.mult, r=[ok, fk], w=[("d_a", qtile)])
            else:
                self.stt(da[:, qtile, :], po[:, 0:128], f, da[:, qtile, :], ALU.mult, ALU.add,
                         r=[ok, fk, ("d_a", qtile)], w=[("d_a", qtile)])
        out_fn.block = block
        self.attention(SEGS, kT, kk, (sub * 64, sub * 64 + 64), qT, qk,
                       lambda tc: dv_[:, tc, 0:129], vk, 128, 0.125, out_fn)
    self.run_bg()
    for t in range(12):
        self.act(self.junk[:, 0:128], da[:, t, :], AF.Square, r=[("d_a", t)], w=["junk", ("ss", t)],
                 accum_out=self.ss[:, t:t + 1])
    self.ts(self.rstd[:, 0:12], self.ss[:, 0:12], 1.0 / 128, EPS, ALU.mult, ALU.add, r=["ss"], w=["rstd"])
    self.tt(self.rstd[:, 0:12], self.rstd[:, 0:12], self.mhalf[:, 0:12], ALU.pow, r=["rstd", "mhalf"],
            w=["rstd"], eng="pool")
    for t in range(12):
        xs = self.xs[t % 2]
        xk = "xs%d" % (t % 2)
        self.stt(xs[:, 0:128], da[:, t, :], self.rstd[:, t:t + 1], dg[:, t, :], ALU.mult, ALU.mult,
                 r=[("d_a", t), ("rstd", t), (gk, t)], w=[xk])
        self.tr(self.psT[:, 0:128], xs[:, 0:128], self.ident_b[:], r=[xk, "identb"], w=["psT"])
        self.cp(self.mixT[:, 4 + h, t * 128:(t + 1) * 128], self.psT[:, 0:128], r=["psT"], w=[("mixT", t // 4)])


def _layer1(self):
    lim = int(os.environ.get("L1S", "99"))
    nhc = int(os.environ.get("L1HC", "4"))
    nhd = int(os.environ.get("L1HD", "4"))
    st_parent = self.P.stack
    self.norm_to_hT("o", False)
    with contextlib.ExitStack() as s2:
        self.P.stack = s2
        self.l1_setup()
        self.load_head_weights(0, "C", 0)
        with contextlib.ExitStack() as s2g:
            self.P.stack = s2g
            if lim >= 2:
                self.gate_prep_all()
        self.S.barrier()
        if lim >= 2:
            self.dma(self.outp["o_m"], self.mout[:], r=["mout"])
        with contextlib.ExitStack() as s2c:
            self.P.stack = s2c
            self.alloc_C()
            for h in range(4):
                if lim < 3 or h >= nhc:
                    break
                if h < 3:
                    self.load_head_weights((h + 1) % 2, "C", h + 1)
                else:
                    self.load_head_weights(0, "D", 0)
                self.phase_C_head(h, h % 2)
        self.S.barrier()
        with contextlib.ExitStack() as s2d:
            self.P.stack = s2d
            self.alloc_D()
            if lim >= 4:
                self.load_head_weights(1, "D", 1)
                self.bg = self.prepD(0, 0)
                self.run_bg()
            for h in range(4):
                if lim < 4 or h >= nhd:
                    break
                if h + 2 < 4:
                    self.load_head_weights(h % 2, "D", h + 2)
                if h < 3:
                    self.bg = self.prepD(h + 1, (h + 1) % 2)
                self.phase_D_head(h, h % 2)
    self.S.barrier()
    with contextlib.ExitStack() as so:
        self.P.stack = so
        self.gate_bc = [self.P.sb("gbc%d" % v, [128, D], F32) for v in range(2)]
        self.wout = self.P.sb("wout", [128, 8, D], BF16)
        if lim >= 5:
            self.load_wout("o")
            self.gate_bcast("o")
            self.out_proj(False)
    self.S.barrier()
    self.P.stack = st_parent


Builder.l1_setup = _l1_setup
Builder.load_head_weights = _load_head_weights
Builder.gate_chain = _gate_chain
Builder.gate_prep_all = _gate_prep_all
Builder.phase_C_head = _phase_C_head
Builder._mlstm_step = _mlstm_step
Builder.mlstm_pre = _mlstm_pre
Builder.nextY = _nextY
Builder.alloc_C = _alloc_C
Builder.alloc_D = _alloc_D
Builder.phase_D_head = _phase_D_head
Builder.prepD = _prepD
Builder.layer1 = _layer1
```

```python
import contextlib
import os
import numpy as np
import concourse.bass as bass
import concourse.mybir as mybir
from concourse.bass_utils import run_bass_kernel_spmd

F32 = mybir.dt.float32
BF16 = mybir.dt.bfloat16
AF = mybir.ActivationFunctionType
ALU = mybir.AluOpType
AX = mybir.AxisListType

COMPUTE = ["pe", "act", "dve", "pool"]
ENGS = ["pe", "act", "dve", "pool", "sp"]
EPS = 1e-6


class Sched:
    def __init__(self, nc, n_dma=8, n_swp=88):
        self.nc = nc
        self.ops = {e: [] for e in ENGS}
        self.count = {}
        self.clock = {e: {} for e in ENGS}
        self.vc = {}
        self.res = {}
        self.n_dma = n_dma
        self.dma_rr = 0
        self.tracks = list(COMPUTE) + ["d%d" % i for i in range(n_dma)]
        for t in self.tracks:
            self.count[t] = 0
        self.n_sw = 0
        self.n_swp = n_swp
        self.sw_last = [None] * n_swp
        self.sw_waiters = {}
        self.retired = {}
        self.phys = {}
        self.n_waits = 0
        self.pending = {e: {} for e in ENGS}

    def barrier(self):
        snap = {t: c for t, c in self.count.items() if c > 0 and t not in self.retired}
        for e in ENGS:
            self.pending[e] = dict(snap)

    def _states(self, key):
        tk, sub = key
        d = self.res.setdefault(tk, {})
        if sub is None:
            if None not in d:
                d[None] = {"w": None, "r": {}}
            return list(d.values())
        out = []
        if None in d:
            out.append(d[None])
        if sub not in d:
            d[sub] = {"w": None, "r": {}}
        out.append(d[sub])
        return out

    @staticmethod
    def _norm(k):
        if isinstance(k, str):
            return (k, None)
        return k

    def emit(self, eng, fn, reads=(), writes=(), dma=False):
        reads = [self._norm(k) for k in reads]
        writes = [self._norm(k) for k in writes]
        deps = set()
        for k in reads:
            for st in self._states(k):
                if st["w"] is not None:
                    deps.add(st["w"])
        for k in writes:
            for st in self._states(k):
                if st["w"] is not None:
                    deps.add(st["w"])
                for t, i in st["r"].items():
                    deps.add((t, i))
        if os.environ.get("RR", "1") == "1":
            for k in reads:
                if k[0].startswith("ps"):
                    for st in self._states(k):
                        for t, i in st["r"].items():
                            deps.add((t, i))
        if eng == "pe":
            deps = {d for d in deps if d[0] != "pe"}
        deps = {(self.retired[t] if t in self.retired else (t, i)) for (t, i) in deps}
        if dma and eng == "pool":
            p = self.n_sw % self.n_swp
            assert self.n_sw < self.n_swp, "software-DMA semaphore reuse (sem_clear) faults on hardware"
            prev = self.sw_last[p]
            if prev is not None:
                extra = {(prev, 0)} | set(self.sw_waiters.get(prev, ()))
                self._emit_clear(p, extra)
                self.retired[prev] = ("pool", self.count["pool"] - 1)
            track = "dw%d" % self.n_sw
            self.n_sw += 1
            self.count[track] = 0
            self.phys[track] = p
            self.sw_last[p] = track
            self.sw_waiters[track] = set()
        elif dma:
            track = "d%d" % self.dma_rr
            self.dma_rr = (self.dma_rr + 1) % self.n_dma
            if self.count[track] > 0:
                deps.add((track, self.count[track] - 1))
        else:
            track = eng
        clk = self.clock[eng]
        need = {}
        for t, i in deps:
            if need.get(t, 0) < i + 1:
                need[t] = i + 1
        if self.pending[eng]:
            for t, v in self.pending[eng].items():
                if need.get(t, 0) < v:
                    need[t] = v
            self.pending[eng] = {}
        waits = [(t, v) for t, v in need.items() if clk.get(t, 0) < v]
        for t, v in waits:
            for tt, vv in self.vc[(t, v - 1)].items():
                if clk.get(tt, 0) < vv:
                    clk[tt] = vv
        waits = [(t, v) for (t, v) in waits if not any(
            (t2, v2) != (t, v) and self.vc[(t2, v2 - 1)].get(t, 0) >= v for (t2, v2) in waits)]
        self.n_waits += len(waits)
        idx = self.count[track]
        self.count[track] = idx + 1
        myvc = dict(clk)
        myvc[track] = idx + 1
        self.vc[(track, idx)] = myvc
        self.ops[eng].append((waits, fn, track))
        for t, v in waits:
            if t in self.sw_waiters:
                self.sw_waiters[t].add((track, idx))
        for k in writes:
            self._states(k)
            st = self.res[k[0]][k[1]]
            st["w"] = (track, idx)
            st["r"] = {}
            if k[1] is None:
                for sub, s2 in self.res[k[0]].items():
                    if sub is not None:
                        s2["w"] = (track, idx)
                        s2["r"] = {}
        for k in reads:
            self._states(k)
            self.res[k[0]][k[1]]["r"][track] = idx
        return (track, idx)

    def _emit_clear(self, p, extra_deps):
        clk = self.clock["pool"]
        need = {}
        for t, i in extra_deps:
            if t in self.retired:
                t, i = self.retired[t]
            if need.get(t, 0) < i + 1:
                need[t] = i + 1
        if self.pending["pool"]:
            for t, v in self.pending["pool"].items():
                if need.get(t, 0) < v:
                    need[t] = v
            self.pending["pool"] = {}
        waits = [(t, v) for t, v in need.items() if clk.get(t, 0) < v]
        for t, v in waits:
            for tt, vv in self.vc[(t, v - 1)].items():
                if clk.get(tt, 0) < vv:
                    clk[tt] = vv
        idx = self.count["pool"]
        self.count["pool"] = idx + 1
        myvc = dict(clk)
        myvc["pool"] = idx + 1
        self.vc[("pool", idx)] = myvc
        for t, v in waits:
            if t in self.sw_waiters:
                self.sw_waiters[t].add(("pool", idx))
        self.ops["pool"].append((waits, ("clear", p), "pool"))
        self.n_waits += len(waits)

    def sem_of(self, sems, t):
        if t in self.phys:
            return sems["swp%d" % self.phys[t]]
        return sems[t]

    def sem_names(self):
        return list(COMPUTE) + ["d%d" % i for i in range(self.n_dma)] + ["swp%d" % i for i in range(min(self.n_swp, max(self.n_sw, 1)))]

    def finish(self, sems, block):
        engmap = {"pe": block.tensor, "act": block.scalar, "dve": block.vector,
                  "pool": block.gpsimd, "sp": block.sync}
        final = {t: c for t, c in self.count.items() if c > 0 and t not in self.retired}

        def body(ename):
            def run(e):
                for waits, fn, track in self.ops[ename]:
                    for t, v in waits:
                        e.wait_ge(self.sem_of(sems, t), v * (16 if t[0] == "d" else 1))
                    if isinstance(fn, tuple):
                        e.sem_clear(sems["swp%d" % fn[1]])
                        e.drain().then_inc(sems["pool"], 1)
                        continue
                    inst = fn(e)
                    inst.then_inc(self.sem_of(sems, track), 16 if track[0] == "d" else 1)
                if ename == "sp":
                    for t, c in final.items():
                        e.wait_ge(self.sem_of(sems, t), c * (16 if t[0] == "d" else 1))
            return run

        for ename in ENGS:
            engmap[ename](body(ename))


class TPool:
    def __init__(self, nc, stack):
        self.nc = nc
        self.stack = stack
        self.n = 0
        self.bytes = 0

    def sb(self, name, shape, dtype):
        self.n += 1
        sz = 1
        for s in shape[1:]:
            sz *= s
        self.bytes += sz * (2 if dtype == BF16 else 4)
        return self.stack.enter_context(self.nc.sbuf_tensor("%s_%d" % (name, self.n), list(shape), dtype))

    def ps(self, name, shape, dtype=F32):
        self.n += 1
        return self.stack.enter_context(self.nc.psum_tensor("%s_%d" % (name, self.n), list(shape), dtype))


D = 1024
NT = 1536
E_IN = 2464
O_IN = 4624
GP = dict(name="P", tok0=0, ntok=512, batches=[0], nseg=2, seglen=256, ctx=0, v=0)
GS = dict(name="S", tok0=512, ntok=1024, batches=[1, 2], nseg=1, seglen=1024, ctx=256, v=1)
TK = 1792
SEGS = [(0, 256, 0, 2, 0), (256, 256, 256, 2, 2), (512, 512, 512, 10, 4), (1024, 512, 512, 10, 8)]


def kcol(b):
    return 0 if b == 0 else 768 + (b - 1) * 512

IN_SPECS = [
    ("x_in", [NT, D]), ("c2", [128, 8, 2]),
    ("e_norm_g", [128, 8]), ("e_ada_w", [D, 3 * D]), ("e_ada_b", [128, 24]),
    ("e_w_in", [D, E_IN]), ("e_qn_g", [128, 3]), ("e_w_uq", [384, 768]), ("e_kvn_g", [128, 2]),
    ("e_w_ukv", [256, 1024]), ("e_gq_g", [128, 1]), ("e_gk_g", [128, 1]), ("e_w_out", [D, D]),
    ("o_norm_g", [128, 8]), ("o_ada_w", [D, 3 * D]), ("o_ada_b", [128, 24]),
    ("o_w_in", [D, O_IN]), ("o_gate_b", [4, 4]), ("o_mn_g", [1, 512]), ("o_lambda", [1, 256]),
    ("o_dn_g", [1, 128]), ("o_w_out", [D, D]), ("final_g", [1, D]),
    ("c_ckvT", [256, 256]), ("c_krT", [32, 256]), ("c_gkT", [128, 256]), ("c_gv", [256, 128]),
    ("c_ST", [8, 128, 128]), ("c_n", [8, 128]), ("c_m", [4, 2]), ("c_kdT", [4, 128, 256]),
    ("c_vd", [256, 512]),
    ("k_ident", [128, 128]), ("k_bones", [128, 128]), ("k_psw", [128, 128]),
    ("k_maskf", [64, 64]), ("k_maskb", [64, 64]),
    ("k_cos64", [128, 1024]), ("k_sin64", [128, 1024]), ("k_cos32", [128, 1024]), ("k_sin32", [128, 1024]),
]
OUT_SPECS = [
    ("y", [NT, D]), ("o_ckv", [512, 256]), ("o_kr", [512, 32]), ("o_gk", [512, 128]), ("o_gv", [512, 128]),
    ("o_C", [16, 128, 128]), ("o_n", [16, 128]), ("o_m", [4, 4]), ("o_dk", [512, 512]), ("o_dv", [512, 512]),
]


class Builder:
    def __init__(self, upto=99):
        self.upto = upto
        self.nc = bass.Bass("TRN2", target_bir_lowering=False)
        nc = self.nc
        self.inh = {n: nc.dram_tensor(n, list(s), F32, kind="ExternalInput") for n, s in IN_SPECS}
        self.inp = {n: h.ap() for n, h in self.inh.items()}
        self.outp = {n: nc.dram_tensor(n, list(s), F32, kind="ExternalOutput").ap() for n, s in OUT_SPECS}
        self.xi = 0
        self.si = 0
        self.fi = 0
        self.bg = None

    def mm(self, out, lhsT, rhs, start=True, stop=True, r=(), w=()):
        self.S.emit("pe", lambda e: e.matmul(out, lhsT=lhsT, rhs=rhs, start=start, stop=stop,
                                             skip_group_check=True), reads=r, writes=w)

    def tr(self, out, in_, ident, r=(), w=()):
        self.S.emit("pe", lambda e: e.transpose(out=out, in_=in_, identity=ident), reads=r, writes=w)

    def act(self, out, in_, func, r=(), w=(), **kw):
        self.S.emit("act", lambda e: e.activation(out=out, in_=in_, func=func, **kw), reads=r, writes=w)

    def ts(self, out, in0, s1, s2, op0, op1=None, r=(), w=(), eng="dve", accum_out=None):
        if op1 is None:
            self.S.emit(eng, lambda e: e.tensor_scalar(out=out, in0=in0, scalar1=s1, scalar2=None, op0=op0),
                        reads=r, writes=w)
        elif accum_out is not None:
            self.S.emit(eng, lambda e: e.tensor_scalar(out=out, in0=in0, scalar1=s1, scalar2=s2, op0=op0, op1=op1,
                                                       accum_out=accum_out), reads=r, writes=w)
        else:
            self.S.emit(eng, lambda e: e.tensor_scalar(out=out, in0=in0, scalar1=s1, scalar2=s2, op0=op0, op1=op1),
                        reads=r, writes=w)

    def tt(self, out, in0, in1, op, r=(), w=(), eng="dve"):
        self.S.emit(eng, lambda e: e.tensor_tensor(out=out, in0=in0, in1=in1, op=op), reads=r, writes=w)

    def stt(self, out, in0, scalar, in1, op0, op1, r=(), w=()):
        self.S.emit("dve", lambda e: e.scalar_tensor_tensor(out=out, in0=in0, scalar=scalar, in1=in1, op0=op0, op1=op1),
                    reads=r, writes=w)

    def cp(self, out, in_, r=(), w=(), eng="dve"):
        if eng == "act":
            self.S.emit("act", lambda e: e.copy(out=out, in_=in_), reads=r, writes=w)
        else:
            self.S.emit(eng, lambda e: e.tensor_copy(out=out, in_=in_), reads=r, writes=w)

    def recip(self, out, in_, r=(), w=()):
        self.S.emit("dve", lambda e: e.reciprocal(out=out, in_=in_), reads=r, writes=w)

    def memset(self, ap, val, w=(), eng="dve"):
        self.S.emit(eng, lambda e: e.memset(ap, val), writes=w)

    def dma(self, out, in_, r=(), w=(), eng="sp"):
        self.S.emit(eng, lambda e: e.dma_start(out=out, in_=in_), reads=r, writes=w, dma=True)

    def scan(self, out, d0, d1, init, op0, op1, r=(), w=()):
        self.S.emit("dve", lambda e: e.tensor_tensor_scan(out=out, data0=d0, data1=d1, initial=init, op0=op0, op1=op1),
                    reads=r, writes=w)

    def reduce(self, out, in_, op, r=(), w=()):
        self.S.emit("dve", lambda e: e.tensor_reduce(out=out, in_=in_, axis=AX.X, op=op), reads=r, writes=w)

    def dbg(self, n):
        return n <= int(os.environ.get("SUB", "99"))

    @staticmethod
    def hk(b):
        return [("hT", 4 * b + j) for j in range(4)]

    def tick(self):
        if self.bg is not None:
            try:
                next(self.bg)
            except StopIteration:
                self.bg = None

    def run_bg(self):
        while self.bg is not None:
            self.tick()

    def nextX(self):
        i = self.xi
        self.xi = (i + 1) % 2
        return self.psX[i], "psX%d" % i

    def nextS(self):
        i = self.si
        self.si = (i + 1) % 3
        if i < 2:
            return self.psS[i], "psS%d" % i
        return self.psX[2], "psX2"

    def nextF(self):
        i = self.fi
        self.fi = (i + 1) % len(self.fs)
        return self.fs[i], "fs%d" % i

    def setup(self, st):
        nc = self.nc
        self.P = TPool(nc, st)
        self.S = Sched(nc)
        P = self.P
        sb = P.sb
        self.psX = [P.ps("psX%d" % i, [128, 512], F32) for i in range(3)]
        self.psS = [P.ps("psS%d" % i, [128, 512], F32) for i in range(2)]
        self.psO = [P.ps("psO%d" % i, [128, 512], F32) for i in range(2)]
        self.psT = P.ps("psT", [128, 1024], BF16)
        self.hT = sb("hT", [128, 8, NT], BF16)
        self.mixT = sb("mixT", [128, 8, NT], BF16)
        self.ident_f = sb("identf", [128, 128], F32)
        self.ident_b = sb("identb", [128, 128], BF16)
        self.bones = sb("bones", [128, 128], BF16)
        self.psw = sb("psw", [128, 128], BF16)
        self.ones_b = sb("onesb", [128, 128], BF16)
        self.ones_f = sb("onesf", [128, 128], F32)
        self.maskf = sb("maskf", [64, 64], F32)
        self.maskb = sb("maskb", [64, 64], F32)
        self.cos64 = sb("cos64", [128, 1024], BF16)
        self.sin64 = sb("sin64", [128, 1024], BF16)
        self.cos32 = sb("cos32", [128, 1024], BF16)
        self.sin32 = sb("sin32", [128, 1024], BF16)
        self.mhalf = sb("mhalf", [128, 16], F32)
        i = self.inp
        self.dma(self.ident_f[:], i["k_ident"], w=["identf"])
        self.dma(self.ident_b[:], i["k_ident"], w=["identb"], eng="pool")
        self.dma(self.bones[:], i["k_bones"], w=["bones"], eng="pool")
        self.dma(self.psw[:], i["k_psw"], w=["psw"], eng="pool")
        self.dma(self.maskf[:], i["k_maskf"], w=["maskf"])
        self.dma(self.maskb[:], i["k_maskb"], w=["maskb"])
        self.dma(self.cos64[:], i["k_cos64"], w=["cos64"], eng="pool")
        self.dma(self.sin64[:], i["k_sin64"], w=["sin64"], eng="pool")
        self.dma(self.cos32[:], i["k_cos32"], w=["cos32"], eng="pool")
        self.dma(self.sin32[:], i["k_sin32"], w=["sin32"], eng="pool")
        self.memset(self.ones_b[:], 1.0, w=["onesb"])
        self.memset(self.ones_f[:], 1.0, w=["onesf"])
        self.memset(self.mhalf[:], -0.5, w=["mhalf"])
        self.c2 = sb("c2", [128, 16], F32)
        self.sc = sb("sc", [128, 16], BF16)
        self.dma(self.c2[:], i["c2"].rearrange("p a b -> p (a b)"), w=["c2"])
        self.ng = {}
        self.adab = {}
        for L in ("e", "o"):
            self.ng[L] = sb("ng" + L, [128, 8], F32)
            self.adab[L] = sb("adab" + L, [128, 24], F32)
            self.dma(self.ng[L][:], i[L + "_norm_g"], w=["ng" + L])
            self.dma(self.adab[L][:], i[L + "_ada_b"], w=["adab" + L])
        self.qn_g = sb("qng", [128, 3], F32)
        self.kvn_g = sb("kvng", [128, 2], F32)
        self.gq_g = sb("gqg", [128, 1], F32)
        self.gk_g = sb("gkg", [128, 1], F32)
        self.dma(self.qn_g[:], i["e_qn_g"], w=["qng"])
        self.dma(self.kvn_g[:], i["e_kvn_g"], w=["kvng"])
        self.dma(self.gq_g[:], i["e_gq_g"], w=["gqg"])
        self.dma(self.gk_g[:], i["e_gk_g"], w=["gkg"])
        self.fs = [sb("fs%d" % k, [128, 512], F32) for k in range(4)]
        self.ropeb = sb("ropeb", [128, 512], BF16)
        self.junk = sb("junk", [128, D], BF16)
        self.xs = [sb("xs%d" % k, [128, D], BF16) for k in range(2)]
        self.ss = sb("ss", [128, 16], F32)
        self.rden = sb("rden", [128, 8], F32)
        self.rdi = 0
        self.rstd = sb("rstd", [128, 16], F32)
        self.mod = {L: sb("mod" + L, [128, 48], F32) for L in ("e", "o")}
        self.Amod = {L: sb("Amod" + L, [128, 16], F32) for L in ("e", "o")}
        self.diag = [sb("diag%d" % k, [128, 128], F32) for k in range(2)]
        self.tmp16 = sb("tmp16", [128, 16], F32)
        self.act(self.tmp16[:], self.c2[:], AF.Tanh, r=["c2"], w=["tmp16"], scale=0.5)
        self.ts(self.tmp16[:], self.tmp16[:], 0.5, 0.5, ALU.mult, ALU.add, r=["tmp16"], w=["tmp16"])
        self.tt(self.sc[:], self.tmp16[:], self.c2[:], ALU.mult, r=["tmp16", "c2"], w=["sc"])

    def adaln(self, L):
        self.adaslot = [self.P.sb("adas%d" % k, [128, 8, 768], BF16) for k in range(2)]
        wv = self.inp[L + "_ada_w"].rearrange("(kc p) n -> p kc n", p=128)
        ps, pk = self.nextX()
        for blk in range(4):
            slot = self.adaslot[blk % 2]
            sk = "adas%d" % (blk % 2)
            self.dma(slot[:], wv[:, :, blk * 768:(blk + 1) * 768], w=[sk], eng="pool")
            for j in range(6):
                oc = blk * 6 + j
                for kc in range(8):
                    self.mm(ps[:, oc * 2:oc * 2 + 2], slot[:, kc, j * 128:(j + 1) * 128],
                            self.sc[:, kc * 2:kc * 2 + 2], start=(kc == 0), stop=(kc == 7),
                            r=[sk, "sc"], w=[pk])
        mod = self.mod[L]
        self.tt(mod[:].rearrange("p (a b) -> p a b", b=2), ps[:, 0:48].rearrange("p (a b) -> p a b", b=2),
                self.adab[L][:].unsqueeze(2).broadcast_to([128, 24, 2]), ALU.add,
                r=[pk, "adab" + L], w=["mod" + L])
        A = self.Amod[L]
        self.ts(self.tmp16[:], mod[:, 16:32], 1.0, None, ALU.add, r=["mod" + L], w=["tmp16"])
        self.tt(A[:].rearrange("p (a b) -> p a b", b=2), self.tmp16[:].rearrange("p (a b) -> p a b", b=2),
                self.ng[L][:].unsqueeze(2).broadcast_to([128, 8, 2]), ALU.mult,
                r=["tmp16", "ng" + L], w=["Amod" + L])

    def gate_bcast(self, L):
        mod = self.mod[L]
        for v in range(2):
            for half in range(2):
                ps, pk = self.nextX()
                for cc in range(4):
                    c = half * 4 + cc
                    dg = self.diag[cc % 2]
                    dk = "diag%d" % (cc % 2)
                    self.ts(dg[:], self.ident_f[:], mod[:, 32 + c * 2 + v:32 + c * 2 + v + 1], None, ALU.mult,
                            r=["identf", "mod" + L], w=[dk])
                    self.mm(ps[:, cc * 128:(cc + 1) * 128], self.ones_f[:], dg[:], r=["onesf", dk], w=[pk])
                self.cp(self.gate_bc[v][:, half * 512:(half + 1) * 512], ps[:], r=[pk], w=["gbc%d" % v], eng="act")

    def norm_to_hT(self, L, first):
        A = self.Amod[L]
        mod = self.mod[L]
        if first:
            xt_of = lambda t: (self.xstage[t], "xst%d" % t)
            for t in range(12):
                xa, xk = xt_of(t)
                self.dma(xa[:], self.inp["x_in"][t * 128:(t + 1) * 128, :], w=[xk])
        else:
            xt_of = lambda t: (self.x1[t], "x1_%d" % t)
        for t in range(12):
            if not first:
                break
            xa, xk = xt_of(t)
            self.act(self.junk[:], xa[:], AF.Square, r=[xk], w=["junk", ("ss", t)], accum_out=self.ss[:, t:t + 1])
        self.ts(self.rstd[:, 0:12], self.ss[:, 0:12], 1.0 / D, EPS, ALU.mult, ALU.add, r=["ss"], w=["rstd"])
        self.tt(self.rstd[:, 0:12], self.rstd[:, 0:12], self.mhalf[:, 0:12], ALU.pow,
                r=["rstd", "mhalf"], w=["rstd"], eng="pool")
        tbanks = [(self.psT[:], "psT"), (self.psS[0][:].bitcast(BF16), "psS0"), (self.psS[1][:].bitcast(BF16), "psS1")]

        def emit_xs(t):
            xa, xk = xt_of(t)
            xs = self.xs[t % 2]
            sk = "xs%d" % (t % 2)
            if t % 2 == 0:
                self.ts(xs[:], xa[:], self.rstd[:, t:t + 1], None, ALU.mult, r=[xk, ("rstd", t)], w=[sk])
            else:
                self.act(xs[:], xa[:], AF.Copy, r=[xk, ("rstd", t)], w=[sk], scale=self.rstd[:, t:t + 1])
        emit_xs(0)
        for t in range(12):
            b = t // 4
            v = 0 if b == 0 else 1
            xs = self.xs[t % 2]
            sk = "xs%d" % (t % 2)
            pt, ptk = tbanks[t % 3]
            for c in range(8):
                self.tr(pt[:, c * 128:(c + 1) * 128], xs[:, c * 128:(c + 1) * 128], self.ident_b[:],
                        r=[sk, "identb"], w=[ptk])
            if t + 1 < 12:
                emit_xs(t + 1)
            for c in range(8):
                o = self.hT[:, c, t * 128:(t + 1) * 128]
                a_ = A[:, c * 2 + v:c * 2 + v + 1]
                b_ = mod[:, c * 2 + v:c * 2 + v + 1]
                if t % 2 == 0:
                    self.ts(o, pt[:, c * 128:(c + 1) * 128], a_, b_, ALU.mult, ALU.add,
                            r=[ptk, "Amod" + L, "mod" + L], w=[("hT", t)])
                else:
                    self.act(o, pt[:, c * 128:(c + 1) * 128], AF.Identity,
                             r=[ptk, "Amod" + L, "mod" + L], w=[("hT", t)], scale=a_, bias=b_)

    def proj_fm(self, ps_ap, pk, w_fn, wk, rhs_fn, rk, nk):
        for kc in range(nk):
            self.mm(ps_ap, w_fn(kc), rhs_fn(kc), start=(kc == 0), stop=(kc == nk - 1), r=[wk] + list(rk), w=[pk])

    def rms_group(self, projs, n, n_feat, lhs_ones, lk, g_fn, gk, consume):
        last = len(projs) - 1
        psy, yk = self.psX[2], "psX2"
        for c, proj in enumerate(projs):
            ps, pk = self.nextX()
            proj(ps[:, :n], pk)
            yield
            self.cp(self.uf[:, c, :n], ps[:, :n], r=[pk], w=[("uf", c)], eng="act")
            self.act(self.sq[c % 2][:, :n], ps[:, :n], AF.Square, r=[pk], w=["sq%d" % (c % 2)])
            yield
            self.mm(psy[:, :n], lhs_ones, self.sq[c % 2][:, :n], start=(c == 0), stop=(c == last),
                    r=[lk, "sq%d" % (c % 2)], w=[yk])
        yield
        rb, rk = self.rb, "rb"
        self.act(rb[:, :n], psy[:, :n], AF.Ln, r=[yk], w=[rk], scale=1.0 / n_feat, bias=EPS)
        self.act(rb[:, :n], rb[:, :n], AF.Exp, r=[rk], w=[rk], scale=-0.5)
        yield
        for c in range(len(projs)):
            un, uk = self.nextF()
            self.stt(un[:, :n], self.uf[:, c, :n], g_fn(c), rb[:, :n], ALU.mult, ALU.mult,
                     r=[("uf", c), gk, rk], w=[uk])
            yield from consume(c, un, uk)

    def rope_fm(self, u, uk, rows, n, pos0, cos_t, ck, sin_t, sk, out, ok):
        r0, r1 = rows
        self.cp(self.ropeb[r0:r1, :n], u, r=uk, w=["ropeb"])
        f1, k1 = self.nextF()
        self.tt(f1[r0:r1, :n], u, cos_t[r0:r1, pos0:pos0 + n], ALU.mult, r=list(uk) + [ck], w=[k1])
        yield
        ps, pk = self.nextX()
        self.mm(ps[r0:r1, :n], self.psw[r0:r1, r0:r1], self.ropeb[r0:r1, :n], r=["psw", "ropeb"], w=[pk])
        yield
        f2, k2 = self.nextF()
        self.tt(f2[r0:r1, :n], ps[r0:r1, :n], sin_t[r0:r1, pos0:pos0 + n], ALU.mult, r=[pk, sk], w=[k2])
        self.tt(out, f1[r0:r1, :n], f2[r0:r1, :n], ALU.add, r=[k1, k2], w=ok, eng="pool")

    def attention(self, segs, kT, kk, krows, qT, qk, v_fn, vk, dv, scale, out_fn):
        r0, r1 = krows
        pend = None
        pi = 0
        for (q0, nq, t0, ntc, qtile0) in segs:
            pT = self.pT[pi % 2]
            pk = "pT%d" % (pi % 2)
            for tc in range(ntc):
                ps, sk = self.nextS()
                self.mm(ps[:, :nq], kT[r0:r1, t0 + tc * 128:t0 + (tc + 1) * 128], qT[r0:r1, q0:q0 + nq],
                        r=[kk, qk], w=[sk])
                self.act(pT[:, tc, :nq], ps[:, :nq], AF.Exp, r=[sk], w=[(pk, tc)], scale=scale)
                if tc % 2 == 1:
                    self.tick()
            if pend is not None:
                self._pv(*pend)
            pend = (pT, pk, nq, t0, ntc, qtile0, v_fn, vk, dv, out_fn, pi)
            pi += 1
        if pend is not None:
            self._pv(*pend)

    def _pv(self, pT, pk, nq, t0, ntc, qtile0, v_fn, vk, dv, out_fn, pi):
        w_ = dv + 1

        def reg(qt):
            if dv == 64:
                return self.psO[pi % 2][:, qt * w_:(qt + 1) * w_], "psO%d" % (pi % 2)
            return self.psO[qt // 2][:, (qt % 2) * w_:(qt % 2 + 1) * w_], "psO%d" % (qt // 2)
        for qt in range(nq // 128):
            po, ok = reg(qt)
            for tc in range(ntc):
                self.mm(po, pT[:, tc, qt * 128:(qt + 1) * 128],
                        v_fn(t0 // 128 + tc), start=(tc == 0), stop=(tc == ntc - 1), r=[(pk, tc), vk], w=[ok])
            self.tick()
        if getattr(out_fn, "block", None) is not None:
            out_fn.block(nq // 128, reg, dv)
        for qt in range(nq // 128):
            po, ok = reg(qt)
            out_fn(qtile0 + qt, po, ok)

    def tm_to_mixT(self, src, skey, tile_abs, c0, nchunk):
        b = tile_abs // 4
        for c in range(nchunk):
            self.tr(self.psT[:, c * 128:(c + 1) * 128], src[:, c * 128:(c + 1) * 128], self.ident_b[:],
                    r=[skey, "identb"], w=["psT"])
        self.cp(self.mixT[:, c0:c0 + nchunk, tile_abs * 128:(tile_abs + 1) * 128],
                self.psT[:, 0:nchunk * 128].rearrange("p (c t) -> p c t", t=128),
                r=["psT"], w=[("mixT", b)])

    def fm_to_out(self, src_ap, skey, rows, out_ap):
        self.tr(self.psT[:, 0:rows], src_ap, self.ident_b[0:rows, 0:rows], r=list(skey) + ["identb"], w=["psT"])
        st, sk = self.nextF()
        self.cp(st[:, 0:rows], self.psT[:, 0:rows], r=["psT"], w=[sk])
        self.dma(out_ap, st[:, 0:rows], r=[sk])

    def gates_tm(self, w_fn, wk, dst, dkey, tile_abs, tile_rel):
        b = tile_abs // 4
        ps, pk = self.nextX()
        for kc in range(8):
            self.mm(ps[:], self.hT[:, kc, tile_abs * 128:(tile_abs + 1) * 128], w_fn(kc),
                    start=(kc == 0), stop=(kc == 7), r=[*self.hk(b), wk], w=[pk])
        f, fk = self.nextF()
        self.act(f[:], ps[:], AF.Tanh, r=[pk], w=[fk], scale=0.5)
        self.stt(dst[:, tile_rel, :], f[:], 1.0, ps[:], ALU.add, ALU.mult, r=[fk, pk], w=[(dkey, tile_rel)])

    def alloc_l0(self):
        sb = self.P.sb
        self.wA = sb("wA", [128, 8, 1184], BF16)
        self.wuq = sb("wuq", [128, 3, 768], BF16)
        self.wukv = sb("wukv", [128, 2, 1024], BF16)
        self.cqn = sb("cqn", [128, 3, NT], BF16)
        self.ckvn = sb("ckvn", [128, 2, TK], BF16)
        self.krT = sb("krT", [128, TK], BF16)
        self.qTh = [sb("qTh%d" % k, [128, NT], BF16) for k in range(2)]
        self.kTh = [sb("kTh%d" % k, [128, TK], BF16) for k in range(2)]
        self.vh = [sb("vh%d" % k, [128, 14, 66], BF16) for k in range(2)]
        for k in range(2):
            self.memset(self.vh[k][:, :, 64:65], 1.0, w=["vh%d" % k])

    def alloc_l0_shared(self):
        sb = self.P.sb
        self.uf = sb("uf", [128, 3, 512], F32)
        self.rb = sb("rb", [128, 512], F32)
        self.sq = [sb("sq%d" % k, [128, 512], BF16) for k in range(2)]
        self.wB = sb("wB", [128, 8, 1280], BF16)
        self.ga = sb("ga", [128, 12, 512], BF16)
        self.pT = [sb("pT%d" % k, [128, 10, 512], BF16) for k in range(2)]

    def load_l0_weights(self):
        i = self.inp
        win = i["e_w_in"].rearrange("(kc p) n -> p kc n", p=128)
        self.dma(self.wA[:], win[:, :, 0:1184], w=["wA"], eng="pool")
        self.dma(self.wuq[:], i["e_w_uq"].rearrange("(kc p) n -> p kc n", p=128), w=["wuq"], eng="pool")
        self.dma(self.wukv[:], i["e_w_ukv"].rearrange("(kc p) n -> p kc n", p=128), w=["wukv"], eng="pool")
        self.dma(self.wB[:], win[:, :, 1184:2464], w=["wB"], eng="pool")

    def rms_lane(self, L, projs, n, n_feat, lhs_ones, lk, g_fn, gk, consume):
        last = len(projs) - 1
        ps, pk = L["pb"]
        psy, yk = L["yb"]
        rb, rbk = L["rb"]
        for c, proj in enumerate(projs):
            uf, ufk = L["uf"][c]
            sq, sqk = L["sq"]
            proj(ps[:, :n], pk)
            yield
            self.cp(uf, ps[:, :n], r=[pk], w=ufk, eng="act")
            self.act(sq, ps[:, :n], AF.Square, r=[pk], w=[sqk])
            yield
            self.mm(psy[:, :n], lhs_ones, sq, start=(c == 0), stop=(c == last), r=[lk, sqk], w=[yk])
        yield
        self.act(rb, psy[:, :n], AF.Ln, r=[yk], w=rbk, scale=1.0 / n_feat, bias=EPS)
        self.act(rb, rb, AF.Exp, r=rbk, w=rbk, scale=-0.5)
        yield
        for c in range(len(projs)):
            uf, ufk = L["uf"][c]
            un, unk = L["un"]
            self.stt(un, uf, g_fn(c), rb, ALU.mult, ALU.mult, r=ufk + [gk] + rbk, w=unk)
            yield from consume(c, un, unk)

    def prepA_group(self):
        hT, wA = self.hT, self.wA

        def f32view(k, i):
            ap = self.pT[k][:, 2 * i:2 * i + 2, :].rearrange("p a b -> p (a b)").bitcast(F32)
            return ap, [("pT%d" % k, 2 * i), ("pT%d" % k, 2 * i + 1)]
        lanes = []
        banks = [(self.psX[0], "psX0"), (self.psX[1], "psX1"), (self.psX[2], "psX2"), (self.psS[0], "psS0")]
        for k in range(2):
            L = {"pb": banks[2 * k], "yb": banks[2 * k + 1], "sq": (self.sq[k][:], "sq%d" % k)}
            L["uf"] = [f32view(k, i) for i in range(3)]
            L["rb"] = f32view(k, 3)
            L["un"] = f32view(k, 4)
            lanes.append(L)

        def lane_cq():
            for b in range(3):
                q0 = b * 512
                rhs = lambda kc, b=b: hT[:, kc, b * 512:(b + 1) * 512]
                mk = lambda col, b=b, rhs=rhs: (lambda ps, pk: self.proj_fm(
                    ps, pk, lambda kc: wA[:, kc, col:col + 128], "wA", rhs, [*self.hk(b)], 8))

                def cons(c, un, unk, q0=q0):
                    self.cp(self.cqn[:, c, q0:q0 + 512], un, r=unk, w=["cqn"], eng="pool")
                    yield
                yield from self.rms_lane(lanes[0], [mk(c * 128) for c in range(3)], 512, 384, self.ones_b[:], "onesb",
                                         lambda c: self.qn_g[:, c:c + 1], "qng", cons)

        def lane_ckv():
            for b in range(3):
                isS = b > 0
                d0 = kcol(b)
                rhs = lambda kc, b=b: hT[:, kc, b * 512:(b + 1) * 512]
                mk = lambda col, b=b, rhs=rhs: (lambda ps, pk: self.proj_fm(
                    ps, pk, lambda kc: wA[:, kc, col:col + 128], "wA", rhs, [*self.hk(b)], 8))

                def cons(c, un, unk, d0=d0, isS=isS):
                    self.cp(self.ckvn[:, c, d0:d0 + 512], un, r=unk, w=["ckvn"], eng="pool")
                    yield
                    if not isS:
                        for j in range(4):
                            self.fm_to_out(self.ckvn[:, c, d0 + j * 128:d0 + (j + 1) * 128], ["ckvn"], 128,
                                           self.outp["o_ckv"][j * 128:(j + 1) * 128, c * 128:(c + 1) * 128])
                            yield
                yield from self.rms_lane(lanes[1], [mk(384 + c * 128) for c in range(2)], 512, 256, self.ones_b[:],
                                         "onesb", lambda c: self.kvn_g[:, c:c + 1], "kvng", cons)

        def lane_kr_gates():
            ps, pk = self.psS[1], "psS1"
            ps2, pk2 = self.psO[0], "psO0"
            for b in range(3):
                isS = b > 0
                d0 = kcol(b)
                pos0 = (b - 1) * 512
                rhs = lambda kc, b=b: hT[:, kc, b * 512:(b + 1) * 512]
                self.proj_fm(ps[0:32, :], pk, lambda kc: wA[:, kc, 640:672], "wA", rhs, [*self.hk(b)], 8)
                yield
                if isS:
                    f1, k1 = self.nextF()
                    self.cp(self.ropeb[0:32, :], ps[0:32, :], r=[pk], w=["ropeb"])
                    self.tt(f1[0:32, :], ps[0:32, :], self.cos32[0:32, pos0:pos0 + 512], ALU.mult, r=[pk, "cos32"], w=[k1])
                    yield
                    self.mm(ps2[0:32, :], self.psw[0:32, 0:32], self.ropeb[0:32, :], r=["psw", "ropeb"], w=[pk2])
                    yield
                    f2, k2 = self.nextF()
                    self.tt(f2[0:32, :], ps2[0:32, :], self.sin32[0:32, pos0:pos0 + 512], ALU.mult, r=[pk2, "sin32"], w=[k2])
                    self.tt(self.krT[0:32, d0:d0 + 512], f1[0:32, :], f2[0:32, :], ALU.add, r=[k1, k2], w=["krT"], eng="pool")
                else:
                    self.cp(self.krT[0:32, d0:d0 + 512], ps[0:32, :], r=[pk], w=["krT"])
                    yield
                    for j in range(4):
                        self.fm_to_out(self.krT[0:32, d0 + j * 128:d0 + (j + 1) * 128], ["krT"], 32,
                                       self.outp["o_kr"][j * 128:(j + 1) * 128, :])
                        yield
                for j in range(4):
                    t = b * 4 + j
                    for kc in range(8):
                        self.mm(ps[:], hT[:, kc, t * 128:(t + 1) * 128], wA[:, kc, 672:1184],
                                start=(kc == 0), stop=(kc == 7), r=[*self.hk(b), "wA"], w=[pk])
                    yield
                    f, fk = self.nextF()
                    self.act(f[:], ps[:], AF.Tanh, r=[pk], w=[fk], scale=0.5)
                    yield
                    self.stt(self.ga[:, t, :], f[:], 1.0, ps[:], ALU.add, ALU.mult, r=[fk, pk], w=[("ga", t)])

        gens = [lane_cq(), lane_ckv(), lane_kr_gates()]
        while gens:
            for g_ in list(gens):
                try:
                    next(g_)
                except StopIteration:
                    gens.remove(g_)
            yield
        self.dma(self.ckvn[:, :, 512:768], self.inp["c_ckvT"].rearrange("(c p) t -> p c t", p=128), w=["ckvn"],
                 eng="pool")
        self.dma(self.krT[0:32, 512:768], self.inp["c_krT"], w=["krT"], eng="pool")

    def prepA_head(self, h):
        T = TK
        qT, qk = self.qTh[h % 2], "qTh%d" % (h % 2)
        kT, kk = self.kTh[h % 2], "kTh%d" % (h % 2)
        vh, vk = self.vh[h % 2], "vh%d" % (h % 2)
        for b in range(3):
            q0 = b * 512
            ps, pk = self.nextX()
            self.proj_fm(ps[0:96, :], pk, lambda kc: self.wuq[:, kc, h * 96:(h + 1) * 96], "wuq",
                         lambda kc: self.cqn[:, kc, q0:q0 + 512], ["cqn"], 3)
            yield
            if b > 0:
                self.cp(qT[0:64, q0:q0 + 512], ps[0:64, :], r=[pk], w=[qk])
                yield from self.rope_fm(ps[64:96, :], [pk], (64, 96), 512, (b - 1) * 512, self.cos32, "cos32",
                                        self.sin32, "sin32", qT[64:96, q0:q0 + 512], [qk])
            else:
                self.cp(qT[0:96, q0:q0 + 512], ps[0:96, :], r=[pk], w=[qk])
            yield
        t0 = 0
        while t0 < T:
            n = min(512, T - t0)
            ps, pk = self.nextX()
            self.proj_fm(ps[0:64, :n], pk, lambda kc: self.wukv[:, kc, h * 128:h * 128 + 64], "wukv",
                         lambda kc, t0=t0, n=n: self.ckvn[:, kc, t0:t0 + n], ["ckvn"], 2)
            yield
            self.cp(kT[0:64, t0:t0 + n], ps[0:64, :n], r=[pk], w=[kk])
            t0 += n
        self.cp(kT[64:96, 0:T], self.krT[0:32, 0:T], r=["krT"], w=[kk], eng="pool")
        yield
        ntc = T // 128
        tc0 = 0
        while tc0 < ntc:
            g = min(7, ntc - tc0)
            ps, pk = self.nextX()
            for j in range(g):
                tc = tc0 + j
                for kc in range(2):
                    self.mm(ps[:, j * 64:(j + 1) * 64], self.ckvn[:, kc, tc * 128:(tc + 1) * 128],
                            self.wukv[:, kc, h * 128 + 64:(h + 1) * 128], start=(kc == 0), stop=(kc == 1),
                            r=["ckvn", "wukv"], w=[pk])
            yield
            self.cp(vh[:, tc0:tc0 + g, 0:64], ps[:, 0:g * 64].rearrange("p (a b) -> p a b", b=64), r=[pk], w=[vk])
            tc0 += g

    def phase_A(self):
        self.bg = self.prepA_group()
        self.run_bg()
        scale = 96.0 ** -0.5
        self.bg = self.prepA_head(0)
        self.run_bg()
        for h in range(8):
            if h < 7:
                self.bg = self.prepA_head(h + 1)
            vh = self.vh[h % 2]
            self.attend_std(self.kTh[h % 2], "kTh%d" % (h % 2), (0, 96), self.qTh[h % 2], "qTh%d" % (h % 2),
                            lambda tc, vh=vh: vh[:, tc, 0:65], "vh%d" % (h % 2), 64, scale, self.ga, "ga", h * 64)
            self.run_bg()
        for j in range(12):
            self.tm_to_mixT(self.ga[:, j, :], ("ga", j), j, 0, 4)

    def attend_std(self, kT, kk, krows, qT, qk, v_fn, vk, dv, scale, dst, dkey, col0):
        segs = SEGS

        st_ = {}

        def block(nqt, reg, dv_):
            h_ = self.rdi
            self.rdi = (h_ + 1) % 2
            po0, ok0 = reg(0)
            bank = self.psO[int(ok0[-1])]
            den = bank[:, 0:nqt * (dv_ + 1)].rearrange("p (q w) -> p q w", w=dv_ + 1)[:, :, dv_]
            f = self.rden[:, h_ * 4:h_ * 4 + nqt]
            fk = ("rden", h_)
            self.ts(f, den, 2.0, None, ALU.mult, r=[ok0], w=[fk])
            self.recip(f, f, r=[fk], w=[fk])
            st_["h"] = h_
            st_["q"] = 0

        def out_fn(qtile, po, ok):
            h_ = st_["h"]
            q_ = st_["q"]
            st_["q"] = q_ + 1
            f = self.rden[:, h_ * 4 + q_:h_ * 4 + q_ + 1]
            fk = ("rden", h_)
            self.stt(dst[:, qtile, col0:col0 + dv], po[:, 0:dv], f, dst[:, qtile, col0:col0 + dv],
                     ALU.mult, ALU.mult, r=[ok, fk, (dkey, qtile)], w=[(dkey, qtile)])
        out_fn.block = block
        self.attention(segs, kT, kk, krows, qT, qk, v_fn, vk, dv, scale, out_fn)

    def alloc_l0b(self):
        sb = self.P.sb
        self.qbT = sb("qbT", [128, 4, NT], BF16)
        self.kbT = sb("kbT", [128, TK], BF16)
        self.kbX = sb("kbX", [128, TK], BF16)
        self.vb = sb("vb", [128, 14, 2, 66], BF16)
        self.memset(self.vb[:, :, :, 64:65], 1.0, w=["vb"])
        banks = [(self.psX[0], "psX0"), (self.psX[1], "psX1"), (self.psX[2], "psX2"),
                 (self.psS[0], "psS0"), (self.psS[1], "psS1"), (self.psO[0], "psO0")]
        self.lanesB = []
        for i in range(3):
            L = {"pb": banks[2 * i], "yb": banks[2 * i + 1]}
            for nm, dt_ in (("uf", F32), ("rb", F32), ("un", F32), ("f1", F32), ("f2", F32), ("sq", BF16), ("ropeb", BF16)):
                t_ = sb("lb%d%s" % (i, nm), [128, 512], dt_)
                L[nm] = (t_[:], "lb%d%s" % (i, nm))
            self.lanesB.append(L)

    def prepB_group(self):
        hT, wB = self.hT, self.wB
        lanes = self.lanesB

        def chain(L, b, c):
            isS = b > 0
            tb0 = b * 512
            d0 = kcol(b)
            q0 = b * 512
            pos0 = (b - 1) * 512
            ps, pk = L["pb"]
            psy, yk = L["yb"]
            uf, ufk = L["uf"]
            sq, sqk = L["sq"]
            rb, rbk = L["rb"]
            un, unk = L["un"]
            col = c * 128
            self.proj_fm(ps[:], pk, lambda kc: wB[:, kc, col:col + 128], "wB",
                         lambda kc: hT[:, kc, tb0:tb0 + 512], [*self.hk(b)], 8)
            yield
            self.cp(uf, ps[:], r=[pk], w=[ufk], eng="act")
            self.act(sq, ps[:], AF.Square, r=[pk], w=[sqk])
            yield
            self.mm(psy[:], self.bones[:], sq, r=["bones", sqk], w=[yk])
            yield
            self.act(rb, psy[:], AF.Ln, r=[yk], w=[rbk], scale=1.0 / 64, bias=EPS)
            self.act(rb, rb, AF.Exp, r=[rbk], w=[rbk], scale=-0.5)
            yield
            g = self.gq_g if c < 4 else self.gk_g
            gk = "gqg" if c < 4 else "gkg"
            self.stt(un, uf, g[:, 0:1], rb, ALU.mult, ALU.mult, r=[ufk, gk, rbk], w=[unk])
            if c < 4:
                dst, dk = self.qbT[:, c, q0:q0 + 512], ["qbT"]
            else:
                dst, dk = self.kbT[:, d0:d0 + 512], ["kbT"]
            if isS:
                rpb, rpk = L["ropeb"]
                f1, k1 = L["f1"]
                f2, k2 = L["f2"]
                self.cp(rpb, un, r=[unk], w=[rpk])
                self.tt(f1, un, self.cos64[:, pos0:pos0 + 512], ALU.mult, r=[unk, "cos64"], w=[k1])
                yield
                self.mm(ps[:], self.psw[:], rpb, r=["psw", rpk], w=[pk])
                yield
                self.tt(f2, ps[:], self.sin64[:, pos0:pos0 + 512], ALU.mult, r=[pk, "sin64"], w=[k2])
                self.tt(dst, f1, f2, ALU.add, r=[k1, k2], w=dk, eng="pool")
            else:
                self.cp(dst, un, r=[unk], w=dk, eng="pool")
                if c == 4:
                    yield
                    for j in range(4):
                        self.fm_to_out(self.kbT[:, d0 + j * 128:d0 + (j + 1) * 128], ["kbT"], 128,
                                       self.outp["o_gk"][j * 128:(j + 1) * 128, :])
                        yield
            yield

        def lane_gen(L, items):
            for (b, c) in items:
                yield from chain(L, b, c)

        def vg_gen():
            ps, pk = self.psO[1], "psO1"
            for b in range(3):
                isS = b > 0
                d0 = kcol(b)
                for j in range(4):
                    t = b * 4 + j
                    tc = (d0 // 128) + j
                    for kc in range(8):
                        self.mm(ps[:, 0:128], hT[:, kc, t * 128:(t + 1) * 128], wB[:, kc, 640:768],
                                start=(kc == 0), stop=(kc == 7), r=[*self.hk(b), "wB"], w=[pk])
                    yield
                    self.cp(self.vb[:, tc, :, 0:64], ps[:, 0:128].rearrange("p (g d) -> p g d", d=64), r=[pk], w=["vb"])
                    if not isS:
                        f, fk = self.nextF()
                        self.cp(f[:, 0:128], ps[:, 0:128], r=[pk], w=[fk])
                        self.dma(self.outp["o_gv"][t * 128:(t + 1) * 128, :], f[:, 0:128], r=[fk])
                    for kc in range(8):
                        self.mm(ps[:], hT[:, kc, t * 128:(t + 1) * 128], wB[:, kc, 768:1280],
                                start=(kc == 0), stop=(kc == 7), r=[*self.hk(b), "wB"], w=[pk])
                    yield
                    f, fk = self.nextF()
                    self.act(f[:], ps[:], AF.Tanh, r=[pk], w=[fk], scale=0.5)
                    yield
                    self.stt(self.ga[:, t, :], f[:], 1.0, ps[:], ALU.add, ALU.mult, r=[fk, pk], w=[("ga", t)])

        items = [(b, c) for b in range(3) for c in range(5)]
        gens = [lane_gen(lanes[i], items[i::3]) for i in range(3)] + [vg_gen()]
        while gens:
            for g_ in list(gens):
                try:
                    next(g_)
                except StopIteration:
                    gens.remove(g_)
            yield
        self.dma(self.kbT[:, 512:768], self.inp["c_gkT"], w=["kbT"], eng="pool")
        for g in range(2):
            self.dma(self.vb[:, 4:6, g, 0:64],
                     self.inp["c_gv"].rearrange("(tc p) n -> p tc n", p=128)[:, :, g * 64:(g + 1) * 64],
                     w=["vb"], eng="pool")
        self.cp(self.kbX[0:64, 0:TK], self.kbT[64:128, 0:TK], r=["kbT"], w=["kbX"])
        self.cp(self.kbX[64:128, 0:TK], self.kbT[0:64, 0:TK], r=["kbT"], w=["kbX"])

    def phase_B(self):
        self.bg = self.prepB_group()
        self.run_bg()
        for hq in range(8):
            g = hq // 4
            c = hq // 2
            base = (hq % 2) * 64
            if base == g * 64:
                kT, kk = self.kbT, "kbT"
            else:
                kT, kk = self.kbX, "kbX"
            self.attend_std(kT, kk, (base, base + 64), self.qbT[:, c, :], "qbT",
                            lambda tc, g=g: self.vb[:, tc, g, 0:65], "vb", 64, 0.125, self.ga, "ga", hq * 64)
        for j in range(12):
            self.tm_to_mixT(self.ga[:, j, :], ("ga", j), j, 4, 4)

    def load_wout(self, L):
        for kc in range(8):
            self.dma(self.wout[:, kc, :], self.inp[L + "_w_out"][kc * 128:(kc + 1) * 128, :], w=[("wout", kc)], eng="pool")

    def out_proj(self, first):
        for t in range(12):
            b = t // 4
            v = 0 if b == 0 else 1
            xa = self.x1[t]
            xk = "x1_%d" % t
            if first:
                self.dma(xa[:], self.inp["x_in"][t * 128:(t + 1) * 128, :], w=[xk])
            for half in range(2):
                ps, pk = self.nextX()
                for c in range(8):
                    self.mm(ps[:], self.mixT[:, c, t * 128:(t + 1) * 128], self.wout[:, c, half * 512:(half + 1) * 512],
                            start=(c == 0), stop=(c == 7), r=[("mixT", b), ("wout", c)], w=[pk])
                f, fk = self.nextF()
                self.tt(f[:], ps[:], self.gate_bc[v][:, half * 512:(half + 1) * 512], ALU.mult,
                        r=[pk, "gbc%d" % v], w=[fk])
                self.tt(self.x1[t][:, half * 512:(half + 1) * 512], xa[:, half * 512:(half + 1) * 512], f[:], ALU.add,
                        r=[xk, fk], w=["x1_%d" % t], eng="pool")
            self.act(self.junk[:], self.x1[t][:], AF.Square, r=["x1_%d" % t], w=["junk", ("ss", t)],
                     accum_out=self.ss[:, t:t + 1])

    def dump_x(self):
        for t in range(12):
            self.dma(self.outp["y"][t * 128:(t + 1) * 128, :], self.x1[t][:], r=["x1_%d" % t])

    def final_norm(self):
        fg = self.P.sb("fg", [128, D], F32)
        self.dma(fg[:], self.inh["final_g"].ap().partition_broadcast(128), w=["fg"])
        self.ts(self.rstd[:, 0:12], self.ss[:, 0:12], 1.0 / D, EPS, ALU.mult, ALU.add, r=["ss"], w=["rstd"])
        self.tt(self.rstd[:, 0:12], self.rstd[:, 0:12], self.mhalf[:, 0:12], ALU.pow,
                r=["rstd", "mhalf"], w=["rstd"], eng="pool")
        for t in range(12):
            xa = self.x1[t]
            xk = "x1_%d" % t
            self.stt(xa[:], xa[:], self.rstd[:, t:t + 1], fg[:], ALU.mult, ALU.mult, r=[xk, ("rstd", t), "fg"], w=[xk])
            self.dma(self.outp["y"][t * 128:(t + 1) * 128, :], xa[:], r=[xk])


def build_program(upto=99):
    B = Builder(upto)
    nc = B.nc
    stage = [0]

    def go():
        stage[0] += 1
        return not (upto < 0 and stage[0] > -upto)

    with contextlib.ExitStack() as st:
        B.setup(st)
        with contextlib.ExitStack() as s0:
            B.P.stack = s0
            B.xstage = [B.P.sb("xst%d" % k, [128, D], F32) for k in range(12)]
            if go():
                B.adaln("e")
            if go():
                B.norm_to_hT("e", True)
                B.adaln("o")
        B.S.barrier()
        with contextlib.ExitStack() as s1:
            B.P.stack = s1
            B.alloc_l0_shared()
            with contextlib.ExitStack() as s1a:
                B.P.stack = s1a
                B.alloc_l0()
                if go():
                    B.load_l0_weights()
                if go():
                    B.phase_A()
                go()
            B.S.barrier()
            go()
            with contextlib.ExitStack() as s1b:
                B.P.stack = s1b
                B.alloc_l0b()
                if go():
                    B.phase_B()
                go()
        B.S.barrier()
        B.P.stack = st
        B.x1 = [B.P.sb("x1_%d" % i, [128, D], F32) for i in range(12)]
        with contextlib.ExitStack() as so:
            B.P.stack = so
            B.gate_bc = [B.P.sb("gbc%d" % v, [128, D], F32) for v in range(2)]
            B.wout = B.P.sb("wout", [128, 8, D], BF16)
            if go():
                B.load_wout("e")
                B.gate_bcast("e")
                B.out_proj(True)
        B.S.barrier()
        B.P.stack = st
        if upto == 0:
            B.dump_x()
        elif upto > 0:
            B.layer1()
            B.final_norm()
        print("ops:", {e: len(v) for e, v in B.S.ops.items()}, "waits:", B.S.n_waits)
        sems = {t: st.enter_context(nc.semaphore("s_" + t)) for t in B.S.sem_names()}
        print("semaphores:", len(sems), "sw dmas:", B.S.n_sw)
        with nc.Block() as block:
            B.S.finish(sems, block)
    return nc


def _consts():
    k = {}
    k["k_ident"] = np.eye(128, dtype=np.float32)
    bo = np.zeros((128, 128), np.float32)
    bo[:64, :64] = 1
    bo[64:, 64:] = 1
    k["k_bones"] = bo
    ps = np.zeros((128, 128), np.float32)
    for i in range(128):
        ps[i, i ^ 1] = 1
    k["k_psw"] = ps
    s_ = np.arange(64)[:, None]
    t_ = np.arange(64)[None, :]
    k["k_maskf"] = (s_ <= t_).astype(np.float32)
    k["k_maskb"] = (s_ >= t_).astype(np.float32)

    def rope_tab(rot):
        n = 1024
        rows = (np.arange(n) // 64).astype(np.float32)
        cols = (np.arange(n) % 64).astype(np.float32)
        half = rot // 2
        inv = (1.0 / (10000.0 ** (np.arange(0, half, 2, dtype=np.float32) / half))).astype(np.float32)
        ang = np.concatenate([rows[:, None] * inv, cols[:, None] * inv], axis=-1)
        cos = np.cos(ang).astype(np.float32)
        sin = np.sin(ang).astype(np.float32)
        cosT = np.repeat(cos, 2, axis=1).T
        sinT = np.repeat(sin, 2, axis=1).T.copy()
        sinT[0::2] *= -1.0
        return np.ascontiguousarray(cosT), np.ascontiguousarray(sinT)

    c64, s64 = rope_tab(64)
    k["k_cos64"] = np.ascontiguousarray(np.tile(c64, (2, 1)))
    k["k_sin64"] = np.ascontiguousarray(np.tile(s64, (2, 1)))
    c32, s32 = rope_tab(32)
    z = np.zeros((128, 1024), np.float32)
    z[0:32] = c32
    z[64:96] = c32
    k["k_cos32"] = z
    z = np.zeros((128, 1024), np.float32)
    z[0:32] = s32
    z[64:96] = s32
    k["k_sin32"] = z
    return k


def _fm(vec, nch):
    return np.ascontiguousarray(np.asarray(vec, np.float32).reshape(nch, 128).T)


_NC_CACHE = {}


def kernel(**inp):
    f = lambda a: np.asarray(a, np.float32)
    consts = _consts()
    shared = dict(consts)
    shared["e_norm_g"] = _fm(inp["e_norm_g"][0], 8)
    shared["e_ada_w"] = f(inp["e_ada_w"][0])
    shared["e_ada_b"] = _fm(inp["e_ada_b"][0], 24)
    shared["e_w_in"] = f(inp["e_w_in"][0])
    shared["e_qn_g"] = _fm(inp["e_mla_qnorm_g"][0], 3)
    shared["e_w_uq"] = f(inp["e_mla_w_uq"][0])
    shared["e_kvn_g"] = _fm(inp["e_mla_kvnorm_g"][0], 2)
    shared["e_w_ukv"] = f(inp["e_mla_w_ukv"][0])
    shared["e_gq_g"] = np.ascontiguousarray(np.tile(f(inp["e_gqa_qnorm_g"][0]), 2)[:, None])
    shared["e_gk_g"] = np.ascontiguousarray(np.tile(f(inp["e_gqa_knorm_g"][0]), 2)[:, None])
    shared["e_w_out"] = f(inp["e_w_out"][0])
    shared["o_norm_g"] = _fm(inp["o_norm_g"][0], 8)
    shared["o_ada_w"] = f(inp["o_ada_w"][0])
    shared["o_ada_b"] = _fm(inp["o_ada_b"][0], 24)
    shared["o_w_in"] = f(inp["o_w_in"][0])
    shared["o_gate_b"] = np.ascontiguousarray(f(inp["o_mlstm_gate_b"][0]).reshape(4, 4).T)
    shared["o_mn_g"] = f(inp["o_mlstm_norm_g"][0])[None, :]
    shared["o_lambda"] = f(inp["o_diff_lambda"][0]).reshape(1, 256)
    shared["o_dn_g"] = f(inp["o_diff_norm_g"][0])[None, :]
    shared["o_w_out"] = f(inp["o_w_out"][0])
    shared["final_g"] = f(inp["final_norm_g"])[None, :]
    xp = f(inp["x_prompt"])
    xs = f(inp["x_sample"])
    in_maps = []
    for core in range(8):
        sq = core // 4
        m = dict(shared)
        m["x_in"] = np.ascontiguousarray(np.concatenate([xp[2 * core], xp[2 * core + 1], xs[sq]], axis=0))
        cv = np.stack([f(inp["c_ctx"]), f(inp["c"][sq])], axis=-1)
        m["c2"] = np.ascontiguousarray(cv.reshape(8, 128, 2).transpose(1, 0, 2))
        m["c_ckvT"] = np.ascontiguousarray(f(inp["cache_mla_ckv"][sq, 0]).T)
        m["c_krT"] = np.ascontiguousarray(f(inp["cache_mla_krope"][sq, 0]).T)
        m["c_gkT"] = np.ascontiguousarray(f(inp["cache_gqa_k"][sq, 0]).reshape(256, 128).T)
        m["c_gv"] = np.ascontiguousarray(f(inp["cache_gqa_v"][sq, 0]).reshape(256, 128))
        m["c_ST"] = np.ascontiguousarray(f(inp["state_mlstm_C"][sq, 0]).reshape(8, 128, 128).transpose(0, 2, 1))
        m["c_n"] = np.ascontiguousarray(f(inp["state_mlstm_n"][sq, 0]).reshape(8, 128))
        m["c_m"] = np.ascontiguousarray(f(inp["state_mlstm_m"][sq, 0]).reshape(2, 4).T)
        m["c_kdT"] = np.ascontiguousarray(f(inp["cache_diff_k"][sq, 0]).transpose(1, 2, 0))
        m["c_vd"] = np.ascontiguousarray(f(inp["cache_diff_v"][sq, 0]).reshape(256, 512))
        in_maps.append(m)
    if "nc" not in _NC_CACHE:
        _NC_CACHE["nc"] = build_program()
    nc = _NC_CACHE["nc"]
    if _NC_CACHE.get("debug_cores"):
        ncore = _NC_CACHE["debug_cores"]
        res = run_bass_kernel_spmd(nc, in_maps[:ncore], core_ids=list(range(ncore)))
        return res.results
    res = run_bass_kernel_spmd(nc, in_maps, core_ids=list(range(8)))
    R = res.results
    y_prompt = np.stack([R[c]["y"][s * 256:(s + 1) * 256] for c in range(8) for s in range(2)], axis=0)
    y_sample = np.stack([R[0]["y"][512:], R[4]["y"][512:]], axis=0)

    def gather(name, shape):
        return np.stack([R[c][name][s * 256:(s + 1) * 256].reshape(shape) for c in range(8) for s in range(2)],
                        axis=0)[:, None]
    st_ckv = gather("o_ckv", (256, 256))
    st_kr = gather("o_kr", (256, 32))
    st_gk = gather("o_gk", (256, 2, 64))
    st_gv = gather("o_gv", (256, 2, 64))
    st_dk = gather("o_dk", (256, 4, 128))
    st_dv = gather("o_dv", (256, 4, 128))
    st_C = np.stack([R[c]["o_C"][s * 8:(s + 1) * 8].reshape(2, 4, 128, 128) for c in range(8) for s in range(2)],
                    axis=0)[:, None]
    st_n = np.stack([R[c]["o_n"][s * 8:(s + 1) * 8].reshape(2, 4, 128) for c in range(8) for s in range(2)],
                    axis=0)[:, None]
    st_m = np.stack([R[c]["o_m"][:, s * 2:(s + 1) * 2].T.reshape(2, 4) for c in range(8) for s in range(2)],
                    axis=0)[:, None]
    outs = (y_prompt, y_sample, st_ckv, st_kr, st_gk, st_gv, st_C, st_n, st_m, st_dk, st_dv)
    return tuple(np.ascontiguousarray(o, dtype=np.float32) for o in outs)


LAM_INIT = 0.8 - 0.6 * float(np.exp(-0.3 * 1))
TILE_SEQS = [(0, 2, False), (2, 4, False), (4, 12, True)]


def _l1_setup(self):
    sb = self.P.sb
    i = self.inp
    self.wG = sb("wG", [128, 8, 16], BF16)
    self.dma(self.wG[:], i["o_w_in"].rearrange("(kc p) n -> p kc n", p=128)[:, :, 2048:2064], w=["wG"], eng="pool")
    self.gb = sb("gb", [4, 4], F32)
    self.dma(self.gb[:], i["o_gate_b"], w=["gb"])
    self.cm = sb("cm", [4, 2], F32)
    self.dma(self.cm[:], i["c_m"], w=["cm"])
    self.scal = sb("scal", [128, 192], F32)
    self.decbc = sb("decbc", [128, 192], F32)
    self.mout = sb("mout", [4, 4], F32)
    self.mng = sb("mng", [128, 512], F32)
    self.dma(self.mng[:], self.inh["o_mn_g"].ap().partition_broadcast(128), w=["mng"])
    self.ts(self.mng[:], self.mng[:], 0.5, None, ALU.mult, r=["mng"], w=["mng"])
    self.dng = sb("dng", [128, 128], F32)
    self.dma(self.dng[:], self.inh["o_dn_g"].ap().partition_broadcast(128), w=["dng"])
    self.ts(self.dng[:], self.dng[:], 0.5 * (1.0 - LAM_INIT), None, ALU.mult, r=["dng"], w=["dng"])
    lam = sb("lam", [128, 256], F32)
    self.dma(lam[:], self.inh["o_lambda"].ap().partition_broadcast(128), w=["lam"])
    self.nlam = sb("nlam", [128, 4], F32)
    pr = sb("lampr", [128, 128], F32)
    self.tt(pr[:].rearrange("p (a b) -> p a b", b=64), lam[:].rearrange("p (a b) -> p a b", b=128)[:, :, 0:64],
            lam[:].rearrange("p (a b) -> p a b", b=128)[:, :, 64:128], ALU.mult, r=["lam"], w=["lampr"])
    self.reduce(self.nlam[:, 0:2], pr[:].rearrange("p (a b) -> p a b", b=64), ALU.add, r=["lampr"], w=["nlam"])
    self.act(self.nlam[:, 0:2], self.nlam[:, 0:2], AF.Exp, r=["nlam"], w=["nlam"])
    self.tt(self.nlam[:, 2:3], self.nlam[:, 1:2], self.nlam[:, 0:1], ALU.subtract, r=["nlam"], w=["nlam"])
    self.ts(self.nlam[:, 2:3], self.nlam[:, 2:3], -LAM_INIT, None, ALU.add, r=["nlam"], w=["nlam"])
    self.mask2 = {}
    for nm in ("maskf", "maskb"):
        m2 = sb(nm + "2", [128, 64], F32)
        self.dma(m2[0:64, :], i["k_" + nm], w=[nm + "2"])
        self.dma(m2[64:128, :], i["k_" + nm], w=[nm + "2"])
        self.mask2[nm] = m2
    self.wL = [sb("wL%d" % k, [128, 8, 640], BF16) for k in range(2)]


def _load_head_weights(self, slot, kind, h):
    win = self.inp["o_w_in"].rearrange("(kc p) n -> p kc n", p=128)
    w = self.wL[slot]
    wk = "wL%d" % slot
    if os.environ.get("MERGEW", "0") == "1":
        o0 = 0 if kind == "C" else 2576
        self.dma(w[:, :, 0:512].rearrange("p kc (g c) -> p kc g c", c=128),
                 win[:, :, o0:o0 + 2048].rearrange("p kc (g c) -> p kc g c", c=512)[:, :, :, h * 128:(h + 1) * 128],
                 w=[wk], eng="pool")
        if kind == "C":
            self.dma(w[:, :, 512:640], win[:, :, 2064 + h * 128:2064 + (h + 1) * 128], w=[wk], eng="pool")
        return
    if kind == "C":
        offs = [0, 512, 1024, 1536, 2064]
    else:
        offs = [2576, 3088, 3600, 4112]
    for j, o in enumerate(offs):
        self.dma(w[:, :, j * 128:(j + 1) * 128], win[:, :, o + h * 128:o + (h + 1) * 128], w=[wk], eng="pool")


def _gate_chain(self, G, d, k):
    sb = self.P.sb
    ntok = G["ntok"]
    nch = ntok // 64
    tok0 = G["tok0"]
    tile0 = tok0 // 128
    ch0 = tok0 // 64
    tl = self.gts[k]
    bank, bkey = self.gbanks[k]
    K_ = str(k)
    small = tl["small"]
    sm = lambda k: small[:, k * 16:k * 16 + nch]
    dexp = tl["dexp"]

    if True:
        for gi, nm in ((d * 2, "ig"), (d * 2 + 1, "lf")):
            for bi, b in enumerate(G["batches"]):
                ps, pk = bank, bkey
                for kc in range(8):
                    self.mm(ps[0:4, :], self.wG[:, kc, gi * 4:(gi + 1) * 4], self.hT[:, kc, b * 512:(b + 1) * 512],
                            start=(kc == 0), stop=(kc == 7), r=["wG", *self.hk(b)], w=[pk])
                yield
                self.ts(tl[nm][:, bi * 512:(bi + 1) * 512], ps[0:4, :], self.gb[:, gi:gi + 1], None, ALU.add,
                        r=[pk, "gb"], w=["g_" + nm + K_])
        t_ = tl["lf"]
        yield
        self.act(t_[:, :ntok], t_[:, :ntok], AF.Exp, r=["g_lf" + K_], w=["g_lf" + K_], scale=-1.0)
        self.act(t_[:, :ntok], t_[:, :ntok], AF.Ln, r=["g_lf" + K_], w=["g_lf" + K_], bias=1.0)
        yield
        self.ts(t_[:, :ntok], t_[:, :ntok], -1.0, None, ALU.mult, r=["g_lf" + K_], w=["g_lf" + K_])
    rm = self.g_rm
    c3 = lambda ap: ap.rearrange("p (c t) -> p c t", t=64)
    if True:
        ig_t, lf_t, cum_t = tl["ig"], tl["lf"], tl["cum"]
        ik, lk, ck = "g_ig" + K_, "g_lf" + K_, "g_cum" + K_
        self.scan(cum_t[:, :ntok], rm[:, :ntok], lf_t[:, :ntok], 0.0, ALU.mult, ALU.add, r=["g_rm", lk], w=[ck])
        tot, A, mseq, mprev, Gm, dec = sm(0 + d), sm(2 + d), sm(4 + d), sm(6 + d), sm(8 + d), sm(10 + d)
        self.cp(tot, c3(cum_t[:, :ntok])[:, :, 63], r=[ck], w=["g_small" + K_])
        if d == 1:
            self.tt(c3(cum_t[:, :ntok]), tot.unsqueeze(2).broadcast_to([4, nch, 64]), c3(cum_t[:, :ntok]), ALU.subtract,
                    r=["g_small" + K_, ck], w=[ck])
            self.tt(cum_t[:, :ntok], cum_t[:, :ntok], lf_t[:, :ntok], ALU.add, r=[ck, lk], w=[ck])
        self.tt(ig_t[:, :ntok], ig_t[:, :ntok], cum_t[:, :ntok], ALU.subtract, r=[ik, ck], w=[ik])
        self.reduce(A, c3(ig_t[:, :ntok]), ALU.max, r=[ik], w=["g_small" + K_])
        for (t0, t1, has_ctx) in TILE_SEQS:
            c0, c1 = t0 * 2 - ch0, t1 * 2 - ch0
            if c0 < 0 or c1 > nch:
                continue
            if has_ctx:
                init = self.cm[:, d:d + 1]
                self_k = ["cm"]
            else:
                init = 0.0
                self_k = []
            if d == 0:
                self.scan(mseq[:, c0:c1], A[:, c0:c1], tot[:, c0:c1], init, ALU.max, ALU.add,
                          r=["g_small" + K_] + self_k, w=["g_small" + K_])
                if has_ctx:
                    self.cp(mprev[:, c0:c0 + 1], init, r=["cm"], w=["g_small" + K_])
                else:
                    self.memset(mprev[:, c0:c0 + 1], 0.0, w=["g_small" + K_])
                self.cp(mprev[:, c0 + 1:c1], mseq[:, c0:c1 - 1], r=["g_small" + K_], w=["g_small" + K_])
            else:
                self.scan(mseq[:, c0:c1][:, ::-1], A[:, c0:c1][:, ::-1], tot[:, c0:c1][:, ::-1], init, ALU.max, ALU.add,
                          r=["g_small" + K_] + self_k, w=["g_small" + K_])
                if has_ctx:
                    self.cp(mprev[:, c1 - 1:c1], init, r=["cm"], w=["g_small" + K_])
                else:
                    self.memset(mprev[:, c1 - 1:c1], 0.0, w=["g_small" + K_])
                self.cp(mprev[:, c0:c1 - 1], mseq[:, c0 + 1:c1], r=["g_small" + K_], w=["g_small" + K_])
            if not has_ctx:
                s_idx = t0 // 2
                last = mseq[:, c1 - 1:c1] if d == 0 else mseq[:, c0:c0 + 1]
                self.cp(self.mout[:, s_idx * 2 + d:s_idx * 2 + d + 1], last, r=["g_small" + K_], w=["mout"])
        self.tt(Gm, mprev, A, ALU.max, r=["g_small" + K_], w=["g_small" + K_])
        self.tt(dec, mprev, Gm, ALU.subtract, r=["g_small" + K_], w=["g_small" + K_])
        yield
        self.act(dec, dec, AF.Exp, r=["g_small" + K_], w=["g_small" + K_])
        yield
        Gbc = Gm.unsqueeze(2).broadcast_to([4, nch, 64])
        self.tt(c3(ig_t[:, :ntok]), c3(ig_t[:, :ntok]), Gbc, ALU.subtract, r=[ik, "g_small" + K_], w=[ik])
        yield
        self.act(ig_t[:, :ntok], ig_t[:, :ntok], AF.Exp, r=[ik], w=[ik])
        self.tt(c3(cum_t[:, :ntok]), c3(cum_t[:, :ntok]), Gbc, ALU.add, r=[ck, "g_small" + K_], w=[ck])
        yield
        self.act(cum_t[:, :ntok], cum_t[:, :ntok], AF.Exp, r=[ck], w=[ck], scale=-1.0)
        yield
        ps, pk = bank, bkey
        ntile = ntok // 128
        for tl_i in range(ntile):
            for q, src, sk in ((0, ig_t, ik), (1, cum_t, ck)):
                col = (tl_i * 2 + q) * 4
                self.mm(ps[:, col:col + 4], src[0:4, tl_i * 128:(tl_i + 1) * 128], self.ident_f[0:4, 0:4],
                        r=[sk, "identf"], w=[pk])
        yield
        for tl_i in range(ntile):
            base = (((tile0 + tl_i) * 2 + d) * 2) * 4
            self.cp(self.scal[:, base:base + 8], ps[:, tl_i * 8:tl_i * 8 + 8], r=[pk], w=["scal"])
        self.tt(dexp[:, :nch, :], dec.unsqueeze(2).broadcast_to([4, nch, 4]),
                self.ident_f[0:4, 0:4].unsqueeze(1).broadcast_to([4, nch, 4]), ALU.mult,
                r=["g_small" + K_, "identf"], w=["g_dexp" + K_])
        yield
        ps2, pk2 = bank, bkey
        self.mm(ps2[:, 0:nch * 4], self.ones_f[0:4, :], dexp[:, :nch, :].rearrange("p c h -> p (c h)"),
                r=["onesf", "g_dexp" + K_], w=[pk2])
        yield
        self.cp(self.decbc[:, (d * 24 + ch0) * 4:(d * 24 + ch0 + nch) * 4], ps2[:, 0:nch * 4], r=[pk2], w=["decbc"])


def _gate_prep_all(self):
    sb = self.P.sb
    self.g_rm = sb("g_rm", [4, 1024], F32)
    self.memset(self.g_rm[:], 1.0, w=["g_rm"])
    self.memset(self.g_rm[:].rearrange("p (c t) -> p c t", t=64)[:, :, 0:1], 0.0, w=["g_rm"])
    self.gts = []
    for k in range(4):
        t_ = {nm: sb("g_%s%d" % (nm, k), [4, 1024], F32) for nm in ("ig", "lf", "cum")}
        t_["small"] = sb("g_small%d" % k, [4, 256], F32)
        t_["dexp"] = sb("g_dexp%d" % k, [4, 16, 4], F32)
        self.gts.append(t_)
    self.gbanks = [(self.psX[0], "psX0"), (self.psX[1], "psX1"), (self.psX[2], "psX2"), (self.psS[0], "psS0")]
    gens = [self.gate_chain(G, d, gi * 2 + d) for gi, G in enumerate((GP, GS)) for d in range(2)]
    while gens:
        for g_ in list(gens):
            try:
                next(g_)
            except StopIteration:
                gens.remove(g_)


def _phase_C_head(self, h, slot):
    w = self.wL[slot]
    wk = "wL%d" % slot
    hT = self.hT
    qT, kT, ktm, vau, og, zg, hbuf = self.c_qT, self.c_kT, self.c_ktm, self.c_vau, self.c_og, self.c_zg, self.c_hbuf
    dscale = 128.0 ** -0.5
    for b in range(3):
        rhs = lambda kc: hT[:, kc, b * 512:(b + 1) * 512]
        ps, pk = self.nextX()
        self.proj_fm(ps[:], pk, lambda kc: w[:, kc, 0:128], wk, rhs, [*self.hk(b)], 8)
        self.cp(qT[:, b * 512:(b + 1) * 512], ps[:], r=[pk], w=["c_qT"], eng="act")
        ps, pk = self.nextX()
        self.proj_fm(ps[:], pk, lambda kc: w[:, kc, 128:256], wk, rhs, [*self.hk(b)], 8)
        self.ts(kT[:, b * 512:(b + 1) * 512], ps[:], dscale, None, ALU.mult, r=[pk], w=["c_kT"])
    for t in range(12):
        b = t // 4
        ps, pk = self.nextX()
        for kc in range(8):
            self.mm(ps[:], hT[:, kc, t * 128:(t + 1) * 128], w[:, kc, 128:640], start=(kc == 0), stop=(kc == 7),
                    r=[*self.hk(b), wk], w=[pk])
        self.ts(ktm[:, t, :], ps[:, 0:128], dscale, None, ALU.mult, r=[pk], w=[("c_ktm", t)])
        self.cp(vau[:, t, 0:128], ps[:, 128:256], r=[pk], w=[("c_vau", t)], eng="act")
        f, fk = self.nextF()
        self.act(f[:, 0:256], ps[:, 256:512], AF.Tanh, r=[pk], w=[fk], scale=0.5)
        self.ts(og[:, t, :], f[:, 0:128], 0.5, 0.5, ALU.mult, ALU.add, r=[fk], w=[("c_og", t)])
        self.stt(zg[:, t, :], f[:, 128:256], 1.0, ps[:, 384:512], ALU.add, ALU.mult, r=[fk, pk], w=[("c_zg", t)])
    cs = int(os.environ.get("CS", "9"))
    if cs < 2:
        return
    self.mlstm_pre(h)
    self.memset(hbuf[:], 0.0, w=["c_hbuf"], eng="pool")
    chains = []
    for si, (t0, t1, has_ctx) in enumerate(TILE_SEQS):
        for d in range(2):
            ci = si * 2 + d
            Sf = self.c_S[ci]
            Sk = "c_S%d" % ci
            if has_ctx:
                self.dma(Sf[:, 0:128], self.inp["c_ST"][d * 4 + h], w=[Sk])
                self.dma(Sf[:, 128:129], self.inp["c_n"][d * 4 + h].rearrange("(k o) -> k o", o=1), w=[Sk])
            else:
                self.memset(Sf[:], 0.0, w=[Sk])
            chunks = list(range(t0 * 2, t1 * 2))
            if d == 1:
                chunks = chunks[::-1]
            chains.append(dict(ci=ci, d=d, si=si, chunks=chunks, has_ctx=has_ctx))
    sched = {}
    pj = 0
    for ch in chains:
        n = len(ch["chunks"])
        if n == 16:
            for s_ in range(16):
                sched.setdefault(s_, []).append((ch, ch["chunks"][s_]))
        else:
            for s_ in range(n):
                sched.setdefault(4 * s_ + pj, []).append((ch, ch["chunks"][s_]))
            pj += 1
    for rnd in sorted(sched):
        live = sched[rnd]
        for stage in range(4):
            for ch, c in live:
                self._mlstm_step(h, ch, c, stage)
    if cs < 3:
        return
    for ch in chains:
        if ch["has_ctx"]:
            continue
        ci, d, si = ch["ci"], ch["d"], ch["si"]
        Sf, Sk = self.c_S[ci], "c_S%d" % ci
        idx = (si * 2 + d) * 4 + h
        ps, pk = self.nextX()
        self.S.emit("pe", lambda e, ps=ps, Sf=Sf: e.transpose(out=ps[:, 0:128], in_=Sf[:, 0:128], identity=self.ident_f[:]),
                    reads=[Sk, "identf"], writes=[pk])
        f, fk = self.nextF()
        self.cp(f[:, 0:128], ps[:, 0:128], r=[pk], w=[fk])
        self.dma(self.outp["o_C"][idx], f[:, 0:128], r=[fk])
        self.dma(self.outp["o_n"][idx].rearrange("(k o) -> k o", o=1), Sf[:, 128:129], r=[Sk])
    if cs < 4:
        return
    for t in range(12):
        self.tt(hbuf[:, t, :], hbuf[:, t, :], og[:, t, :], ALU.mult, r=[("c_hbuf", t), ("c_og", t)], w=[("c_hbuf", t)])
        self.act(self.junk[:, 0:128], hbuf[:, t, :], AF.Square, r=[("c_hbuf", t)], w=["junk", ("ss", t)],
                 accum_out=self.ss[:, t:t + 1])
        self.tt(zg[:, t, :], zg[:, t, :], self.mng[:, h * 128:(h + 1) * 128], ALU.mult, r=[("c_zg", t), "mng"],
                w=[("c_zg", t)], eng="pool")
    self.ts(self.rstd[:, 0:12], self.ss[:, 0:12], 1.0 / 128, EPS, ALU.mult, ALU.add, r=["ss"], w=["rstd"])
    self.tt(self.rstd[:, 0:12], self.rstd[:, 0:12], self.mhalf[:, 0:12], ALU.pow, r=["rstd", "mhalf"], w=["rstd"],
            eng="pool")
    for t in range(12):
        xs = self.xs[t % 2]
        xk = "xs%d" % (t % 2)
        self.stt(xs[:, 0:128], hbuf[:, t, :], self.rstd[:, t:t + 1], zg[:, t, :], ALU.mult, ALU.mult,
                 r=[("c_hbuf", t), ("rstd", t), ("c_zg", t)], w=[xk])
        self.tr(self.psT[:, 0:128], xs[:, 0:128], self.ident_b[:], r=[xk, "identb"], w=["psT"])
        self.cp(self.mixT[:, h, t * 128:(t + 1) * 128], self.psT[:, 0:128], r=["psT"], w=[("mixT", t // 4)])


def _mlstm_pre(self, h):
    qT, kT, ktm = self.c_qT, self.c_kT, self.c_ktm
    for t in range(12):
        ps, pk = self.nextY()
        for half in range(2):
            c = t * 2 + half
            cols = slice(c * 64, (c + 1) * 64)
            p0 = half * 64
            self.mm(ps[p0:p0 + 64, 0:64], kT[:, cols], qT[:, cols], r=["c_kT", "c_qT"], w=[pk])
        for d in range(2):
            col = ((t * 2 + d) * 2 + 0) * 4 + h
            e_ap = self.scal[:, col:col + 1]
            mask = self.mask2["maskf" if d == 0 else "maskb"]
            mk = "maskf2" if d == 0 else "maskb2"
            self.stt(self.c_smta[:, t, d, :], ps[:, 0:64], e_ap, mask[:, :], ALU.mult, ALU.mult,
                     r=[pk, "scal", mk], w=[("c_smta", t)])
            self.act(self.c_kpa[:, t, d, :], ktm[:, t, :], AF.Copy, r=[("c_ktm", t), "scal"], w=[("c_kpa", t)],
                     scale=e_ap)


def _mlstm_step(self, h, ch, c, stage):
    ci, d = ch["ci"], ch["d"]
    t = c // 2
    p0 = (c % 2) * 64
    p1 = p0 + 64
    cols = slice(c * 64, (c + 1) * 64)
    qT, vau, hbuf = self.c_qT, self.c_vau, self.c_hbuf
    Sf, Sk = self.c_S[ci], "c_S%d" % ci
    Sb, Sbk = self.c_Sb[ci], "c_Sb%d" % ci
    dd, ddk = self.c_dd[ci], "c_dd%d" % ci
    thr_ap = self.scal[p0:p1, ((t * 2 + d) * 2 + 1) * 4 + h:((t * 2 + d) * 2 + 1) * 4 + h + 1]
    dec_ap = self.decbc[:, (d * 24 + c) * 4 + h:(d * 24 + c) * 4 + h + 1]
    if stage == 0:
        self.act(Sb[:, 0:129], Sf[:, 0:129], AF.Copy, r=[Sk, "decbc"], w=[Sbk], scale=dec_ap)
    elif stage == 1:
        ps2, pk2 = self.nextY()
        ch["ps2"] = (ps2, pk2)
        self.mm(ps2[:, 256:385], self.c_kpa[p0:p1, t, d, :], vau[p0:p1, t, 0:129], r=[("c_kpa", t), ("c_vau", t)], w=[pk2])
        self.mm(ps2[p0:p1, 0:129], qT[:, cols], Sb[:, 0:129], start=True, stop=False, r=["c_qT", Sbk], w=[pk2])
        self.mm(ps2[p0:p1, 0:129], self.c_smta[p0:p1, t, d, :], vau[p0:p1, t, 0:129], start=False, stop=True,
                r=[("c_smta", t), ("c_vau", t)], w=[pk2])
    elif stage == 2:
        ps2, pk2 = ch["ps2"]
        self.stt(Sf[:, 0:129], Sf[:, 0:129], dec_ap, ps2[:, 256:385], ALU.mult, ALU.add, r=[Sk, "decbc", pk2], w=[Sk])
        self.act(dd[p0:p1, 0:1], ps2[p0:p1, 128:129], AF.Abs, r=[pk2], w=[ddk])
    else:
        ps2, pk2 = ch["ps2"]
        self.ts(dd[p0:p1, 0:1], dd[p0:p1, 0:1], thr_ap, None, ALU.max, r=[ddk, "scal"], w=[ddk])
        self.recip(dd[p0:p1, 0:1], dd[p0:p1, 0:1], r=[ddk], w=[ddk])
        self.stt(hbuf[p0:p1, t, :], ps2[p0:p1, 0:128], dd[p0:p1, 0:1], hbuf[p0:p1, t, :], ALU.mult, ALU.add,
                 r=[pk2, ddk, ("c_hbuf", t)], w=[("c_hbuf", t)])


def _nextY(self):
    i = self.yi
    self.yi = (i + 1) % 7
    if i < 3:
        return self.psX[i], "psX%d" % i
    if i < 5:
        return self.psS[i - 3], "psS%d" % (i - 3)
    return self.psO[i - 5], "psO%d" % (i - 5)


def _alloc_C(self):
    sb = self.P.sb
    self.c_qT = sb("c_qT", [128, NT], BF16)
    self.c_kT = sb("c_kT", [128, NT], BF16)
    self.c_ktm = sb("c_ktm", [128, 12, 128], BF16)
    self.c_vau = sb("c_vau", [128, 12, 130], BF16)
    self.c_og = sb("c_og", [128, 12, 128], BF16)
    self.c_zg = sb("c_zg", [128, 12, 128], BF16)
    self.c_hbuf = sb("c_hbuf", [128, 12, 128], F32)
    self.c_S = [sb("c_S%d" % k, [128, 130], F32) for k in range(6)]
    self.c_Sb = [sb("c_Sb%d" % k, [128, 130], BF16) for k in range(6)]
    self.c_smta = sb("c_smta", [128, 12, 2, 64], BF16)
    self.c_kpa = sb("c_kpa", [128, 12, 2, 128], BF16)
    self.c_dd = [sb("c_dd%d" % k, [128, 2], F32) for k in range(6)]
    self.memset(self.c_vau[:, :, 128:129], 1.0, w=["c_vau"])
    self.yi = 0


def _alloc_D(self):
    sb = self.P.sb
    self.d_qT = [sb("d_qT%d" % k, [128, NT], BF16) for k in range(2)]
    self.d_kT = [sb("d_kT%d" % k, [128, TK], BF16) for k in range(2)]
    self.d_v = [sb("d_v%d" % k, [128, 14, 130], BF16) for k in range(2)]
    self.d_g = [sb("d_g%d" % k, [128, 12, 128], BF16) for k in range(2)]
    self.d_a = sb("d_a", [128, 12, 128], F32)
    self.pT = [sb("pT%d" % k, [128, 10, 512], BF16) for k in range(2)]
    for k in range(2):
        self.memset(self.d_v[k][:, :, 128:129], 1.0, w=["d_v%d" % k])


def _prepD(self, h, slot):
    w = self.wL[slot]
    wk = "wL%d" % slot
    hT = self.hT
    qT, kT, dv_, dg = self.d_qT[slot], self.d_kT[slot], self.d_v[slot], self.d_g[slot]
    qk, kk, vk, gk = "d_qT%d" % slot, "d_kT%d" % slot, "d_v%d" % slot, "d_g%d" % slot
    for b in range(3):
        isS = b > 0
        d0 = kcol(b)
        q0 = b * 512
        pos0 = (b - 1) * 512
        rhs = lambda kc, b=b: hT[:, kc, b * 512:(b + 1) * 512]
        for which in range(2):
            ps, pk = self.nextX()
            self.proj_fm(ps[:], pk, lambda kc: w[:, kc, which * 128:(which + 1) * 128], wk, rhs, [*self.hk(b)], 8)
            yield
            if which == 0:
                dst, dk = qT[:, q0:q0 + 512], [qk]
            else:
                dst, dk = kT[:, d0:d0 + 512], [kk]
            if isS:
                yield from self.rope_fm(ps[:], [pk], (0, 128), 512, pos0, self.cos64, "cos64", self.sin64, "sin64", dst, dk)
            else:
                self.cp(dst, ps[:], r=[pk], w=dk)
                yield
                if which == 1:
                    for j in range(4):
                        self.fm_to_out(kT[:, d0 + j * 128:d0 + (j + 1) * 128], [kk], 128,
                                       self.outp["o_dk"][j * 128:(j + 1) * 128, h * 128:(h + 1) * 128])
                        yield
        for j in range(4):
            t = b * 4 + j
            tc = d0 // 128 + j
            ps, pk = self.nextX()
            for kc in range(8):
                self.mm(ps[:, 0:256], hT[:, kc, t * 128:(t + 1) * 128], w[:, kc, 256:512], start=(kc == 0), stop=(kc == 7),
                        r=[*self.hk(b), wk], w=[pk])
            yield
            self.cp(dv_[:, tc, 0:128], ps[:, 0:128], r=[pk], w=[vk])
            f, fk = self.nextF()
            if not isS:
                self.cp(f[:, 256:384], ps[:, 0:128], r=[pk], w=[fk], eng="act")
                self.dma(self.outp["o_dv"][t * 128:(t + 1) * 128, h * 128:(h + 1) * 128], f[:, 256:384], r=[fk])
            self.act(f[:, 0:128], ps[:, 128:256], AF.Tanh, r=[pk], w=[fk], scale=0.5)
            yield
            self.stt(dg[:, t, :], f[:, 0:128], 1.0, ps[:, 128:256], ALU.add, ALU.mult, r=[fk, pk], w=[(gk, t)])
            self.tt(dg[:, t, :], dg[:, t, :], self.dng[:], ALU.mult, r=[(gk, t), "dng"], w=[(gk, t)], eng="pool")
    self.dma(kT[:, 512:768], self.inp["c_kdT"][h], w=[kk], eng="pool")
    self.dma(dv_[:, 4:6, 0:128],
             self.inp["c_vd"].rearrange("(tc p) n -> p tc n", p=128)[:, :, h * 128:(h + 1) * 128], w=[vk],
             eng="pool")


def _phase_D_head(self, h, slot):
    qT, kT, dv_, dg, da = self.d_qT[slot], self.d_kT[slot], self.d_v[slot], self.d_g[slot], self.d_a
    qk, kk, vk, gk = "d_qT%d" % slot, "d_kT%d" % slot, "d_v%d" % slot, "d_g%d" % slot
    for sub in range(2):
        def out_fn(qtile, po, ok, sub=sub):
            j = self.rdi
            self.rdi = (j + 1) % 8
            f, fk = self.rden[:, j:j + 1], ("rden", j)
            self.recip(f, po[:, 128:129], r=[ok], w=[fk])
            if sub == 0:
                self.ts(da[:, qtile, :], po[:, 0:128], f, None, ALU.mult, r=[ok, fk], w=[("d_a", qtile)])
            else:
                self.tt(f, f, self.nlam[:, 2:3], ALU.mult, r=[fk, "nlam"], w=[fk])
                self.stt(da[:, qtile, :], po[:, 0:128], f, da[:, qtile, :], ALU.mult, ALU.add,
                         r=[ok, fk, ("d_a", qtile)], w=[("d_a", qtile)])
        self.attention(SEGS, kT, kk, (sub * 64, sub * 64 + 64), qT, qk,
                       lambda tc: dv_[:, tc, 0:129], vk, 128, 0.125, out_fn)
    self.run_bg()
    for t in range(12):
        self.act(self.junk[:, 0:128], da[:, t, :], AF.Square, r=[("d_a", t)], w=["junk", ("ss", t)],
                 accum_out=self.ss[:, t:t + 1])
    self.ts(self.rstd[:, 0:12], self.ss[:, 0:12], 1.0 / 128, EPS, ALU.mult, ALU.add, r=["ss"], w=["rstd"])
    self.tt(self.rstd[:, 0:12], self.rstd[:, 0:12], self.mhalf[:, 0:12], ALU.pow, r=["rstd", "mhalf"],
            w=["rstd"], eng="pool")
    for t in range(12):
        xs = self.xs[t % 2]
        xk = "xs%d" % (t % 2)
        self.stt(xs[:, 0:128], da[:, t, :], self.rstd[:, t:t + 1], dg[:, t, :], ALU.mult, ALU.mult,
                 r=[("d_a", t), ("rstd", t), (gk, t)], w=[xk])
        self.tr(self.psT[:, 0:128], xs[:, 0:128], self.ident_b[:], r=[xk, "identb"], w=["psT"])
        self.cp(self.mixT[:, 4 + h, t * 128:(t + 1) * 128], self.psT[:, 0:128], r=["psT"], w=[("mixT", t // 4)])


def _layer1(self):
    lim = int(os.environ.get("L1S", "99"))
    nhc = int(os.environ.get("L1HC", "4"))
    nhd = int(os.environ.get("L1HD", "4"))
    st_parent = self.P.stack
    self.norm_to_hT("o", False)
    with contextlib.ExitStack() as s2:
        self.P.stack = s2
        self.l1_setup()
        self.load_head_weights(0, "C", 0)
        with contextlib.ExitStack() as s2g:
            self.P.stack = s2g
            if lim >= 2:
                self.gate_prep_all()
        self.S.barrier()
        if lim >= 2:
            self.dma(self.outp["o_m"], self.mout[:], r=["mout"])
        with contextlib.ExitStack() as s2c:
            self.P.stack = s2c
            self.alloc_C()
            for h in range(4):
                if lim < 3 or h >= nhc:
                    break
                if h < 3:
                    self.load_head_weights((h + 1) % 2, "C", h + 1)
                else:
                    self.load_head_weights(0, "D", 0)
                self.phase_C_head(h, h % 2)
        self.S.barrier()
        with contextlib.ExitStack() as s2d:
            self.P.stack = s2d
            self.alloc_D()
            if lim >= 4:
                self.load_head_weights(1, "D", 1)
                self.bg = self.prepD(0, 0)
                self.run_bg()
            for h in range(4):
                if lim < 4 or h >= nhd:
                    break
                if h + 2 < 4:
                    self.load_head_weights(h % 2, "D", h + 2)
                if h < 3:
                    self.bg = self.prepD(h + 1, (h + 1) % 2)
                self.phase_D_head(h, h % 2)
    self.S.barrier()
    with contextlib.ExitStack() as so:
        self.P.stack = so
        self.gate_bc = [self.P.sb("gbc%d" % v, [128, D], F32) for v in range(2)]
        self.wout = self.P.sb("wout", [128, 8, D], BF16)
        if lim >= 5:
            self.load_wout("o")
            self.gate_bcast("o")
            self.out_proj(False)
    self.S.barrier()
    self.P.stack = st_parent


Builder.l1_setup = _l1_setup
Builder.load_head_weights = _load_head_weights
Builder.gate_chain = _gate_chain
Builder.gate_prep_all = _gate_prep_all
Builder.phase_C_head = _phase_C_head
Builder._mlstm_step = _mlstm_step
Builder.mlstm_pre = _mlstm_pre
Builder.nextY = _nextY
Builder.alloc_C = _alloc_C
Builder.alloc_D = _alloc_D
Builder.phase_D_head = _phase_D_head
Builder.prepD = _prepD
Builder.layer1 = _layer1
```
